# Optimizing a Trainium2 kernel written in Bass

```python
import jax, jax.numpy as jnp
from jax import lax
import numpy as np

D_MODEL = 1024
BATCH = 8
SEQ = 4096
DEPTH = 2

N_META = 16
N_FOURIER_GROUPS = 4
FOURIER_GROUP_DIM = D_MODEL // 8
FOURIER_WIDTH = N_FOURIER_GROUPS * FOURIER_GROUP_DIM
N_DELTA_HEADS = 4
HEAD_DIM = D_MODEL // 8
DELTA_WIDTH = N_DELTA_HEADS * HEAD_DIM
N_DIRS = 2
CONV_WIDTH = 5
CHUNK = 64
META_PAD = (-N_META) % CHUNK
D_FF = 4 * D_MODEL
DEEPNORM_ALPHA = (2 * DEPTH) ** 0.25
DEEPNORM_BETA = (8 * DEPTH) ** -0.25
LN_EPS = 1e-5
RMS_EPS = 1e-6
QKV_WIDTH = 3 * DELTA_WIDTH
GATE_WIDTH = 2 * D_MODEL
IN_SPLITS = (FOURIER_WIDTH, QKV_WIDTH, N_DIRS * N_DELTA_HEADS, N_DIRS * N_DELTA_HEADS, DELTA_WIDTH, GATE_WIDTH)
IN_WIDTH = sum(IN_SPLITS)

kernel_name = "fourier_gated_deltanet_deepnorm_encoder"


def layer_norm(x, g, b):
    xf = x.astype(jnp.float32)
    mu = jnp.mean(xf, -1, keepdims=True)
    var = jnp.mean(jnp.square(xf - mu), -1, keepdims=True)
    y = (xf - mu) * lax.rsqrt(var + LN_EPS) * g.astype(jnp.float32) + b.astype(jnp.float32)
    return y.astype(x.dtype)


def l2norm(t):
    return t * lax.rsqrt(jnp.sum(jnp.square(t), -1, keepdims=True) + 1e-6)


def fourier_mix(u):
    Bn, L, _ = u.shape
    ug = u.astype(jnp.float32).reshape(Bn, L, N_FOURIER_GROUPS, FOURIER_GROUP_DIM)
    y = jnp.fft.fft2(ug, axes=(1, 3), norm="ortho").real
    return y.reshape(Bn, L, FOURIER_WIDTH).astype(u.dtype)


def short_conv(x, w):
    C = x.shape[-1]
    y = lax.conv_general_dilated(
        x, w[:, None, :], window_strides=(1,),
        padding=[(CONV_WIDTH // 2, CONV_WIDTH // 2)],
        dimension_numbers=("NWC", "WIO", "NWC"), feature_group_count=C)
    return jax.nn.silu(y)


def chunked_gated_delta_rule(q, k, v, beta, g):
    R, Bn, H, Lp, dk = q.shape
    dv = v.shape[-1]
    N = Lp // CHUNK

    def chunks(t):
        return t.reshape(t.shape[:3] + (N, CHUNK) + t.shape[4:])

    q, k, v, beta, g = chunks(q), chunks(k), chunks(v), chunks(beta), chunks(g)
    g_cum = jnp.cumsum(g, axis=-1)
    idx = jnp.arange(CHUNK)
    lower_incl = idx[:, None] >= idx[None, :]
    lower_strict = idx[:, None] > idx[None, :]
    diff = g_cum[..., :, None] - g_cum[..., None, :]
    decay_incl = jnp.exp(jnp.where(lower_incl, diff, -jnp.inf))
    decay_strict = jnp.where(lower_strict, decay_incl, 0.0)

    k_beta = k * beta[..., None]
    v_beta = v * beta[..., None]
    l_mat = jnp.einsum("rbhncd,rbhnsd->rbhncs", k_beta, k) * decay_strict
    unit_lower = l_mat + jnp.eye(CHUNK, dtype=l_mat.dtype)
    rhs = jnp.concatenate([v_beta, k_beta * jnp.exp(g_cum)[..., None]], axis=-1)
    sol = lax.linalg.triangular_solve(unit_lower, rhs, left_side=True, lower=True, unit_diagonal=True)
    value, k_cumdecay = sol[..., :dv], sol[..., dv:]

    attn = jnp.einsum("rbhncd,rbhnsd->rbhncs", q, k) * decay_incl
    q_decay = q * jnp.exp(g_cum)[..., None]
    g_last = g_cum[..., -1]
    k_tail = k * jnp.exp(g_last[..., None] - g_cum)[..., None]
    chunk_decay = jnp.exp(g_last)

    def to_front(t):
        return jnp.moveaxis(t, 3, 0)

    def step(S, inp):
        val, kcd, qd, at, kt, cd = inp
        v_new = val - jnp.einsum("rbhcd,rbhde->rbhce", kcd, S)
        o = jnp.einsum("rbhcd,rbhde->rbhce", qd, S) + jnp.einsum("rbhcs,rbhse->rbhce", at, v_new)
        S = S * cd[..., None, None] + jnp.einsum("rbhcd,rbhce->rbhde", kt, v_new)
        return S, o

    S0 = jnp.zeros((R, Bn, H, dk, dv), jnp.float32)
    xs = (to_front(value), to_front(k_cumdecay), to_front(q_decay), to_front(attn),
          to_front(k_tail), to_front(chunk_decay))
    _, o = lax.scan(step, S0, xs)
    return jnp.moveaxis(o, 0, 3).reshape(R, Bn, H, Lp, dv)


def gated_delta_branch(qkv, beta_logits, decay_logits, z, conv_w, a_log, dt_bias, norm_g):
    f32 = jnp.float32
    Bn, L, _ = qkv.shape
    qkv = short_conv(qkv.astype(f32), conv_w.astype(f32))
    q, k, v = jnp.split(qkv, 3, axis=-1)

    def heads(t):
        return t.reshape(Bn, L, N_DELTA_HEADS, HEAD_DIM)

    q = l2norm(heads(q)) * HEAD_DIM ** -0.5
    k = l2norm(heads(k))
    v = heads(v)
    beta = jax.nn.sigmoid(beta_logits.astype(f32)).reshape(Bn, L, N_DIRS, N_DELTA_HEADS)
    g = -jnp.exp(a_log.astype(f32)) * jax.nn.softplus(
        decay_logits.astype(f32).reshape(Bn, L, N_DIRS, N_DELTA_HEADS) + dt_bias.astype(f32))

    pad = ((0, 0), (META_PAD, 0), (0, 0), (0, 0))
    q, k, v, beta, g = [jnp.pad(t, pad) for t in (q, k, v, beta, g)]

    def both_dirs(t):
        t = t.transpose(0, 2, 1, 3)
        return jnp.stack([t, jnp.flip(t, 2)])

    def per_dir(t):
        t = t.transpose(2, 0, 3, 1)
        return jnp.stack([t[0], jnp.flip(t[1], -1)])

    o = chunked_gated_delta_rule(both_dirs(q), both_dirs(k), both_dirs(v), per_dir(beta), per_dir(g))
    o = o[0] + jnp.flip(o[1], 2)
    o = o[:, :, META_PAD:].transpose(0, 2, 1, 3)
    zh = z.astype(f32).reshape(Bn, L, N_DELTA_HEADS, HEAD_DIM)
    o = o * lax.rsqrt(jnp.mean(jnp.square(o), -1, keepdims=True) + RMS_EPS) * norm_g.astype(f32) * jax.nn.silu(zh)
    return o.reshape(Bn, L, DELTA_WIDTH).astype(z.dtype)


def hybrid_layer(x, w_in, conv_w, a_log, dt_bias, delta_norm_g, w_fourier_proj, w_delta_proj, w_out,
                 ln1_g, ln1_b, w_up, w_down, ln2_g, ln2_b):
    proj = jnp.einsum("bld,de->ble", x, w_in)
    cuts = np.cumsum(IN_SPLITS)[:-1].tolist()
    u_f, qkv, beta_l, decay_l, z, gates = jnp.split(proj, cuts, axis=-1)
    y_a = jnp.einsum("blf,fd->bld", fourier_mix(u_f), w_fourier_proj)
    y_b = jnp.einsum("blf,fd->bld",
                     gated_delta_branch(qkv, beta_l, decay_l, z, conv_w, a_log, dt_bias, delta_norm_g),
                     w_delta_proj)
    gate_a, gate_b = jnp.split(jax.nn.sigmoid(gates), 2, axis=-1)
    mix = jnp.einsum("bld,de->ble", gate_a * y_a + gate_b * y_b, w_out)
    x = layer_norm(DEEPNORM_ALPHA * x + mix, ln1_g, ln1_b)
    ff = jnp.einsum("blf,fd->bld", jnp.square(jax.nn.relu(jnp.einsum("bld,df->blf", x, w_up))), w_down)
    return layer_norm(DEEPNORM_ALPHA * x + ff, ln2_g, ln2_b)


def setup_inputs(seed: int = 0) -> dict:
    key = jax.random.key(seed)
    ks = jax.random.split(key, 20)
    f32 = jnp.float32

    def nrm(k, shape, scale):
        return jax.random.normal(k, shape, f32) * scale

    x = nrm(ks[0], (BATCH, SEQ, D_MODEL), 1.0)
    meta_tokens = nrm(ks[1], (N_META, D_MODEL), 1.0)
    ln0_g = 1.0 + nrm(ks[2], (D_MODEL,), 0.02)
    ln0_b = nrm(ks[3], (D_MODEL,), 0.02)
    w_in = nrm(ks[4], (DEPTH, D_MODEL, IN_WIDTH), D_MODEL ** -0.5)
    conv_w = nrm(ks[5], (DEPTH, CONV_WIDTH, QKV_WIDTH), CONV_WIDTH ** -0.5)
    a_log = jnp.log(jax.random.uniform(ks[6], (DEPTH, N_DIRS, N_DELTA_HEADS), f32, 1.0, 16.0))
    dt = jnp.exp(jax.random.uniform(ks[7], (DEPTH, N_DIRS, N_DELTA_HEADS), f32,
                                    float(np.log(1e-3)), float(np.log(1e-1))))
    dt_bias = dt + jnp.log(-jnp.expm1(-dt))
    delta_norm_g = 1.0 + nrm(ks[8], (DEPTH, HEAD_DIM), 0.02)
    w_fourier_proj = nrm(ks[9], (DEPTH, FOURIER_WIDTH, D_MODEL), FOURIER_WIDTH ** -0.5 * DEEPNORM_BETA)
    w_delta_proj = nrm(ks[10], (DEPTH, DELTA_WIDTH, D_MODEL), DELTA_WIDTH ** -0.5 * DEEPNORM_BETA)
    w_out = nrm(ks[11], (DEPTH, D_MODEL, D_MODEL), D_MODEL ** -0.5 * DEEPNORM_BETA)
    ln1_g = 1.0 + nrm(ks[12], (DEPTH, D_MODEL), 0.02)
    ln1_b = nrm(ks[13], (DEPTH, D_MODEL), 0.02)
    w_up = nrm(ks[14], (DEPTH, D_MODEL, D_FF), D_MODEL ** -0.5)
    w_down = nrm(ks[15], (DEPTH, D_FF, D_MODEL), D_FF ** -0.5 * DEEPNORM_BETA)
    ln2_g = 1.0 + nrm(ks[16], (DEPTH, D_MODEL), 0.02)
    ln2_b = nrm(ks[17], (DEPTH, D_MODEL), 0.02)
    return {"x": x, "meta_tokens": meta_tokens, "ln0_g": ln0_g, "ln0_b": ln0_b, "w_in": w_in,
            "conv_w": conv_w, "a_log": a_log, "dt_bias": dt_bias, "delta_norm_g": delta_norm_g,
            "w_fourier_proj": w_fourier_proj, "w_delta_proj": w_delta_proj, "w_out": w_out,
            "ln1_g": ln1_g, "ln1_b": ln1_b, "w_up": w_up, "w_down": w_down,
            "ln2_g": ln2_g, "ln2_b": ln2_b}


def reference(x, meta_tokens, ln0_g, ln0_b, w_in, conv_w, a_log, dt_bias, delta_norm_g,
              w_fourier_proj, w_delta_proj, w_out, ln1_g, ln1_b, w_up, w_down, ln2_g, ln2_b):
    Bn = x.shape[0]
    meta = jnp.broadcast_to(meta_tokens[None].astype(x.dtype), (Bn, N_META, x.shape[-1]))
    h = layer_norm(jnp.concatenate([meta, x], axis=1), ln0_g, ln0_b)
    for l in range(DEPTH):
        h = hybrid_layer(h, w_in[l], conv_w[l], a_log[l], dt_bias[l], delta_norm_g[l],
                         w_fourier_proj[l], w_delta_proj[l], w_out[l],
                         ln1_g[l], ln1_b[l], w_up[l], w_down[l], ln2_g[l], ln2_b[l])
    return h[:, N_META:]
```

```python
import contextlib
import numpy as np
import ml_dtypes
import concourse.bass as bass
import concourse.mybir as mybir
from concourse.bass_utils import run_bass_kernel_spmd

F32 = mybir.dt.float32
BF16 = mybir.dt.bfloat16
AF = mybir.ActivationFunctionType
ALU = mybir.AluOpType

L = 4112
D = 1024
NCH = 33
ALPHA = 4.0 ** 0.25
COMPUTE = ('pe', 'act', 'dve', 'pool')


def tiles(n, step):
    return [(t0, min(step, n - t0)) for t0 in range(0, n, step)]


class Buf:
    __slots__ = ('name', 'last_w', 'rd_c', 'rd_d', 'excl')

    def __init__(self, name='', excl=False):
        self.name = name
        self.excl = excl
        self.last_w = None
        self.rd_c = {}
        self.rd_d = []


class MK:
    def __init__(self, nc, n_dma_sems=24):
        self.nc = nc
        self.ops = {k: [] for k in ('pe', 'act', 'dve', 'pool', 'sp')}
        self.n_dma_sems = n_dma_sems
        self.dma_rr = {k: 0 for k in self.ops}
        self.dma_last = {}
        self.pending = {k: set() for k in self.ops}
        self.open_dma = []
        self.all_dma = []

    def barrier(self):
        evs = set()
        for e, lst in self.ops.items():
            if lst:
                evs.add((e, len(lst) - 1))
        for ev in self.open_dma:
            evs.add(ev)
        self.open_dma = []
        for e in self.ops:
            self.pending[e] |= evs

    def op(self, eng, fn, reads=(), writes=(), dma=False):
        lst = self.ops[eng]
        idx = len(lst)
        deps = set()
        if any(b.excl for b in reads):
            writes = list(writes) + [b for b in reads if b.excl]
            reads = [b for b in reads if not b.excl]

        def add(ev):
            if ev is None:
                return
            if ev[0] == 'pe' and eng == 'pe' and not dma:
                return
            deps.add(ev)
        for b in reads:
            add(b.last_w)
        for b in writes:
            add(b.last_w)
            for e, i in b.rd_c.items():
                add((e, i))
            for ev in b.rd_d:
                add(ev)
        if self.pending[eng]:
            for ev in self.pending[eng]:
                if ev[0] == eng and not self.ops[ev[0]][ev[1]]['dma']:
                    continue
                deps.add(ev)
            self.pending[eng] = set()
        slot = None
        if dma:
            slot = self.dma_rr[eng]
            self.dma_rr[eng] = (slot + 1) % self.n_dma_sems
            prev = self.dma_last.get((eng, slot))
            if prev is not None:
                deps.add(prev)
            self.dma_last[(eng, slot)] = (eng, idx)
            self.open_dma.append((eng, idx))
            self.all_dma.append((eng, idx))
        lst.append(dict(fn=fn, deps=deps, dma=dma, slot=slot, sig=False))
        me = (eng, idx)
        for b in reads:
            if dma:
                b.rd_d.append(me)
            else:
                b.rd_c[eng] = idx
        for b in writes:
            b.last_w = me
            b.rd_c = {}
            b.rd_d = []
        return me

    def finalize(self, final_waits=()):
        nc = self.nc
        ops = self.ops
        for eng, lst in ops.items():
            for o in lst:
                for (e, i) in o['deps']:
                    if not ops[e][i]['dma']:
                        ops[e][i]['sig'] = True
        st = contextlib.ExitStack()
        csem = {e: st.enter_context(nc.semaphore('s_' + e)) for e in COMPUTE}
        dsem = {}
        for eng in ops:
            if any(o['dma'] for o in ops[eng]):
                for s in range(self.n_dma_sems):
                    dsem[(eng, s)] = st.enter_context(nc.semaphore('d_%s_%d' % (eng, s)))
        ccount = {e: 0 for e in COMPUTE}
        dcount = {k: 0 for k in dsem}
        for eng, lst in ops.items():
            for o in lst:
                if o['dma']:
                    k = (eng, o['slot'])
                    dcount[k] += 16
                    o['ev'] = (('d',) + k, dcount[k])
                elif o['sig']:
                    ccount[eng] += 1
                    o['ev'] = (('c', eng), ccount[eng])
                else:
                    o['ev'] = None

        def semof(key):
            return csem[key[1]] if key[0] == 'c' else dsem[(key[1], key[2])]

        handles = {'pe': 'tensor', 'act': 'scalar', 'dve': 'vector', 'pool': 'gpsimd', 'sp': 'sync'}
        final_evs = [ops[e][i]['ev'] for (e, i) in final_waits]

        def emit_engine(eng, h):
            waited = {}
            for o in ops[eng]:
                need = {}
                for (e, i) in o['deps']:
                    key, val = ops[e][i]['ev']
                    if waited.get(key, 0) >= val:
                        continue
                    if need.get(key, 0) < val:
                        need[key] = val
                for key, val in need.items():
                    h.wait_ge(semof(key), val)
                    waited[key] = val
                ins = o['fn'](h)
                if o['dma']:
                    ins.then_inc(dsem[(eng, o['slot'])], 16)
                elif o['sig']:
                    ins.then_inc(csem[eng], 1)
            if eng == 'sp':
                for key, val in final_evs:
                    if waited.get(key, 0) < val:
                        h.wait_ge(semof(key), val)
                        waited[key] = val

        with nc.Block() as block:
            for eng in ('pe', 'act', 'dve', 'pool', 'sp'):
                getattr(block, handles[eng])(lambda h, eng=eng: emit_engine(eng, h))
        st.close()


class Prog:
    def __init__(self, debug=False, upto=99, nlayers=2):
        self.debug = debug
        self.upto = upto
        self.nlayers = nlayers
        nc = self.nc = bass.Bass("TRN2", target_bir_lowering=False)
        self.mk = MK(nc)
        self.final = []
        I = lambda name, shape, dt=F32: nc.dram_tensor(name, list(shape), dt, kind="ExternalInput").ap()
        self.x = I("x", [4096, D])
        self.meta = I("meta", [16, D])
        self.ln0 = I("ln0", [2, D])
        self.w_in = I("w_in", [2, D, 4624])
        self.convw = I("convw", [2, 128, 60])
        self.adt = I("adt", [2, 2, 264])
        self.dng = I("dng", [2, 128, 1])
        self.w_fp = I("w_fp", [2, 512, D])
        self.w_dp = I("w_dp", [2, 512, D])
        self.w_out = I("w_out", [2, D, D])
        self.ln1 = I("ln1", [2, 2, D])
        self.w_up = I("w_up", [2, D, 4096])
        self.w_dn = I("w_dn", [2, 4096, D])
        self.ln2 = I("ln2", [2, 2, D])
        self.dft = I("dft", [2, 2176, 2064], BF16)
        self.cs128 = I("cs128", [128, 256], BF16)
        self.y = nc.dram_tensor("y", [4096, D], F32, kind="ExternalOutput").ap()
        skind = "ExternalOutput" if debug else "Internal"
        S = lambda name, shape, dt: nc.dram_tensor(name, list(shape), dt, kind=skind).ap()
        self.h_tm = S("h_tm", [L, D], F32)
        self.hT_d = S("hT_d", [D, L], BF16)
        self.yfT_d = S("yfT_d", [512, L], BF16)
        self.qkT_d = S("qkT_d", [D, NCH * 128], BF16)
        self.kvtm_d = S("kvtm_d", [NCH, 128, 8, 128], BF16)
        self.dnT_d = S("dnT_d", [512, L], BF16)
        self.h1_tm = S("h1_tm", [L, D], F32)
        self.h1T_d = S("h1T_d", [D, L], BF16)
        self.oacc_d = S("oacc_d", [128, NCH * 512], F32)
        self.sc_d = S("sc_d", [128, 12 * NCH * 8], F32)
        self.B = {}
        self.ps = [nc.alloc_psum_tensor("ps%d" % i, [128, 512], F32) for i in range(7)]
        self.psb = nc.alloc_psum_tensor("psb", [128, 1024], BF16)
        self.PB = {}
        self.build()

    def buf(self, name):
        b = self.B.get(name)
        if b is None:
            b = self.B[name] = Buf(name, excl=name.startswith("ps"))
        return b

    def MM(self, out, lhsT, rhs, start=True, stop=True, rd=(), wr=()):
        self.mk.op('pe', lambda e: e.matmul(out, lhsT=lhsT, rhs=rhs, start=start, stop=stop), reads=rd, writes=wr)

    def TR(self, out, in_, ident, rd=(), wr=()):
        self.mk.op('pe', lambda e: e.transpose(out=out, in_=in_, identity=ident), reads=rd, writes=wr)

    def ACT(self, out, in_, func, bias=0.0, scale=1.0, accum_out=None, rd=(), wr=()):
        if accum_out is None:
            self.mk.op('act', lambda e: e.activation(out=out, in_=in_, func=func, bias=bias, scale=scale), reads=rd, writes=wr)
        else:
            self.mk.op('act', lambda e: e.activation(out=out, in_=in_, func=func, bias=bias, scale=scale, accum_out=accum_out), reads=rd, writes=wr)

    def COPY(self, eng, out, in_, rd=(), wr=()):
        if eng == 'act':
            self.mk.op('act', lambda e: e.copy(out=out, in_=in_), reads=rd, writes=wr)
        else:
            self.mk.op(eng, lambda e: e.tensor_copy(out=out, in_=in_), reads=rd, writes=wr)

    def TS(self, eng, out, in0, s1, s2, op0, op1=None, rd=(), wr=()):
        if op1 is None:
            self.mk.op(eng, lambda e: e.tensor_scalar(out=out, in0=in0, scalar1=s1, scalar2=None, op0=op0), reads=rd, writes=wr)
        else:
            self.mk.op(eng, lambda e: e.tensor_scalar(out=out, in0=in0, scalar1=s1, scalar2=s2, op0=op0, op1=op1), reads=rd, writes=wr)

    def TT(self, eng, out, in0, in1, op, rd=(), wr=()):
        self.mk.op(eng, lambda e: e.tensor_tensor(out=out, in0=in0, in1=in1, op=op), reads=rd, writes=wr)

    def STT(self, out, in0, scalar, in1, op0, op1, rd=(), wr=()):
        self.mk.op('dve', lambda e: e.scalar_tensor_tensor(out=out, in0=in0, scalar=scalar, in1=in1, op0=op0, op1=op1), reads=rd, writes=wr)

    def MEMSET(self, eng, ap, val, wr=()):
        self.mk.op(eng, lambda e: e.memset(ap, val), writes=wr)

    def DMA(self, q, out, in_, rd=(), wr=()):
        return self.mk.op(q, lambda e: e.dma_start(out=out, in_=in_), reads=rd, writes=wr, dma=True)

    def sb(self, es, name, shape, dt=F32):
        self._uid = getattr(self, "_uid", 0) + 1
        t = es.enter_context(self.nc.sbuf_tensor("%s_u%d" % (name, self._uid), list(shape), dt))
        return t, Buf(name)

    def ln_tile(self, xt, XB, n, gbt, bbt, GB, sc, eps=1e-5):
        stats, mv, rstd, SB_ = sc
        for c in range(2):
            self.mk.op('dve', lambda e, c=c: e.bn_stats(out=stats[:n, c, :], in_=xt[:n, c * 512:(c + 1) * 512]), reads=[XB], writes=[SB_])
        self.mk.op('dve', lambda e: e.bn_aggr(out=mv[:n, :], in_=stats[:n, :, :]), reads=[SB_], writes=[SB_])
        self.ACT(rstd[:n, :], mv[:n, 1:2], AF.Sqrt, bias=eps, rd=[SB_], wr=[SB_])
        self.mk.op('dve', lambda e: e.reciprocal(out=rstd[:n, :], in_=rstd[:n, :]), reads=[SB_], writes=[SB_])
        self.TS('dve', xt[:n, :], xt[:n, :], mv[:n, 0:1], rstd[:n, 0:1], ALU.subtract, ALU.mult, rd=[XB, SB_], wr=[XB])
        self.TT('pool', xt[:n, :], xt[:n, :], gbt[:n, :], ALU.mult, rd=[XB, GB], wr=[XB])
        self.TT('pool', xt[:n, :], xt[:n, :], bbt[:n, :], ALU.add, rd=[XB, GB], wr=[XB])

    def ln_scratch(self, es, tag):
        out = []
        for i in range(2):
            stats, b = self.sb(es, "%s_st%d" % (tag, i), [128, 2, 6])
            mv, _ = self.sb(es, "%s_mv%d" % (tag, i), [128, 2])
            rstd, _ = self.sb(es, "%s_rs%d" % (tag, i), [128, 1])
            out.append((stats, mv, rstd, b))
        return out

    def transpose_tile(self, xt, XB, n, stage, STB, off, pbanks, cnt):
        for half in range(2):
            pt, PBk = pbanks[half]
            for k in range(4):
                kk = half * 4 + k
                self.TR(pt[:, k * 128:k * 128 + n], xt[:n, kk * 128:(kk + 1) * 128], self.ident[:n, :n], rd=[XB, self.IDB], wr=[PBk])
            src = pt[:, :].rearrange("p (k t) -> p k t", k=4)[:, :, :n]
            dst = stage[:, half * 4:(half + 1) * 4, off:off + n]
            self.COPY('act' if (cnt + half) % 2 == 0 else 'dve', dst, src, rd=[PBk], wr=[STB])

    def build(self):
        nc, mk = self.nc, self.mk
        with contextlib.ExitStack() as es0:
            self.ident, self.IDB = self.sb(es0, "ident", [128, 128])
            self.ident16, _ = self.sb(es0, "ident16", [128, 128], BF16)
            self.ones32, _ = self.sb(es0, "ones32", [128, 128])
            self.ones16, _ = self.sb(es0, "ones16", [128, 128], BF16)
            self.triu, _ = self.sb(es0, "triu", [128, 128])
            self.tril, _ = self.sb(es0, "tril", [128, 128])
            self.m1f, _ = self.sb(es0, "m1f", [128, 128])
            self.m1b, _ = self.sb(es0, "m1b", [128, 128])
            self.nm16f, _ = self.sb(es0, "nm16f", [128, 4, 128], BF16)
            self.nm16b, _ = self.sb(es0, "nm16b", [128, 4, 128], BF16)
            CB = self.IDB
            sel = lambda t, pat, cm, op, fill, base=0: mk.op('pool', lambda e: e.affine_select(out=t[:], in_=t[:], pattern=pat, compare_op=op, fill=fill, base=base, channel_multiplier=cm), reads=[CB], writes=[CB])
            self.MEMSET('pool', self.ident[:], 0.0, wr=[CB])
            sel(self.ident, [[-1, 128]], 1, ALU.not_equal, 1.0)
            self.COPY('pool', self.ident16[:], self.ident[:], rd=[CB], wr=[CB])
            self.MEMSET('pool', self.ones32[:], 1.0, wr=[CB])
            self.MEMSET('pool', self.ones16[:], 1.0, wr=[CB])
            self.MEMSET('pool', self.triu[:], 1.0, wr=[CB])
            sel(self.triu, [[1, 128]], -1, ALU.is_ge, 0.0)
            self.MEMSET('pool', self.tril[:], 1.0, wr=[CB])
            sel(self.tril, [[-1, 128]], 1, ALU.is_ge, 0.0)
            self.MEMSET('pool', self.m1f[:], 1.0, wr=[CB])
            sel(self.m1f, [[-1, 128]], 1, ALU.is_gt, 0.0)
            self.MEMSET('pool', self.m1b[:], 1.0, wr=[CB])
            sel(self.m1b, [[1, 128]], -1, ALU.is_gt, 0.0)
            self.MEMSET('pool', self.nm16f[:], 0.0, wr=[CB])
            sel(self.nm16f, [[0, 4], [1, 128]], -1, ALU.is_ge, -1e5)
            self.MEMSET('pool', self.nm16b[:], 0.0, wr=[CB])
            sel(self.nm16b, [[0, 4], [-1, 128]], 1, ALU.is_ge, -1e5)
            self.phase0()
            for l in range(self.nlayers):
                last = (l == 1)
                if self.upto >= 1:
                    self.phaseA(l)
                if self.upto >= 2:
                    self.phaseB1(l)
                if self.upto >= 3:
                    self.phaseB2(l)
                with contextlib.ExitStack() as esw:
                    if self.upto >= 5:
                        W_ = self.c1_weights(esw, l)
                    if self.upto >= 4:
                        self.phaseB3(l)
                    if self.upto >= 5:
                        self.phaseC1(l, W_)
                if self.upto >= 6:
                    self.phaseC2(l, last)
            mk.barrier()
            mk.finalize(final_waits=list(mk.all_dma))

    def phase0(self):
        mk = self.mk
        with contextlib.ExitStack() as es:
            gbt, GB = self.sb(es, "p0_g", [128, D])
            bbt, _ = self.sb(es, "p0_b", [128, D])
            self.DMA('sp', gbt[:], self.ln0[0:1, :].to_broadcast([128, D]), wr=[GB])
            self.DMA('sp', bbt[:], self.ln0[1:2, :].to_broadcast([128, D]), wr=[GB])
            xts = [self.sb(es, "p0_x%d" % i, [128, D]) for i in range(6)]
            stg = [self.sb(es, "p0_s%d" % i, [128, 8, 512], BF16) for i in range(2)]
            scs = self.ln_scratch(es, "p0")
            pb = [(self.ps[0], self.buf("ps0")), (self.ps[1], self.buf("ps1"))]
            pb2 = [(self.ps[2], self.buf("ps2")), (self.ps[3], self.buf("ps3"))]
            pend = []
            for gi, (T0, gn) in enumerate(tiles(L, 512)):
                stage, STB = stg[gi % 2]
                for s, (s0, m) in enumerate(tiles(gn, 128)):
                    j = gi * 4 + s
                    t0 = T0 + s0
                    xt, XB = xts[j % 6]
                    if j == 0:
                        self.DMA('sp', xt[0:16, :], self.meta[:, :], wr=[XB])
                        self.DMA('sp', xt[16:128, :], self.x[0:112, :], wr=[XB])
                    else:
                        self.DMA('sp', xt[:m, :], self.x[t0 - 16:t0 - 16 + m, :], wr=[XB])
                    self.ln_tile(xt, XB, m, gbt, bbt, GB, scs[j % 2])
                    while pend:
                        pend.pop(0)()

                    def part2(xt=xt, XB=XB, m=m, t0=t0, j=j, stage=stage, STB=STB, s0=s0, lastsub=(s0 + m == gn), T0=T0, gn=gn, gi=gi):
                        self.DMA('pool', self.h_tm[t0:t0 + m, :], xt[:m, :], rd=[XB], wr=[self.buf("h_tm%d" % j)])
                        self.transpose_tile(xt, XB, m, stage, STB, s0, pb if j % 2 == 0 else pb2, j)
                        if lastsub:
                            self.DMA('pool', self.hT_d[:, T0:T0 + gn].rearrange("(k p) t -> p k t", p=128), stage[:, :, :gn], rd=[STB], wr=[self.buf("hT%d" % gi)])
                    pend.append(part2)
            while pend:
                pend.pop(0)()
        mk.barrier()

    def phaseA(self, l):
        mk = self.mk
        HL = L // 2 + 1
        NJ = 17
        with contextlib.ExitStack() as es:
            cs, CSB = self.sb(es, "a_cs", [128, 256], BF16)
            self.DMA('sp', cs[:], self.cs128[:, :], wr=[CSB])
            A_sb, AB = self.sb(es, "a_A", [128, NJ, 1024], BF16)
            Ue, UEB = self.sb(es, "a_Ue", [128, 4, NJ * 128], BF16)
            Uo, UOB = self.sb(es, "a_Uo", [128, 4, NJ * 128], BF16)
            ys, YSB = self.sb(es, "a_ys", [128, 4, L], BF16)
            ABs = [Buf("A%d" % j) for j in range(NJ)]
            with contextlib.ExitStack() as es1:
                wuf, WB = self.sb(es1, "a_wuf", [128, 8, 512], BF16)
                self.DMA('pool', wuf[:], self.w_in[l, :, 0:512].rearrange("(k p) n -> p k n", p=128), wr=[WB])
                u32, U32B = self.sb(es1, "a_u32", [128, 4, L])
                hts = [self.sb(es1, "a_h%d" % i, [128, 8, 512], BF16) for i in range(2)]
                pu = [(self.ps[i], self.buf("ps%d" % i)) for i in range(4)]
                T5 = tiles(L, 512)

                def a_load(g):
                    T0_, gn_ = T5[g]
                    self.DMA('sp', hts[g % 2][0][:, :, :gn_], self.hT_d[:, T0_:T0_ + gn_].rearrange("(k p) t -> p k t", p=128), rd=[self.buf("hT%d" % g)], wr=[hts[g % 2][1]])
                a_load(0)
                UBs = [Buf("u%d" % g) for g in range(len(T5))]
                for gi, (T0, gn) in enumerate(T5):
                    ht, HB = hts[gi % 2]
                    if gi + 1 < len(T5):
                        a_load(gi + 1)
                    for g in range(4):
                        pt, PBk = pu[g]
                        for k in range(8):
                            self.MM(pt[:, :gn], wuf[:, k, g * 128:(g + 1) * 128], ht[:, k, :gn], start=(k == 0), stop=(k == 7), rd=[WB, HB], wr=[PBk])
                        self.COPY('act' if g % 2 == 0 else 'dve', u32[:, g, T0:T0 + gn], pt[:, :gn], rd=[PBk], wr=[UBs[gi]])
                self.MEMSET('pool', Ue[:, :, HL:NJ * 128], 0.0, wr=[UEB])
                self.MEMSET('pool', Uo[:, :, HL:NJ * 128], 0.0, wr=[UOB])
                self.MEMSET('pool', Uo[:, :, 0:1], 0.0, wr=[UOB])
                self.MEMSET('pool', Uo[:, :, HL - 1:HL], 0.0, wr=[UOB])
                self.COPY('pool', Ue[:, :, 0:1], u32[:, :, 0:1], rd=UBs, wr=[UEB])
                self.COPY('pool', Ue[:, :, HL - 1:HL], u32[:, :, HL - 1:HL], rd=UBs, wr=[UEB])
                for g in range(4):
                    self.TT('dve', Ue[:, g, 1:HL - 1], u32[:, g, 1:HL - 1], u32[:, g, L - 1:HL - 1:-1], ALU.add, rd=UBs, wr=[UEB])
                    self.TT('dve', Uo[:, g, 1:HL - 1], u32[:, g, 1:HL - 1], u32[:, g, L - 1:HL - 1:-1], ALU.subtract, rd=UBs, wr=[UOB])
                pa = [(self.ps[4], self.buf("ps4")), (self.ps[5], self.buf("ps5"))]
                cnt = 0
                for j, (c0, m) in enumerate(tiles(NJ * 128, 128)):
                    for half in range(2):
                        pt, PBk = pa[half]
                        for gg in range(2):
                            g = half * 2 + gg
                            self.MM(pt[:m, gg * 256:gg * 256 + 128], Ue[:, g, c0:c0 + m], cs[:, 0:128], rd=[UEB, CSB], wr=[PBk])
                            self.MM(pt[:m, gg * 256 + 128:gg * 256 + 256], Uo[:, g, c0:c0 + m], cs[:, 128:256], rd=[UOB, CSB], wr=[PBk])
                        self.COPY('act' if cnt % 2 == 0 else 'dve', A_sb[:m, j, half * 512:(half + 1) * 512], pt[:m, :], rd=[PBk], wr=[ABs[j]])
                        cnt += 1
            mk.barrier()
            with contextlib.ExitStack() as es2:
                cts = [self.sb(es2, "a_ct%d" % i, [128, NJ, 512], BF16) for i in range(2)]
                sts = [self.sb(es2, "a_st%d" % i, [128, NJ, 512], BF16) for i in range(2)]
                pcs = [self.sb(es2, "a_pc%d" % i, [128, 512]) for i in range(2)]
                pb4 = [(self.ps[i], self.buf("ps%d" % i)) for i in range(4)]
                TH = tiles(HL, 512)

                def d_load(ti):
                    T0_, n_ = TH[ti]
                    n_ = max(n_, 16)
                    self.DMA('sp', cts[ti % 2][0][:, :, :n_], self.dft[0, :, T0_:T0_ + n_].rearrange("(j p) t -> p j t", p=128), wr=[cts[ti % 2][1]])
                    self.DMA('pool', sts[ti % 2][0][:, :, :n_], self.dft[1, :, T0_:T0_ + n_].rearrange("(j p) t -> p j t", p=128), wr=[sts[ti % 2][1]])
                d_load(0)
                cnt = 0
                for ti, (T0, n) in enumerate(TH):
                    ct, CTB = cts[ti % 2]
                    st, STB_ = sts[ti % 2]
                    nc_ = max(n, 16)
                    if ti + 1 < len(TH):
                        d_load(ti + 1)
                    for g in range(4):
                        pP, PPB = pb4[(2 * g) % 4]
                        pQ, PQB = pb4[(2 * g + 1) % 4]
                        for j in range(NJ):
                            kp = 128
                            self.MM(pP[:, :nc_], A_sb[:kp, j, g * 256:g * 256 + 128], ct[:kp, j, :nc_], start=(j == 0), stop=(j == NJ - 1), rd=[ABs[j], CTB], wr=[PPB])
                        for j in range(NJ):
                            kp = 128
                            self.MM(pQ[:, :nc_], A_sb[:kp, j, g * 256 + 128:g * 256 + 256], st[:kp, j, :nc_], start=(j == 0), stop=(j == NJ - 1), rd=[ABs[j], STB_], wr=[PQB])
                        pc, PCB = pcs[cnt % 2]
                        cnt += 1
                        self.COPY('act', pc[:, :n], pP[:, :n], rd=[PPB], wr=[PCB])
                        self.TT('dve', ys[:, g, T0:T0 + n], pQ[:, :n], pc[:, :n], ALU.add, rd=[PQB, PCB], wr=[YSB])
                        a_ = max(T0, 1)
                        b_ = min(T0 + n, HL - 1)
                        if b_ > a_:
                            self.TT('dve', ys[:, g, L - a_:L - b_:-1], pc[:, a_ - T0:b_ - T0], pQ[:, a_ - T0:b_ - T0], ALU.subtract, rd=[PQB, PCB], wr=[YSB])
                for gi, (T0, gn) in enumerate(tiles(L, 512)):
                    self.DMA('pool' if gi % 2 else 'sp', self.yfT_d[:, T0:T0 + gn].rearrange("(g p) t -> p g t", p=128), ys[:, :, T0:T0 + gn], rd=[YSB], wr=[self.buf("yfT%d" % gi)])
        mk.barrier()

    def phaseB1(self, l):
        mk = self.mk
        with contextlib.ExitStack() as es:
            wq, WB = self.sb(es, "b_wq", [128, 8, 1536], BF16)
            WQB = [Buf("wq%d" % c) for c in range(3)]
            for c in range(3):
                self.DMA('pool', wq[:, :, c * 512:(c + 1) * 512], self.w_in[l, :, 512 + c * 512:1024 + c * 512].rearrange("(k p) n -> p k n", p=128), wr=[WQB[c]])
            wbd, _ = self.sb(es, "b_wbd", [128, 8, 16], BF16)
            self.DMA('pool', wbd[:], self.w_in[l, :, 2048:2064].rearrange("(k p) n -> p k n", p=128), wr=[WB])
            cw, CWB = self.sb(es, "b_cw", [128, 60])
            self.DMA('sp', cw[:], self.convw[l, :, :], wr=[CWB])
            dg, DGB = self.sb(es, "b_dg", [128, 60, 128], BF16)
            for i in range(60):
                self.TS('pool', dg[:, i, :], self.ident[:, :], cw[:, i:i + 1], None, ALU.mult, rd=[CWB, self.IDB], wr=[DGB])
            xpre, XPB = self.sb(es, "b_xp", [128, 12, L + 4], BF16)
            XPBs = [[Buf("xp%d_%d" % (cc, gi)) for gi in range(9)] for cc in range(12)]
            self.MEMSET('pool', xpre[:, :, 0:2], 0.0, wr=[XPB])
            self.MEMSET('pool', xpre[:, :, L + 2:L + 4], 0.0, wr=[XPB])
            bg, BGB = self.sb(es, "b_bg", [128, 4, NCH, 4])
            self.MEMSET('pool', bg[:], 0.0, wr=[BGB])
            ss, SSB = self.sb(es, "b_ss", [128, NCH, 8])
            hts = [self.sb(es, "b_h%d" % i, [128, 8, 512], BF16) for i in range(2)]
            pp = [(self.ps[i], self.buf("ps%d" % i)) for i in range(4)]
            pbgt, PBG = self.ps[4], self.buf("ps4")
            pss, PSS = self.ps[5], self.buf("ps5")
            ptr = [(self.ps[6], self.buf("ps6"))]
            cnt = 0
            for gi, (T0, gn) in enumerate(tiles(L, 512)):
                ht, HB = hts[gi % 2]
                self.DMA('sp', ht[:, :, :gn], self.hT_d[:, T0:T0 + gn].rearrange("(k p) t -> p k t", p=128), rd=[self.buf("hT%d" % gi)], wr=[HB])
                for cc in range(12):
                    pt, PBk = pp[cnt % 4]
                    for k in range(8):
                        self.MM(pt[:, :gn], wq[:, k, cc * 128:(cc + 1) * 128], ht[:, k, :gn], start=(k == 0), stop=(k == 7), rd=[WQB[cc // 4], HB], wr=[PBk])
                    self.COPY('act' if cnt % 2 == 0 else 'dve', xpre[:, cc, 2 + T0:2 + T0 + gn], pt[:, :gn], rd=[PBk, XPB], wr=[XPBs[cc][gi]])
                    cnt += 1
                for s, (s0, m) in enumerate(tiles(gn, 128)):
                    j = gi * 4 + s
                    for k in range(8):
                        self.MM(pbgt[:m, 0:16], ht[:, k, s0:s0 + m], wbd[:, k, :], start=(k == 0), stop=(k == 7), rd=[WB, HB], wr=[PBG])
                    self.COPY('dve', bg[:m, :, j, :], pbgt[:m, 0:16].rearrange("p (a h) -> p a h", a=4), rd=[PBG], wr=[BGB])
            self.bg, self.BGB = bg, BGB
            import os
            CUT = int(os.environ.get("B1CUT", "9"))
            if CUT <= 1:
                mk.barrier(); return
            zt, ZB = self.sb(es, "b_z", [128, 1024], BF16)
            self.MEMSET('pool', zt[:], 0.0, wr=[ZB])
            self.DMA('sp', self.qkT_d[:, L:NCH * 128].rearrange("(k p) t -> p k t", p=128), zt[:, 0:8 * 112].rearrange("p (k t) -> p k t", k=8), rd=[ZB], wr=[self.buf("qkT8")])
            self.DMA('sp', self.kvtm_d[32, :, :, :], zt[:, :].rearrange("p (c d) -> p c d", c=8), rd=[ZB], wr=[self.buf("kvtm8")])
            if CUT <= 2:
                mk.barrier(); return
            sacts = [self.sb(es, "b_sa%d" % i, [128, 512]) for i in range(3)]
            q16s = [self.sb(es, "b_q16%d" % i, [128, 512], BF16) for i in range(3)]
            sqs = [self.sb(es, "b_sq%d" % i, [128, 512], BF16) for i in range(2)]
            kvs = [self.sb(es, "b_kv%d" % i, [128, 4, 8, 128], BF16) for i in range(2)]
            cnt = 0
            pend = []
            for gi, (T0, gn) in enumerate(tiles(L, 512)):
                kv, KVB = kvs[gi % 2]
                subt = tiles(gn, 128)
                for cc in range(12):
                    pt, PBk = pp[cnt % 4]
                    sa, SAB = sacts[cnt % 3]
                    rdx = [DGB] + [XPBs[cc][g2] for g2 in (gi - 1, gi, gi + 1) if 0 <= g2 < 9] + [XPB]
                    for tau in range(5):
                        self.MM(pt[:, :gn], dg[:, cc * 5 + tau, :], xpre[:, cc, T0 + tau:T0 + tau + gn], start=(tau == 0), stop=(tau == 4), rd=rdx, wr=[PBk])
                    self.ACT(sa[:, :gn], pt[:, :gn], AF.Silu, rd=[PBk], wr=[SAB])
                    sq, SQB = sqs[cnt % 2]
                    if cc < 8:
                        q16, QB = q16s[cnt % 3]
                        self.COPY('dve', q16[:, :gn], sa[:, :gn], rd=[SAB], wr=[QB])
                        self.DMA('sp', self.qkT_d[cc * 128:(cc + 1) * 128, T0:T0 + gn], q16[:, :gn], rd=[QB], wr=[self.buf("qkT%d_%d" % (gi, cc))])
                        self.TT('pool', sq[:, :gn], sa[:, :gn], sa[:, :gn], ALU.mult, rd=[SAB], wr=[SQB])
                    while pend:
                        pend.pop(0)()

                    def part2(cc=cc, gi=gi, T0=T0, gn=gn, sa=sa, SAB=SAB, sq=sq, SQB=SQB, kv=kv, KVB=KVB, subt=subt):
                        if cc < 8:
                            for s, (s0, m) in enumerate(subt):
                                j = gi * 4 + s
                                self.MM(pss[:m, j * 8 + cc:j * 8 + cc + 1], sq[:, s0:s0 + m], self.ones16[:, 0:1], rd=[SQB, self.IDB], wr=[PSS])
                        if cc >= 4:
                            tp, TPB = ptr[0]
                            for s, (s0, m) in enumerate(subt):
                                self.TR(tp[:, s * 128:(s + 1) * 128], sa[:, s0:s0 + 128], self.ident[:, :], rd=[SAB, self.IDB], wr=[TPB])
                            ns = len(subt)
                            mlast = subt[-1][1]
                            if mlast == 128:
                                self.COPY('dve', kv[:, 0:ns, cc - 4, :], tp[:, 0:ns * 128].rearrange("p (s d) -> p s d", s=ns), rd=[TPB], wr=[KVB])
                            else:
                                self.COPY('dve', kv[:mlast, 0, cc - 4, :], tp[:mlast, 0:128], rd=[TPB], wr=[KVB])
                        if cc == 11:
                            if gn == 512:
                                self.DMA('sp', self.kvtm_d[gi * 4:gi * 4 + 4, :, :, :].rearrange("j p c d -> p j c d"), kv[:, :, :, :], rd=[KVB], wr=[self.buf("kvtm%d" % gi)])
                            else:
                                self.DMA('sp', self.kvtm_d[32, 0:gn, :, :], kv[:gn, 0, :, :], rd=[KVB, self.buf("kvtm8")], wr=[self.buf("kvtm8")])
                    pend.append(part2)
                    cnt += 1
            while pend:
                pend.pop(0)()
            self.MEMSET('pool', ss[:], 1.0, wr=[SSB])
            self.COPY('dve', ss[:, 0:32, :], pss[:, 0:256].rearrange("p (j c) -> p j c", c=8), rd=[PSS], wr=[SSB])
            self.COPY('dve', ss[:16, 32, :], pss[:16, 256:264], rd=[PSS], wr=[SSB])
            self.DMA('sp', self.sc_d[:, 0:NCH * 8], ss[:].rearrange("p j c -> p (j c)"), rd=[SSB], wr=[self.buf("sc_ss")])
            self.DMA('sp', self.sc_d[:, NCH * 8:NCH * 8 + 16 * NCH], bg[:].rearrange("p a j h -> p (a j h)"), rd=[BGB], wr=[self.buf("sc_bg")])
        mk.barrier()

    def phaseB2(self, l):
        mk = self.mk
        with contextlib.ExitStack() as es:
            ss, SCB = self.sb(es, "d_ss", [128, NCH, 8])
            bg, _ = self.sb(es, "d_bg", [128, 4, NCH, 4])
            self.DMA('sp', ss[:].rearrange("p j c -> p (j c)"), self.sc_d[:, 0:NCH * 8], rd=[self.buf("sc_ss")], wr=[SCB])
            self.DMA('sp', bg[:].rearrange("p a j h -> p (a j h)"), self.sc_d[:, NCH * 8:NCH * 8 + 16 * NCH], rd=[self.buf("sc_bg")], wr=[SCB])
            adt, _ = self.sb(es, "d_adt", [128, 2, 2, NCH, 4])
            self.DMA('sp', adt[:].rearrange("p a r j h -> p (a r j h)"), self.adt[l:l + 1, :, :].rearrange("o a n -> o (a n)").to_broadcast([128, 528]), wr=[SCB])
            names = ["rk2", "irk", "s2m", "scq"]
            A4 = {n: self.sb(es, "d_" + n, [128, NCH, 4])[0] for n in names}
            names8 = ["beta", "g", "b2", "nb2", "gc", "egc", "ekt", "cd", "b2e", "tmp8"]
            A8 = {n: self.sb(es, "d_" + n, [128, 2, NCH, 4])[0] for n in names8}
            nea, _ = self.sb(es, "d_nea", [128, 2, NCH, 4])
            R = [SCB]
            ssk = ss[:, :, 4:8]
            ssq = ss[:, :, 0:4]
            self.TS('dve', A4["rk2"][:], ssk, 1e-6, None, ALU.add, rd=R, wr=R)
            self.ACT(A4["irk"][:], A4["rk2"][:], AF.Sqrt, rd=R, wr=R)
            self.mk.op('dve', lambda e: e.reciprocal(out=A4["rk2"][:], in_=A4["rk2"][:]), reads=R, writes=R)
            self.TS('dve', A4["s2m"][:], ssq, 1e-6, 128.0, ALU.add, ALU.mult, rd=R, wr=R)
            self.mk.op('dve', lambda e: e.reciprocal(out=A4["s2m"][:], in_=A4["s2m"][:]), reads=R, writes=R)
            self.ACT(A4["scq"][:], A4["s2m"][:], AF.Sqrt, rd=R, wr=R)
            self.TS('dve', A4["s2m"][:], A4["s2m"][:], 1.0 / 128.0, None, ALU.mult, rd=R, wr=R)
            self.MEMSET('pool', A8["beta"][:], 0.0, wr=R)
            self.MEMSET('pool', A8["g"][:], 0.0, wr=R)
            self.ACT(nea[:], adt[:, 0, :, :, :], AF.Exp, rd=R, wr=R)
            for (ps_, js) in ((slice(0, 128), slice(0, 32)), (slice(0, 16), slice(32, 33))):
                self.ACT(A8["beta"][ps_, :, js, :], bg[ps_, 0:2, js, :], AF.Sigmoid, rd=R, wr=R)
                self.TT('dve', A8["tmp8"][ps_, :, js, :], bg[ps_, 2:4, js, :], adt[ps_, 1, :, js, :], ALU.add, rd=R, wr=R)
                self.ACT(A8["tmp8"][ps_, :, js, :], A8["tmp8"][ps_, :, js, :], AF.Exp, rd=R, wr=R)
                self.ACT(A8["tmp8"][ps_, :, js, :], A8["tmp8"][ps_, :, js, :], AF.Ln, bias=1.0, rd=R, wr=R)
                self.STT(A8["g"][ps_, :, js, :], A8["tmp8"][ps_, :, js, :], -1.0, nea[ps_, :, js, :], ALU.mult, ALU.mult, rd=R, wr=R)
            for r in range(2):
                self.TT('dve', A8["b2"][:, r, :, :], A8["beta"][:, r, :, :], A4["rk2"][:], ALU.mult, rd=R, wr=R)
            self.TS('dve', A8["nb2"][:], A8["b2"][:], -1.0, None, ALU.mult, rd=R, wr=R)
            pg, PG = self.ps[0], self.buf("ps0")
            pl, PL = self.ps[1], self.buf("ps1")
            g2 = A8["g"]
            self.MM(pg[:, 0:132], self.triu[:, :], g2[:, 0, :, :].rearrange("p j h -> p (j h)"), rd=R + [self.IDB], wr=[PG])
            self.MM(pg[:, 132:264], self.tril[:, :], g2[:, 1, :, :].rearrange("p j h -> p (j h)"), rd=R + [self.IDB], wr=[PG])
            self.MM(pl[:, 0:264], self.ones32[:, :], g2[:].rearrange("p r j h -> p (r j h)"), rd=R + [self.IDB], wr=[PL])
            fl = lambda t: t[:].rearrange("p r j h -> p (r j h)")
            self.COPY('dve', fl(A8["gc"]), pg[:, 0:264], rd=[PG], wr=R)
            self.ACT(fl(A8["egc"]), fl(A8["gc"]), AF.Exp, rd=R, wr=R)
            self.ACT(fl(A8["cd"]), pl[:, 0:264], AF.Exp, rd=[PL], wr=R)
            self.TT('dve', fl(A8["ekt"]), pl[:, 0:264], fl(A8["gc"]), ALU.subtract, rd=[PL] + R, wr=R)
            self.ACT(fl(A8["ekt"]), fl(A8["ekt"]), AF.Exp, rd=R, wr=R)
            self.TT('dve', fl(A8["b2e"]), fl(A8["b2"]), fl(A8["ekt"]), ALU.mult, rd=R, wr=R)
            self.DMA('sp', self.sc_d[:, 800:800 + 132], A4["s2m"][:].rearrange("p j h -> p (j h)"), rd=R, wr=[self.buf("sc_s2m")])
            self.DMA('sp', self.sc_d[:, 1000:1000 + 132], A4["scq"][:].rearrange("p j h -> p (j h)"), rd=R, wr=[self.buf("sc_scq")])
            if self.debug:
                for i, n in enumerate(["beta", "g", "gc", "b2", "egc", "ekt", "cd"]):
                    self.DMA('sp', self.sc_d[:, 1200 + i * 264:1200 + (i + 1) * 264], fl(A8[n]), rd=R, wr=[self.buf("sc_dbg%d" % i)])
            import os
            CUT = int(os.environ.get("B2CUT", "9"))
            if CUT <= 1:
                mk.barrier(); return
            oacc, OB = self.sb(es, "d_oacc", [128, NCH, 4, 128])
            OBs = [Buf("oacc%d" % j) for j in range(NCH)]
            owritten = set()
            H4 = [128, 4, 128]
            bc = lambda ap4: ap4.unsqueeze(2).to_broadcast(H4)
            hb = lambda m: m[:, :].unsqueeze(1).to_broadcast(H4)
            DR = []
            for r in range(2):
                d = {}
                for n in ["tmp", "tmp2", "tS", "S32"]:
                    d[n] = self.sb(es, "d_%s_r%d" % (n, r), H4)
                for n in ["vA", "vS", "S16"]:
                    d[n] = self.sb(es, "d_%s_r%d" % (n, r), H4, BF16)
                d["tsets"] = []
                for c in range(2):
                    t_ = {}
                    for n in ["gM2", "Ei"]:
                        t_[n] = self.sb(es, "d_%s_r%d_c%d" % (n, r, c), H4)
                    t_["EsN"] = t_["gM2"]
                    for n in ["NT", "Nn", "Ma", "Mb", "MTa", "MTb", "Pa", "Pb", "Tw"]:
                        t_[n] = self.sb(es, "d_%s_r%d_c%d" % (n, r, c), H4, BF16)
                    d["tsets"].append(t_)
                for n in ["AT", "Tu", "nW"]:
                    d[n] = [self.sb(es, "d_%s_r%d_%d" % (n, r, i), H4, BF16) for i in range(3)]
                d["kq"] = [self.sb(es, "d_kq_r%d_%d" % (r, i), [128, 2, 4, 128], BF16) for i in range(3)]
                d["kv"] = [self.sb(es, "d_kv_r%d_%d" % (r, i), [128, 8, 128], BF16) for i in range(3)]
                self.MEMSET('pool', d["S32"][0][:], 0.0, wr=[d["S32"][1]])
                self.MEMSET('pool', d["S16"][0][:], 0.0, wr=[d["S16"][1]])
                DR.append(d)
            bank = lambda i: (self.ps[i], self.buf("ps%d" % i))
            b4 = lambda pt: pt[:, :].rearrange("p (h c) -> p h c", h=4)
            psb4 = self.psb[:, 0:512].rearrange("p (h c) -> p h c", h=4)
            PSBB = self.buf("psb")

            def pre(r, j, st, c):
                T = dict(DR[r])
                T.update(DR[r]["tsets"][c])
                M1 = self.m1f if r == 0 else self.m1b
                M2 = self.triu if r == 0 else self.tril
                NM = self.nm16f if r == 0 else self.nm16b
                mybank = bank(2 * r + c)

                def nxt():
                    return mybank
                kq, KQB = T["kq"][st]
                kv, KVB = T["kv"][st]
                s4 = lambda n: A8[n][:, r, j, :]
                self.DMA('sp', kq[:, :, :, :], self.qkT_d[:, j * 128:(j + 1) * 128].rearrange("(x h p) t -> p x h t", x=2, h=4),
                         rd=[self.buf("qkT%d_%d" % (j // 4, c_)) for c_ in range(8)] + [self.buf("qkT8")], wr=[KQB])
                self.DMA('sp', kv[:, :, :], self.kvtm_d[j, :, :, :], rd=[self.buf("kvtm%d" % (j // 4)), self.buf("kvtm8")], wr=[KVB])
                (gM2, GMB), (Ei, EIB), (EsN, ESB) = T["gM2"], T["Ei"], T["EsN"]
                (NT, NTB), (Nn, NNB), (Tw, TWB) = T["NT"], T["Nn"], T["Tw"]
                AT, ATB = T["AT"][st]
                Tu, TUB = T["Tu"][st]
                nW, NWB = T["nW"][st]
                self.TT(os.environ.get("B2_GM2", "pool"), gM2[:], hb(M2), bc(s4("g")), ALU.mult, rd=R + [self.IDB], wr=[GMB])
                pt, PB_ = nxt()
                self.MM(pt[:, :], M1[:, :], gM2[:].rearrange("p h c -> p (h c)"), start=True, stop=False, rd=[GMB, self.IDB], wr=[PB_])
                self.MM(pt[:, :], self.ident16[:, :], NM[:].rearrange("p h c -> p (h c)"), start=False, stop=True, rd=[self.IDB], wr=[PB_])
                self.ACT(Ei[:], b4(pt), AF.Exp, rd=[PB_], wr=[EIB])
                yield
                self.TT('pool', EsN[:], Ei[:], bc(s4("nb2")), ALU.mult, rd=[EIB] + R, wr=[ESB])
                self.mk.op('pool', lambda e: e.affine_select(out=EsN[:], in_=EsN[:], pattern=[[0, 4], [-1, 128]], compare_op=ALU.not_equal, fill=0.0, base=0, channel_multiplier=1), reads=[ESB], writes=[ESB])
                pt, PB_ = nxt()
                for h in range(4):
                    self.MM(pt[:, h * 128:(h + 1) * 128], kq[:, 1, h, :], kq[:, 0, h, :], rd=[KQB], wr=[PB_])
                self.TT('dve', AT[:], b4(pt), Ei[:], ALU.mult, rd=[PB_, EIB], wr=[ATB])
                yield
                pt, PB_ = nxt()
                for h in range(4):
                    self.MM(pt[:, h * 128:(h + 1) * 128], kq[:, 1, h, :], kq[:, 1, h, :], rd=[KQB], wr=[PB_])
                self.TT('dve', NT[:], b4(pt), EsN[:], ALU.mult, rd=[PB_, ESB], wr=[NTB])
                yield
                for h in range(4):
                    self.TR(self.psb[:, h * 128:(h + 1) * 128], NT[:, h, :], self.ident16[:, :], rd=[NTB, self.IDB], wr=[PSBB])
                self.COPY('act', Nn[:], psb4, rd=[PSBB], wr=[NNB])
                (Pa, PAB) = T["Pa"]
                self.TT('pool', Pa[:], NT[:], hb(self.ident16), ALU.add, rd=[NTB, self.IDB], wr=[PAB])
                yield
                M_prev, MB_prev = Nn, NNB
                MT_prev, MTB_prev = NT, NTB
                P_prev, PB_prev = Pa, PAB
                Ms = [T["Ma"], T["Mb"]]
                MTs = [T["MTa"], T["MTb"]]
                Ps = [T["Pb"], T["Pa"]]
                for k in range(1, 7):
                    Mk, MKB = Ms[k % 2]
                    pt, PB_ = nxt()
                    for h in range(4):
                        self.MM(pt[:, h * 128:(h + 1) * 128], MT_prev[:, h, :], M_prev[:, h, :], rd=[MB_prev, MTB_prev], wr=[PB_])
                    self.COPY('act', Mk[:], b4(pt), rd=[PB_], wr=[MKB])
                    yield
                    if k <= 5:
                        MTk, MTKB = MTs[k % 2]
                        pt, PB_ = nxt()
                        for h in range(4):
                            self.MM(pt[:, h * 128:(h + 1) * 128], M_prev[:, h, :], MT_prev[:, h, :], rd=[MB_prev, MTB_prev], wr=[PB_])
                        self.COPY('dve', MTk[:], b4(pt), rd=[PB_], wr=[MTKB])
                        yield
                    pt, PB_ = nxt()
                    self.MM(pt[:, :], self.ident16[:, :], P_prev[:].rearrange("p h c -> p (h c)"), start=True, stop=False, rd=[PB_prev, self.IDB], wr=[PB_])
                    for h in range(4):
                        self.MM(pt[:, h * 128:(h + 1) * 128], Mk[:, h, :], P_prev[:, h, :], start=False, stop=(h == 3), rd=[MKB, PB_prev], wr=[PB_])
                    if k <= 5:
                        Pk, PKB = Ps[(k - 1) % 2]
                        self.COPY('act', Pk[:], b4(pt), rd=[PB_], wr=[PKB])
                        P_prev, PB_prev = Pk, PKB
                        M_prev, MB_prev = Mk, MKB
                        MT_prev, MTB_prev = MTk, MTKB
                    else:
                        self.TT('dve', Tu[:], b4(pt), bc(A4["irk"][:, j, :]), ALU.mult, rd=[PB_] + R, wr=[TUB])
                        self.TT('dve', Tw[:], b4(pt), bc(s4("egc")), ALU.mult, rd=[PB_] + R, wr=[TWB])
                    yield
                pt, PB_ = nxt()
                for h in range(4):
                    self.MM(pt[:, h * 128:(h + 1) * 128], kv[:, h, :], Tw[:, h, :], rd=[KVB, TWB], wr=[PB_])
                self.ACT(nW[:], b4(pt), AF.Identity, scale=-1.0, rd=[PB_], wr=[NWB])
                yield

            def scan(r, j, st):
                T = DR[r]
                banks = [bank(4 + r), bank(6)]
                kq, KQB = T["kq"][st]
                kv, KVB = T["kv"][st]
                AT, ATB = T["AT"][st]
                Tu, TUB = T["Tu"][st]
                nW, NWB = T["nW"][st]
                s4 = lambda n: A8[n][:, r, j, :]
                (s32, S32B), (s16, S16B) = T["S32"], T["S16"]
                (vA, VAB), (vS, VSB) = T["vA"], T["vS"]
                (tmp, TMB), (tmp2, TM2B), (tS, TSB) = T["tmp"], T["tmp2"], T["tS"]
                self.TT('pool', tS[:], s32[:], bc(s4("cd")), ALU.mult, rd=[S32B] + R, wr=[TSB])
                pt, PB_ = banks[0]
                for h in range(4):
                    self.MM(pt[:, h * 128:(h + 1) * 128], Tu[:, h, :], kv[:, 4 + h, :], start=True, stop=False, rd=[TUB, KVB], wr=[PB_])
                    self.MM(pt[:, h * 128:(h + 1) * 128], nW[:, h, :], s16[:, h, :], start=False, stop=True, rd=[NWB, S16B], wr=[PB_])
                self.TT('dve', vS[:], b4(pt), bc(s4("b2e")), ALU.mult, rd=[PB_] + R, wr=[VSB])
                self.TT('dve', vA[:], b4(pt), bc(s4("b2")), ALU.mult, rd=[PB_] + R, wr=[VAB])
                yield
                pt, PB_ = banks[1]
                for h in range(4):
                    self.MM(pt[:, h * 128:(h + 1) * 128], kq[:, 0, h, :], s16[:, h, :], rd=[KQB, S16B], wr=[PB_])
                self.TT('dve', tmp[:], b4(pt), bc(s4("egc")), ALU.mult, rd=[PB_] + R, wr=[TMB])
                yield
                pt, PB_ = banks[0]
                for h in range(4):
                    self.MM(pt[:, h * 128:(h + 1) * 128], kv[:, h, :], vS[:, h, :], rd=[KVB, VSB], wr=[PB_])
                self.TT('dve', s32[:], b4(pt), tS[:], ALU.add, rd=[PB_, TSB], wr=[S32B])
                self.COPY('act', s16[:], s32[:], rd=[S32B], wr=[S16B])
                yield
                pt, PB_ = banks[1]
                for h in range(4):
                    self.MM(pt[:, h * 128:(h + 1) * 128], AT[:, h, :], vA[:, h, :], rd=[ATB, VAB], wr=[PB_])
                if j not in owritten:
                    owritten.add(j)
                    self.TT('dve', oacc[:, j, :, :], b4(pt), tmp[:], ALU.add, rd=[PB_, TMB], wr=[OBs[j]])
                else:
                    self.TT('pool', tmp2[:], tmp[:], oacc[:, j, :, :], ALU.add, rd=[TMB, OBs[j]], wr=[TM2B])
                    self.TT('dve', oacc[:, j, :, :], b4(pt), tmp2[:], ALU.add, rd=[PB_, TM2B], wr=[OBs[j]])
                yield

            def dir_gen(r):
                order = list(range(NCH)) if r == 0 else list(range(NCH - 1, -1, -1))
                N = len(order)
                pres = {}
                pre_done = set()
                next_pre = 0
                scans_done = 0
                scan_g = None
                next_scan = 0
                tick = 0
                for _ in range(6 * r):
                    yield
                while scans_done < N:
                    tick += 1
                    for c in range(2):
                        if c == 1 and tick < 12:
                            continue
                        if c not in pres and next_pre < N and next_pre < scans_done + 3:
                            pres[c] = (next_pre, pre(r, order[next_pre], next_pre % 3, c))
                            next_pre += 1
                    if scan_g is None and next_scan < N and next_scan in pre_done:
                        scan_g = scan(r, order[next_scan], next_scan % 3)
                    progressed = False
                    for c in list(pres):
                        idx, g = pres[c]
                        try:
                            next(g)
                            progressed = True
                            yield
                        except StopIteration:
                            pre_done.add(idx)
                            del pres[c]
                            progressed = True
                    if scan_g is not None:
                        try:
                            next(scan_g)
                            progressed = True
                            yield
                        except StopIteration:
                            scan_g = None
                            scans_done += 1
                            next_scan += 1
                            progressed = True
                    assert progressed

            gens = [dir_gen(0), dir_gen(1)]
            while gens:
                for g in list(gens):
                    try:
                        next(g)
                    except StopIteration:
                        gens.remove(g)
            self.DMA('sp', self.oacc_d[:, :], oacc[:].rearrange("p j h d -> p (j h d)"), rd=OBs, wr=[self.buf("oacc")])
        mk.barrier()

    def phaseB3(self, l):
        mk = self.mk
        with contextlib.ExitStack() as es:
            wz, WB = self.sb(es, "e_wz", [128, 8, 512], BF16)
            self.DMA('pool', wz[:], self.w_in[l, :, 2064:2576].rearrange("(k p) n -> p k n", p=128), wr=[WB])
            oacc, OB = self.sb(es, "e_oacc", [128, NCH, 4, 128])
            self.DMA('sp', oacc[:].rearrange("p j h d -> p (j h d)"), self.oacc_d[:, :], rd=[self.buf("oacc")], wr=[OB])
            s2m, SB_ = self.sb(es, "e_s2m", [128, NCH, 4])
            scq, _ = self.sb(es, "e_scq", [128, NCH, 4])
            self.DMA('sp', s2m[:].rearrange("p j h -> p (j h)"), self.sc_d[:, 800:932], rd=[self.buf("sc_s2m")], wr=[SB_])
            self.DMA('sp', scq[:].rearrange("p j h -> p (j h)"), self.sc_d[:, 1000:1132], rd=[self.buf("sc_scq")], wr=[SB_])
            ng, NGB = self.sb(es, "e_ng", [128, 1])
            self.DMA('sp', ng[:], self.dng[l, :, :], wr=[NGB])
            hts = [self.sb(es, "e_h%d" % i, [128, 8, 512], BF16) for i in range(2)]
            szs = [self.sb(es, "e_sz%d" % i, [128, 512]) for i in range(2)]
            dns = [self.sb(es, "e_dn%d" % i, [128, 4, 128]) for i in range(2)]
            junk, JB = self.sb(es, "e_junk", [128, 128])
            sms = [self.sb(es, "e_sm%d" % i, [128, 4, 4]) for i in range(2)]
            stg = [self.sb(es, "e_st%d" % i, [128, 4, 512], BF16) for i in range(2)]
            pz = [(self.ps[0], self.buf("ps0")), (self.ps[1], self.buf("ps1"))]
            ptt = [(self.ps[2], self.buf("ps2")), (self.ps[3], self.buf("ps3"))]
            T5 = tiles(L, 512)
            ssum, FB = self.sb(es, "e_ssum", [128, NCH, 4])
            fac, _ = self.sb(es, "e_fac", [128, NCH, 4])
            self.MEMSET('pool', ssum[:], 1.0, wr=[FB])
            for j, (t0_, m_) in enumerate(tiles(L, 128)):
                for h in range(4):
                    self.ACT(junk[:m_, :], oacc[:m_, j, h, :], AF.Square, accum_out=ssum[:m_, j, h:h + 1], rd=[OB], wr=[JB, FB])
            self.TT('dve', fac[:], ssum[:], s2m[:], ALU.mult, rd=[FB, SB_], wr=[FB])
            self.ACT(fac[:], fac[:], AF.Sqrt, bias=1e-6, rd=[FB], wr=[FB])
            self.mk.op('dve', lambda e: e.reciprocal(out=fac[:], in_=fac[:]), reads=[FB], writes=[FB])
            self.TT('dve', fac[:], fac[:], scq[:], ALU.mult, rd=[FB, SB_], wr=[FB])

            def b3_load(g):
                T0_, gn_ = T5[g]
                self.DMA('sp', hts[g % 2][0][:, :, :gn_], self.hT_d[:, T0_:T0_ + gn_].rearrange("(k p) t -> p k t", p=128), rd=[self.buf("hT%d" % g)], wr=[hts[g % 2][1]])
            pend = []
            b3_load(0)
            for gi, (T0, gn) in enumerate(T5):
                ht, HB = hts[gi % 2]
                stage, STB = stg[gi % 2]
                if gi + 1 < len(T5):
                    b3_load(gi + 1)
                for s, (s0, m) in enumerate(tiles(gn, 128)):
                    j = gi * 4 + s
                    pt, PBk = pz[j % 2]
                    sz, SZB = szs[j % 2]
                    dn, DNB = dns[j % 2]
                    sm, SMB = sms[j % 2]
                    for k in range(8):
                        self.MM(pt[:m, :], ht[:, k, s0:s0 + m], wz[:, k, :], start=(k == 0), stop=(k == 7), rd=[HB, WB], wr=[PBk])
                    self.ACT(sz[:m, :], pt[:m, :], AF.Silu, rd=[PBk], wr=[SZB])
                    while pend:
                        pend.pop(0)()

                    def part2(m=m, j=j, sz=sz, SZB=SZB, dn=dn, DNB=DNB, sm=sm, SMB=SMB, stage=stage, STB=STB, s0=s0, lastsub=(s0 + m == gn), T0=T0, gn=gn, gi=gi):
                        for h in range(4):
                            self.STT(dn[:m, h, :], oacc[:m, j, h, :], fac[:m, j, h:h + 1], sz[:m, h * 128:(h + 1) * 128], ALU.mult, ALU.mult, rd=[OB, FB, SZB], wr=[DNB])
                        tp, TPB = ptt[j % 2]
                        for h in range(4):
                            self.TR(tp[:, h * 128:h * 128 + m], dn[:m, h, :], self.ident[:m, :m], rd=[DNB, self.IDB], wr=[TPB])
                        self.ACT(stage[:, :, s0:s0 + m], tp[:, :].rearrange("p (h t) -> p h t", h=4)[:, :, :m], AF.Identity, scale=ng[:, 0:1], rd=[TPB, NGB], wr=[STB])
                        if lastsub:
                            self.DMA('pool', self.dnT_d[:, T0:T0 + gn].rearrange("(g p) t -> p g t", p=128), stage[:, :, :gn], rd=[STB], wr=[self.buf("dnT%d" % gi)])
                    pend.append(part2)
            while pend:
                pend.pop(0)()
        mk.barrier()

    def c1_weights(self, es, l):
        wg, WB = self.sb(es, "c_wg", [128, 8, 2048], BF16)
        wfp, WFB = self.sb(es, "c_wfp", [128, 4, D], BF16)
        wdp, WDPB = self.sb(es, "c_wdp", [128, 4, D], BF16)
        wo, WOB = self.sb(es, "c_wo", [128, 8, D], BF16)
        self.DMA('pool', wfp[:], self.w_fp[l, :, :].rearrange("(k p) n -> p k n", p=128), wr=[WFB])
        self.DMA('pool', wdp[:], self.w_dp[l, :, :].rearrange("(k p) n -> p k n", p=128), wr=[WDPB])
        WGB = [Buf("wg%d" % c) for c in range(4)]
        for c in (0, 2, 1, 3):
            self.DMA('pool', wg[:, :, c * 512:(c + 1) * 512], self.w_in[l, :, 2576 + c * 512:2576 + (c + 1) * 512].rearrange("(k p) n -> p k n", p=128), wr=[WGB[c]])
        self.DMA('pool', wo[:], self.w_out[l, :, :].rearrange("(k p) n -> p k n", p=128), wr=[WOB])
        return (wg, WB, wfp, WFB, wdp, WDPB, wo, WOB, WGB)

    def phaseC1(self, l, W):
        mk = self.mk
        (wg, WB, wfp, WFB, wdp, WDPB, wo, WOB, WGB) = W
        with contextlib.ExitStack() as es:
            gbt, GB = self.sb(es, "c_g", [128, D])
            bbt, _ = self.sb(es, "c_b", [128, D])
            self.DMA('sp', gbt[:], self.ln1[l, 0:1, :].to_broadcast([128, D]), wr=[GB])
            self.DMA('sp', bbt[:], self.ln1[l, 1:2, :].to_broadcast([128, D]), wr=[GB])
            hts = [self.sb(es, "c_h%d" % i, [128, 8, 512], BF16) for i in range(2)]
            yfs = [self.sb(es, "c_yf%d" % i, [128, 4, 512], BF16) for i in range(2)]
            dnt = [self.sb(es, "c_dn%d" % i, [128, 4, 512], BF16) for i in range(2)]
            GT, GTB = self.sb(es, "c_GT", [128, 8, 512], BF16)
            gas = [self.sb(es, "c_ga%d" % i, [128, 512]) for i in range(2)]
            gbs = [self.sb(es, "c_gb%d" % i, [128, 512]) for i in range(2)]
            t1s = [self.sb(es, "c_t1%d" % i, [128, 512]) for i in range(2)]
            t2s = [self.sb(es, "c_t2%d" % i, [128, 512]) for i in range(2)]
            hrs = [self.sb(es, "c_hr%d" % i, [128, D]) for i in range(2)]
            rs = [self.sb(es, "c_r%d" % i, [128, D]) for i in range(2)]
            stg = [self.sb(es, "c_st%d" % i, [128, 8, 512], BF16) for i in range(2)]
            scs = self.ln_scratch(es, "c1")
            pya, pyb, pga, pgb = [(self.ps[i], self.buf("ps%d" % i)) for i in range(4)]
            pm = [(self.ps[4], self.buf("ps4")), (self.ps[4], self.buf("ps4"))]
            ptb = [(self.ps[5], self.buf("ps5")), (self.ps[6], self.buf("ps6"))]
            deferred = []

            def flush():
                while deferred:
                    deferred.pop(0)()
            T5 = tiles(L, 512)
            T1 = tiles(L, 128)

            def c1_loads(g):
                T0_, gn_ = T5[g]
                ht_, HB_ = hts[g % 2]
                yf_, YB_ = yfs[g % 2]
                dn_, DB_ = dnt[g % 2]
                self.DMA('sp', ht_[:, :, :gn_], self.hT_d[:, T0_:T0_ + gn_].rearrange("(k p) t -> p k t", p=128), rd=[self.buf("hT%d" % g)], wr=[HB_])
                self.DMA('sp', yf_[:, :, :gn_], self.yfT_d[:, T0_:T0_ + gn_].rearrange("(g p) t -> p g t", p=128), rd=[self.buf("yfT%d" % g)], wr=[YB_])
                self.DMA('sp', dn_[:, :, :gn_], self.dnT_d[:, T0_:T0_ + gn_].rearrange("(g p) t -> p g t", p=128), rd=[self.buf("dnT%d" % g)], wr=[DB_])

            def c1_hr(jj):
                if jj < len(T1):
                    t0_, m_ = T1[jj]
                    self.DMA('sp', hrs[jj % 2][0][:m_, :], self.h_tm[t0_:t0_ + m_, :], rd=[self.buf("h_tm%d" % jj)], wr=[hrs[jj % 2][1]])
            c1_hr(0)
            for gi, (T0, gn) in enumerate(T5):
                ht, HB = hts[gi % 2]
                yf, YB = yfs[gi % 2]
                dn, DB = dnt[gi % 2]
                stage, STB = stg[gi % 2]
                if gi == 0:
                    c1_loads(0)
                if gi + 1 < len(T5):
                    c1_loads(gi + 1)
                for fc in range(8):
                    fs = slice(fc * 128, (fc + 1) * 128)
                    for k in range(8):
                        self.MM(pga[0][:, :gn], wg[:, k, fs], ht[:, k, :gn], start=(k == 0), stop=(k == 7), rd=[WGB[fc // 4], HB], wr=[pga[1]])
                    for k in range(8):
                        self.MM(pgb[0][:, :gn], wg[:, k, 1024 + fc * 128:1024 + (fc + 1) * 128], ht[:, k, :gn], start=(k == 0), stop=(k == 7), rd=[WGB[2 + fc // 4], HB], wr=[pgb[1]])
                    for c in range(4):
                        self.MM(pya[0][:, :gn], wfp[:, c, fs], yf[:, c, :gn], start=(c == 0), stop=(c == 3), rd=[WFB, YB], wr=[pya[1]])
                    for c in range(4):
                        self.MM(pyb[0][:, :gn], wdp[:, c, fs], dn[:, c, :gn], start=(c == 0), stop=(c == 3), rd=[WDPB, DB], wr=[pyb[1]])
                    ga, GAB = gas[fc % 2]
                    gb, GBB = gbs[fc % 2]
                    t1, T1B = t1s[fc % 2]
                    t2, T2B = t2s[fc % 2]
                    self.ACT(ga[:, :gn], pga[0][:, :gn], AF.Sigmoid, rd=[pga[1]], wr=[GAB])
                    self.ACT(gb[:, :gn], pgb[0][:, :gn], AF.Sigmoid, rd=[pgb[1]], wr=[GBB])
                    self.TT('dve', t1[:, :gn], pya[0][:, :gn], ga[:, :gn], ALU.mult, rd=[pya[1], GAB], wr=[T1B])
                    self.TT('dve', t2[:, :gn], pyb[0][:, :gn], gb[:, :gn], ALU.mult, rd=[pyb[1], GBB], wr=[T2B])
                    self.TT('pool', GT[:, fc, :gn], t1[:, :gn], t2[:, :gn], ALU.add, rd=[T1B, T2B], wr=[GTB])
                    if fc == 1:
                        flush()
                for s, (s0, m) in enumerate(tiles(gn, 128)):
                    j = gi * 4 + s
                    t0 = T0 + s0
                    hr, HRB = hrs[j % 2]
                    rr, RB = rs[j % 2]
                    c1_hr(j + 1)
                    for half in range(2):
                        pmt, PMB = pm[half]
                        for k in range(8):
                            self.MM(pmt[:m, :], GT[:, k, s0:s0 + m], wo[:, k, half * 512:(half + 1) * 512], start=(k == 0), stop=(k == 7), rd=[GTB, WOB], wr=[PMB])
                        self.STT(rr[:m, half * 512:(half + 1) * 512], hr[:m, half * 512:(half + 1) * 512], ALPHA, pmt[:m, :], ALU.mult, ALU.add, rd=[HRB, PMB], wr=[RB])
                    flush()
                    self.ln_tile(rr, RB, m, gbt, bbt, GB, scs[j % 2])
                    self.DMA('pool', self.h1_tm[t0:t0 + m, :], rr[:m, :], rd=[RB], wr=[self.buf("h1_tm%d" % j)])

                    def later(rr=rr, RB=RB, m=m, stage=stage, STB=STB, s0=s0, j=j, lastsub=(s0 + m == gn), T0=T0, gn=gn, gi=gi):
                        self.transpose_tile(rr, RB, m, stage, STB, s0, ptb, j)
                        if lastsub:
                            self.DMA('pool', self.h1T_d[:, T0:T0 + gn].rearrange("(k p) t -> p k t", p=128), stage[:, :, :gn], rd=[STB], wr=[self.buf("h1T%d" % gi)])
                    deferred.append(later)
            flush()
        mk.barrier()

    def phaseC2(self, l, last):
        mk = self.mk
        with contextlib.ExitStack() as es:
            wup, WB = self.sb(es, "f_wup", [128, 8, 4096], BF16)
            wdn, _ = self.sb(es, "f_wdn", [128, 32, D], BF16)
            WUB = [Buf("wup%d" % c) for c in range(8)]
            WDB = [Buf("wdn%d" % c) for c in range(4)]
            for c in range(8):
                self.DMA('pool', wup[:, :, c * 512:(c + 1) * 512], self.w_up[l, :, c * 512:(c + 1) * 512].rearrange("(k p) n -> p k n", p=128), wr=[WUB[c]])
            for c in range(4):
                self.DMA('pool', wdn[:, c * 8:(c + 1) * 8, :], self.w_dn[l, c * 1024:(c + 1) * 1024, :].rearrange("(k p) n -> p k n", p=128), wr=[WDB[c]])
            gbt, GB = self.sb(es, "f_g", [128, D])
            bbt, _ = self.sb(es, "f_b", [128, D])
            self.DMA('sp', gbt[:], self.ln2[l, 0:1, :].to_broadcast([128, D]), wr=[GB])
            self.DMA('sp', bbt[:], self.ln2[l, 1:2, :].to_broadcast([128, D]), wr=[GB])
            hts = [self.sb(es, "f_h%d" % i, [128, 8, 256], BF16) for i in range(2)]
            aT, ATB = self.sb(es, "f_aT", [128, 32, 256], BF16)
            a1s = [self.sb(es, "f_a1%d" % i, [128, 256], BF16) for i in range(3)]
            hrs = [self.sb(es, "f_hr%d" % i, [128, D]) for i in range(2)]
            rs = [self.sb(es, "f_r%d" % i, [128, D]) for i in range(2)]
            stg = [self.sb(es, "f_st%d" % i, [128, 8, 256], BF16) for i in range(2)]
            scs = self.ln_scratch(es, "c2")
            pu = [(self.ps[i], self.buf("ps%d" % i)) for i in range(3)]
            pm = [(self.ps[3], self.buf("ps3")), (self.ps[4], self.buf("ps4"))]
            ptb = [(self.ps[5], self.buf("ps5")), (self.ps[6], self.buf("ps6"))]
            ATBs = [Buf() for _ in range(32)]
            deferred = []

            def flush():
                while deferred:
                    deferred.pop(0)()
            T2 = tiles(L, 256)
            T1 = tiles(L, 128)

            def c2_loads(g):
                T0_, gn_ = T2[g]
                self.DMA('sp', hts[g % 2][0][:, :, :gn_], self.h1T_d[:, T0_:T0_ + gn_].rearrange("(k p) t -> p k t", p=128), rd=[self.buf("h1T%d" % (T0_ // 512))], wr=[hts[g % 2][1]])

            def c2_hr(jj):
                if jj < len(T1):
                    t0_, m_ = T1[jj]
                    self.DMA('sp', hrs[jj % 2][0][:m_, :], self.h1_tm[t0_:t0_ + m_, :], rd=[self.buf("h1_tm%d" % jj)], wr=[hrs[jj % 2][1]])
            c2_hr(0)
            for gi, (T0, gn) in enumerate(T2):
                ht, HB = hts[gi % 2]
                stage, STB = stg[gi % 2]
                if gi == 0:
                    c2_loads(0)
                if gi + 1 < len(T2):
                    c2_loads(gi + 1)
                for fc in range(32):
                    pt, PBk = pu[fc % 3]
                    a1, A1B = a1s[fc % 3]
                    for k in range(8):
                        self.MM(pt[:, :gn], wup[:, k, fc * 128:(fc + 1) * 128], ht[:, k, :gn], start=(k == 0), stop=(k == 7), rd=[WUB[fc // 4], HB], wr=[PBk])
                    self.ACT(a1[:, :gn], pt[:, :gn], AF.Relu, rd=[PBk], wr=[A1B])
                    self.TT('pool', aT[:, fc, :gn], a1[:, :gn], a1[:, :gn], ALU.mult, rd=[A1B, ATB], wr=[ATBs[fc]])
                    if fc == 3:
                        flush()
                for s, (s0, m) in enumerate(tiles(gn, 128)):
                    j = gi * 2 + s
                    t0 = T0 + s0
                    hr, HRB = hrs[j % 2]
                    rr, RB = rs[j % 2]
                    c2_hr(j + 1)
                    for half in range(2):
                        pmt, PMB = pm[half]
                        for kc in range(32):
                            self.MM(pmt[:m, :], aT[:, kc, s0:s0 + m], wdn[:, kc, half * 512:(half + 1) * 512], start=(kc == 0), stop=(kc == 31), rd=[ATBs[kc], WDB[kc // 8]], wr=[PMB])
                        self.STT(rr[:m, half * 512:(half + 1) * 512], hr[:m, half * 512:(half + 1) * 512], ALPHA, pmt[:m, :], ALU.mult, ALU.add, rd=[HRB, PMB], wr=[RB])
                    flush()
                    self.ln_tile(rr, RB, m, gbt, bbt, GB, scs[j % 2])
                    if last:
                        if j == 0:
                            self.final.append(self.DMA('pool', self.y[0:112, :], rr[16:128, :], rd=[RB], wr=[self.buf("y%d" % j)]))
                        else:
                            self.final.append(self.DMA('pool', self.y[t0 - 16:t0 - 16 + m, :], rr[:m, :], rd=[RB], wr=[self.buf("y%d" % j)]))
                    else:
                        self.DMA('pool', self.h_tm[t0:t0 + m, :], rr[:m, :], rd=[RB], wr=[self.buf("h_tm%d" % j)])

                        def later(rr=rr, RB=RB, m=m, stage=stage, STB=STB, s0=s0, j=j, lastsub=(s0 + m == gn), T0=T0, gn=gn):
                            self.transpose_tile(rr, RB, m, stage, STB, s0, ptb, j)
                            if lastsub:
                                self.DMA('pool', self.hT_d[:, T0:T0 + gn].rearrange("(k p) t -> p k t", p=128), stage[:, :, :gn], rd=[STB], wr=[self.buf("hT%d" % (T0 // 512))])
                        deferred.append(later)
            flush()
        mk.barrier()


_CONST = {}


def _consts():
    if _CONST:
        return _CONST
    HL = L // 2 + 1
    t = np.arange(HL, dtype=np.int64)
    m = (t[:, None] * t[None, :]) % L
    ang = (2.0 * np.pi / L) * m.astype(np.float64)
    dft = np.zeros((2, 2176, 2064), dtype=ml_dtypes.bfloat16)
    dft[0, :HL, :HL] = (np.cos(ang) / np.sqrt(L)).astype(np.float32).astype(ml_dtypes.bfloat16)
    dft[1, :HL, :HL] = (np.sin(ang) / np.sqrt(L)).astype(np.float32).astype(ml_dtypes.bfloat16)
    c = np.arange(128, dtype=np.int64)
    a2 = (2.0 * np.pi / 128) * ((c[:, None] * c[None, :]) % 128).astype(np.float64)
    cs = np.concatenate([np.cos(a2), -np.sin(a2)], axis=1) / np.sqrt(128.0)
    _CONST["dft"] = dft
    _CONST["cs128"] = cs.astype(np.float32).astype(ml_dtypes.bfloat16)
    return _CONST


def make_in_maps(inputs, cores):
    f = lambda a: np.ascontiguousarray(np.asarray(a, dtype=np.float32))
    c = _consts()
    conv = f(inputs["conv_w"])
    convw = np.ascontiguousarray(conv.reshape(2, 5, 12, 128).transpose(0, 3, 2, 1).reshape(2, 128, 60))
    al = f(inputs["a_log"]).reshape(2, 2, 1, 4)
    dtb = f(inputs["dt_bias"]).reshape(2, 2, 1, 4)
    adt = np.stack([np.broadcast_to(al, (2, 2, NCH, 4)).reshape(2, 264),
                    np.broadcast_to(dtb, (2, 2, NCH, 4)).reshape(2, 264)], axis=1)
    shared = {
        "meta": f(inputs["meta_tokens"]),
        "ln0": np.stack([f(inputs["ln0_g"]), f(inputs["ln0_b"])]),
        "w_in": f(inputs["w_in"]),
        "convw": convw,
        "adt": np.ascontiguousarray(adt),
        "dng": f(inputs["delta_norm_g"]).reshape(2, 128, 1),
        "w_fp": f(inputs["w_fourier_proj"]),
        "w_dp": f(inputs["w_delta_proj"]),
        "w_out": f(inputs["w_out"]),
        "ln1": np.stack([f(inputs["ln1_g"]), f(inputs["ln1_b"])], axis=1),
        "w_up": f(inputs["w_up"]),
        "w_dn": f(inputs["w_down"]),
        "ln2": np.stack([f(inputs["ln2_g"]), f(inputs["ln2_b"])], axis=1),
        "dft": c["dft"],
        "cs128": c["cs128"],
    }
    x = f(inputs["x"])
    return [dict(shared, x=x[b]) for b in cores]


_PROG = {}


def kernel(**inputs):
    if "p" not in _PROG:
        _PROG["p"] = Prog()
    nc = _PROG["p"].nc
    in_maps = make_in_maps(inputs, list(range(8)))
    res = run_bass_kernel_spmd(nc, in_maps, core_ids=list(range(8)))
    return np.stack([np.asarray(r["y"], dtype=np.float32) for r in res.results], axis=0)
```

```python
import contextlib
import numpy as np
import ml_dtypes
import concourse.bass as bass
import concourse.mybir as mybir
from concourse.bass_utils import run_bass_kernel_spmd

F32 = mybir.dt.float32
BF16 = mybir.dt.bfloat16
AF = mybir.ActivationFunctionType
ALU = mybir.AluOpType

L = 4112
D = 1024
NCH = 33
ALPHA = 4.0 ** 0.25
COMPUTE = ('pe', 'act', 'dve', 'pool')


def tiles(n, step):
    return [(t0, min(step, n - t0)) for t0 in range(0, n, step)]


class Buf:
    __slots__ = ('name', 'last_w', 'rd_c', 'rd_d', 'excl')

    def __init__(self, name='', excl=False):
        self.name = name
        self.excl = excl
        self.last_w = None
        self.rd_c = {}
        self.rd_d = []


class MK:
    def __init__(self, nc, n_dma_sems=32):
        self.nc = nc
        self.ops = {k: [] for k in ('pe', 'act', 'dve', 'pool', 'sp')}
        self.n_dma_sems = n_dma_sems
        self.dma_rr = {k: 0 for k in self.ops}
        self.dma_last = {}
        self.pending = {k: set() for k in self.ops}
        self.open_dma = []
        self.all_dma = []

    def barrier(self):
        evs = set()
        for e, lst in self.ops.items():
            if lst:
                evs.add((e, len(lst) - 1))
        for ev in self.open_dma:
            evs.add(ev)
        self.open_dma = []
        for e in self.ops:
            self.pending[e] |= evs

    def op(self, eng, fn, reads=(), writes=(), dma=False):
        lst = self.ops[eng]
        idx = len(lst)
        deps = set()
        if any(b.excl for b in reads):
            writes = list(writes) + [b for b in reads if b.excl]
            reads = [b for b in reads if not b.excl]

        def add(ev):
            if ev is None:
                return
            if ev[0] == 'pe' and eng == 'pe' and not dma:
                return
            deps.add(ev)
        for b in reads:
            add(b.last_w)
        for b in writes:
            add(b.last_w)
            for e, i in b.rd_c.items():
                add((e, i))
            for ev in b.rd_d:
                add(ev)
        if self.pending[eng]:
            for ev in self.pending[eng]:
                if ev[0] == eng and not self.ops[ev[0]][ev[1]]['dma']:
                    continue
                deps.add(ev)
            self.pending[eng] = set()
        slot = None
        if dma:
            slot = self.dma_rr[eng]
            self.dma_rr[eng] = (slot + 1) % self.n_dma_sems
            prev = self.dma_last.get((eng, slot))
            if prev is not None:
                deps.add(prev)
            self.dma_last[(eng, slot)] = (eng, idx)
            self.open_dma.append((eng, idx))
            self.all_dma.append((eng, idx))
        lst.append(dict(fn=fn, deps=deps, dma=dma, slot=slot, sig=False))
        me = (eng, idx)
        for b in reads:
            if dma:
                b.rd_d.append(me)
            else:
                b.rd_c[eng] = idx
        for b in writes:
            b.last_w = me
            b.rd_c = {}
            b.rd_d = []
        return me

    def finalize(self, final_waits=()):
        nc = self.nc
        ops = self.ops
        for eng, lst in ops.items():
            for o in lst:
                for (e, i) in o['deps']:
                    if not ops[e][i]['dma']:
                        ops[e][i]['sig'] = True
        st = contextlib.ExitStack()
        csem = {e: st.enter_context(nc.semaphore('s_' + e)) for e in COMPUTE}
        dsem = {}
        for eng in ops:
            if any(o['dma'] for o in ops[eng]):
                for s in range(self.n_dma_sems):
                    dsem[(eng, s)] = st.enter_context(nc.semaphore('d_%s_%d' % (eng, s)))
        ccount = {e: 0 for e in COMPUTE}
        dcount = {k: 0 for k in dsem}
        for eng, lst in ops.items():
            for o in lst:
                if o['dma']:
                    k = (eng, o['slot'])
                    dcount[k] += 16
                    o['ev'] = (('d',) + k, dcount[k])
                elif o['sig']:
                    ccount[eng] += 1
                    o['ev'] = (('c', eng), ccount[eng])
                else:
                    o['ev'] = None

        def semof(key):
            return csem[key[1]] if key[0] == 'c' else dsem[(key[1], key[2])]

        handles = {'pe': 'tensor', 'act': 'scalar', 'dve': 'vector', 'pool': 'gpsimd', 'sp': 'sync'}
        final_evs = [ops[e][i]['ev'] for (e, i) in final_waits]

        def emit_engine(eng, h):
            waited = {}
            for o in ops[eng]:
                need = {}
                for (e, i) in o['deps']:
                    key, val = ops[e][i]['ev']
                    if waited.get(key, 0) >= val:
                        continue
                    if need.get(key, 0) < val:
                        need[key] = val
                for key, val in need.items():
                    h.wait_ge(semof(key), val)
                    waited[key] = val
                ins = o['fn'](h)
                if o['dma']:
                    ins.then_inc(dsem[(eng, o['slot'])], 16)
                elif o['sig']:
                    ins.then_inc(csem[eng], 1)
            if eng == 'sp':
                for key, val in final_evs:
                    if waited.get(key, 0) < val:
                        h.wait_ge(semof(key), val)
                        waited[key] = val

        with nc.Block() as block:
            for eng in ('pe', 'act', 'dve', 'pool', 'sp'):
                getattr(block, handles[eng])(lambda h, eng=eng: emit_engine(eng, h))
        st.close()


class Prog:
    def __init__(self, debug=False, upto=99, nlayers=2):
        self.debug = debug
        self.upto = upto
        self.nlayers = nlayers
        nc = self.nc = bass.Bass("TRN2", target_bir_lowering=False)
        self.mk = MK(nc)
        self.final = []
        I = lambda name, shape, dt=F32: nc.dram_tensor(name, list(shape), dt, kind="ExternalInput").ap()
        self.x = I("x", [4096, D])
        self.meta = I("meta", [16, D])
        self.ln0 = I("ln0", [2, D])
        self.w_in = I("w_in", [2, D, 4624])
        self.convw = I("convw", [2, 128, 60])
        self.adt = I("adt", [2, 2, 264])
        self.dng = I("dng", [2, 128, 1])
        self.w_fp = I("w_fp", [2, 512, D])
        self.w_dp = I("w_dp", [2, 512, D])
        self.w_out = I("w_out", [2, D, D])
        self.ln1 = I("ln1", [2, 2, D])
        self.w_up = I("w_up", [2, D, 4096])
        self.w_dn = I("w_dn", [2, 4096, D])
        self.ln2 = I("ln2", [2, 2, D])
        self.dft = I("dft", [2, 2176, 2064], BF16)
        self.cs128 = I("cs128", [128, 256], BF16)
        self.y = nc.dram_tensor("y", [4096, D], F32, kind="ExternalOutput").ap()
        skind = "ExternalOutput" if debug else "Internal"
        S = lambda name, shape, dt: nc.dram_tensor(name, list(shape), dt, kind=skind).ap()
        self.h_tm = S("h_tm", [L, D], F32)
        self.hT_d = S("hT_d", [D, L], BF16)
        self.yfT_d = S("yfT_d", [512, L], BF16)
        self.qkT_d = S("qkT_d", [D, NCH * 128], BF16)
        self.kvtm_d = S("kvtm_d", [NCH, 128, 8, 128], BF16)
        self.dnT_d = S("dnT_d", [512, L], BF16)
        self.h1_tm = S("h1_tm", [L, D], F32)
        self.h1T_d = S("h1T_d", [D, L], BF16)
        self.oacc_d = S("oacc_d", [128, NCH * 512], F32)
        self.sc_d = S("sc_d", [128, 12 * NCH * 8], F32)
        self.B = {}
        self.ps = [nc.alloc_psum_tensor("ps%d" % i, [128, 512], F32) for i in range(7)]
        self.psb = nc.alloc_psum_tensor("psb", [128, 1024], BF16)
        self.PB = {}
        self.build()

    def buf(self, name):
        b = self.B.get(name)
        if b is None:
            b = self.B[name] = Buf(name, excl=name.startswith("ps"))
        return b

    def MM(self, out, lhsT, rhs, start=True, stop=True, rd=(), wr=()):
        self.mk.op('pe', lambda e: e.matmul(out, lhsT=lhsT, rhs=rhs, start=start, stop=stop), reads=rd, writes=wr)

    def TR(self, out, in_, ident, rd=(), wr=()):
        self.mk.op('pe', lambda e: e.transpose(out=out, in_=in_, identity=ident), reads=rd, writes=wr)

    def ACT(self, out, in_, func, bias=0.0, scale=1.0, accum_out=None, rd=(), wr=()):
        if accum_out is None:
            self.mk.op('act', lambda e: e.activation(out=out, in_=in_, func=func, bias=bias, scale=scale), reads=rd, writes=wr)
        else:
            self.mk.op('act', lambda e: e.activation(out=out, in_=in_, func=func, bias=bias, scale=scale, accum_out=accum_out), reads=rd, writes=wr)

    def COPY(self, eng, out, in_, rd=(), wr=()):
        if eng == 'act':
            self.mk.op('act', lambda e: e.copy(out=out, in_=in_), reads=rd, writes=wr)
        else:
            self.mk.op(eng, lambda e: e.tensor_copy(out=out, in_=in_), reads=rd, writes=wr)

    def TS(self, eng, out, in0, s1, s2, op0, op1=None, rd=(), wr=()):
        if op1 is None:
            self.mk.op(eng, lambda e: e.tensor_scalar(out=out, in0=in0, scalar1=s1, scalar2=None, op0=op0), reads=rd, writes=wr)
        else:
            self.mk.op(eng, lambda e: e.tensor_scalar(out=out, in0=in0, scalar1=s1, scalar2=s2, op0=op0, op1=op1), reads=rd, writes=wr)

    def TT(self, eng, out, in0, in1, op, rd=(), wr=()):
        self.mk.op(eng, lambda e: e.tensor_tensor(out=out, in0=in0, in1=in1, op=op), reads=rd, writes=wr)

    def STT(self, out, in0, scalar, in1, op0, op1, rd=(), wr=()):
        self.mk.op('dve', lambda e: e.scalar_tensor_tensor(out=out, in0=in0, scalar=scalar, in1=in1, op0=op0, op1=op1), reads=rd, writes=wr)

    def MEMSET(self, eng, ap, val, wr=()):
        self.mk.op(eng, lambda e: e.memset(ap, val), writes=wr)

    def DMA(self, q, out, in_, rd=(), wr=()):
        return self.mk.op(q, lambda e: e.dma_start(out=out, in_=in_), reads=rd, writes=wr, dma=True)

    def sb(self, es, name, shape, dt=F32):
        self._uid = getattr(self, "_uid", 0) + 1
        t = es.enter_context(self.nc.sbuf_tensor("%s_u%d" % (name, self._uid), list(shape), dt))
        return t, Buf(name)

    def ln_tile(self, xt, XB, n, gbt, bbt, GB, sc, eps=1e-5):
        stats, mv, rstd, SB_ = sc
        for c in range(2):
            self.mk.op('dve', lambda e, c=c: e.bn_stats(out=stats[:n, c, :], in_=xt[:n, c * 512:(c + 1) * 512]), reads=[XB], writes=[SB_])
        self.mk.op('dve', lambda e: e.bn_aggr(out=mv[:n, :], in_=stats[:n, :, :]), reads=[SB_], writes=[SB_])
        self.ACT(rstd[:n, :], mv[:n, 1:2], AF.Sqrt, bias=eps, rd=[SB_], wr=[SB_])
        self.mk.op('dve', lambda e: e.reciprocal(out=rstd[:n, :], in_=rstd[:n, :]), reads=[SB_], writes=[SB_])
        self.TS('dve', xt[:n, :], xt[:n, :], mv[:n, 0:1], rstd[:n, 0:1], ALU.subtract, ALU.mult, rd=[XB, SB_], wr=[XB])
        self.TT('pool', xt[:n, :], xt[:n, :], gbt[:n, :], ALU.mult, rd=[XB, GB], wr=[XB])
        self.TT('pool', xt[:n, :], xt[:n, :], bbt[:n, :], ALU.add, rd=[XB, GB], wr=[XB])

    def ln_scratch(self, es, tag):
        out = []
        for i in range(2):
            stats, b = self.sb(es, "%s_st%d" % (tag, i), [128, 2, 6])
            mv, _ = self.sb(es, "%s_mv%d" % (tag, i), [128, 2])
            rstd, _ = self.sb(es, "%s_rs%d" % (tag, i), [128, 1])
            out.append((stats, mv, rstd, b))
        return out

    def transpose_tile(self, xt, XB, n, stage, STB, off, pbanks, cnt):
        for half in range(2):
            pt, PBk = pbanks[half]
            for k in range(4):
                kk = half * 4 + k
                self.TR(pt[:, k * 128:k * 128 + n], xt[:n, kk * 128:(kk + 1) * 128], self.ident[:n, :n], rd=[XB, self.IDB], wr=[PBk])
            src = pt[:, :].rearrange("p (k t) -> p k t", k=4)[:, :, :n]
            dst = stage[:, half * 4:(half + 1) * 4, off:off + n]
            self.COPY('act' if (cnt + half) % 2 == 0 else 'dve', dst, src, rd=[PBk], wr=[STB])

    def build(self):
        nc, mk = self.nc, self.mk
        with contextlib.ExitStack() as es0:
            self.ident, self.IDB = self.sb(es0, "ident", [128, 128])
            self.ident16, _ = self.sb(es0, "ident16", [128, 128], BF16)
            self.ones32, _ = self.sb(es0, "ones32", [128, 128])
            self.ones16, _ = self.sb(es0, "ones16", [128, 128], BF16)
            self.triu, _ = self.sb(es0, "triu", [128, 128])
            self.tril, _ = self.sb(es0, "tril", [128, 128])
            self.m1f, _ = self.sb(es0, "m1f", [128, 128])
            self.m1b, _ = self.sb(es0, "m1b", [128, 128])
            self.nm16f, _ = self.sb(es0, "nm16f", [128, 4, 128], BF16)
            self.nm16b, _ = self.sb(es0, "nm16b", [128, 4, 128], BF16)
            CB = self.IDB
            sel = lambda t, pat, cm, op, fill, base=0: mk.op('pool', lambda e: e.affine_select(out=t[:], in_=t[:], pattern=pat, compare_op=op, fill=fill, base=base, channel_multiplier=cm), reads=[CB], writes=[CB])
            self.MEMSET('pool', self.ident[:], 0.0, wr=[CB])
            sel(self.ident, [[-1, 128]], 1, ALU.not_equal, 1.0)
            self.COPY('pool', self.ident16[:], self.ident[:], rd=[CB], wr=[CB])
            self.MEMSET('pool', self.ones32[:], 1.0, wr=[CB])
            self.MEMSET('pool', self.ones16[:], 1.0, wr=[CB])
            self.MEMSET('pool', self.triu[:], 1.0, wr=[CB])
            sel(self.triu, [[1, 128]], -1, ALU.is_ge, 0.0)
            self.MEMSET('pool', self.tril[:], 1.0, wr=[CB])
            sel(self.tril, [[-1, 128]], 1, ALU.is_ge, 0.0)
            self.MEMSET('pool', self.m1f[:], 1.0, wr=[CB])
            sel(self.m1f, [[-1, 128]], 1, ALU.is_gt, 0.0)
            self.MEMSET('pool', self.m1b[:], 1.0, wr=[CB])
            sel(self.m1b, [[1, 128]], -1, ALU.is_gt, 0.0)
            self.MEMSET('pool', self.nm16f[:], 0.0, wr=[CB])
            sel(self.nm16f, [[0, 4], [1, 128]], -1, ALU.is_ge, -1e5)
            self.MEMSET('pool', self.nm16b[:], 0.0, wr=[CB])
            sel(self.nm16b, [[0, 4], [-1, 128]], 1, ALU.is_ge, -1e5)
            self.phase0()
            for l in range(self.nlayers):
                last = (l == 1)
                if self.upto >= 1:
                    self.phaseA(l)
                if self.upto >= 2:
                    self.phaseB1(l)
                if self.upto >= 3:
                    self.phaseB2(l)
                with contextlib.ExitStack() as esw:
                    if self.upto >= 5:
                        W_ = self.c1_weights(esw, l)
                    if self.upto >= 4:
                        self.phaseB3(l)
                    if self.upto >= 5:
                        self.phaseC1(l, W_)
                if self.upto >= 6:
                    self.phaseC2(l, last)
            mk.barrier()
            mk.finalize(final_waits=list(mk.all_dma))

    def phase0(self):
        mk = self.mk
        with contextlib.ExitStack() as es:
            gbt, GB = self.sb(es, "p0_g", [128, D])
            bbt, _ = self.sb(es, "p0_b", [128, D])
            self.DMA('sp', gbt[:], self.ln0[0:1, :].to_broadcast([128, D]), wr=[GB])
            self.DMA('sp', bbt[:], self.ln0[1:2, :].to_broadcast([128, D]), wr=[GB])
            xts = [self.sb(es, "p0_x%d" % i, [128, D]) for i in range(6)]
            stg = [self.sb(es, "p0_s%d" % i, [128, 8, 512], BF16) for i in range(2)]
            scs = self.ln_scratch(es, "p0")
            pb = [(self.ps[0], self.buf("ps0")), (self.ps[1], self.buf("ps1"))]
            pb2 = [(self.ps[2], self.buf("ps2")), (self.ps[3], self.buf("ps3"))]
            pend = []
            for gi, (T0, gn) in enumerate(tiles(L, 512)):
                stage, STB = stg[gi % 2]
                for s, (s0, m) in enumerate(tiles(gn, 128)):
                    j = gi * 4 + s
                    t0 = T0 + s0
                    xt, XB = xts[j % 6]
                    if j == 0:
                        self.DMA('sp', xt[0:16, :], self.meta[:, :], wr=[XB])
                        self.DMA('sp', xt[16:128, :], self.x[0:112, :], wr=[XB])
                    else:
                        self.DMA('sp', xt[:m, :], self.x[t0 - 16:t0 - 16 + m, :], wr=[XB])
                    self.ln_tile(xt, XB, m, gbt, bbt, GB, scs[j % 2])
                    while pend:
                        pend.pop(0)()

                    def part2(xt=xt, XB=XB, m=m, t0=t0, j=j, stage=stage, STB=STB, s0=s0, lastsub=(s0 + m == gn), T0=T0, gn=gn, gi=gi):
                        self.DMA('pool', self.h_tm[t0:t0 + m, :], xt[:m, :], rd=[XB], wr=[self.buf("h_tm%d" % j)])
                        self.transpose_tile(xt, XB, m, stage, STB, s0, pb if j % 2 == 0 else pb2, j)
                        if lastsub:
                            self.DMA('pool', self.hT_d[:, T0:T0 + gn].rearrange("(k p) t -> p k t", p=128), stage[:, :, :gn], rd=[STB], wr=[self.buf("hT%d" % gi)])
                    pend.append(part2)
            while pend:
                pend.pop(0)()
        mk.barrier()

    def phaseA(self, l):
        mk = self.mk
        HL = L // 2 + 1
        NJ = 17
        with contextlib.ExitStack() as es:
            cs, CSB = self.sb(es, "a_cs", [128, 256], BF16)
            self.DMA('sp', cs[:], self.cs128[:, :], wr=[CSB])
            A_sb, AB = self.sb(es, "a_A", [128, NJ, 1024], BF16)
            Ue, UEB = self.sb(es, "a_Ue", [128, 4, NJ * 128], BF16)
            Uo, UOB = self.sb(es, "a_Uo", [128, 4, NJ * 128], BF16)
            ys, YSB = self.sb(es, "a_ys", [128, 4, L], BF16)
            ABs = [Buf("A%d" % j) for j in range(NJ)]
            with contextlib.ExitStack() as es1:
                wuf, WB = self.sb(es1, "a_wuf", [128, 8, 512], BF16)
                self.DMA('pool', wuf[:], self.w_in[l, :, 0:512].rearrange("(k p) n -> p k n", p=128), wr=[WB])
                u32, U32B = self.sb(es1, "a_u32", [128, 4, L])
                hts = [self.sb(es1, "a_h%d" % i, [128, 8, 512], BF16) for i in range(2)]
                pu = [(self.ps[i], self.buf("ps%d" % i)) for i in range(4)]
                T5 = tiles(L, 512)

                def a_load(g):
                    T0_, gn_ = T5[g]
                    self.DMA('sp', hts[g % 2][0][:, :, :gn_], self.hT_d[:, T0_:T0_ + gn_].rearrange("(k p) t -> p k t", p=128), rd=[self.buf("hT%d" % g)], wr=[hts[g % 2][1]])
                a_load(0)
                UBs = [Buf("u%d" % g) for g in range(len(T5))]
                for gi, (T0, gn) in enumerate(T5):
                    ht, HB = hts[gi % 2]
                    if gi + 1 < len(T5):
                        a_load(gi + 1)
                    for g in range(4):
                        pt, PBk = pu[g]
                        for k in range(8):
                            self.MM(pt[:, :gn], wuf[:, k, g * 128:(g + 1) * 128], ht[:, k, :gn], start=(k == 0), stop=(k == 7), rd=[WB, HB], wr=[PBk])
                        self.COPY('act' if g % 2 == 0 else 'dve', u32[:, g, T0:T0 + gn], pt[:, :gn], rd=[PBk], wr=[UBs[gi]])
                self.MEMSET('pool', Ue[:, :, HL:NJ * 128], 0.0, wr=[UEB])
                self.MEMSET('pool', Uo[:, :, HL:NJ * 128], 0.0, wr=[UOB])
                self.MEMSET('pool', Uo[:, :, 0:1], 0.0, wr=[UOB])
                self.MEMSET('pool', Uo[:, :, HL - 1:HL], 0.0, wr=[UOB])
                self.COPY('pool', Ue[:, :, 0:1], u32[:, :, 0:1], rd=UBs, wr=[UEB])
                self.COPY('pool', Ue[:, :, HL - 1:HL], u32[:, :, HL - 1:HL], rd=UBs, wr=[UEB])
                for g in range(4):
                    self.TT('dve', Ue[:, g, 1:HL - 1], u32[:, g, 1:HL - 1], u32[:, g, L - 1:HL - 1:-1], ALU.add, rd=UBs, wr=[UEB])
                    self.TT('dve', Uo[:, g, 1:HL - 1], u32[:, g, 1:HL - 1], u32[:, g, L - 1:HL - 1:-1], ALU.subtract, rd=UBs, wr=[UOB])
                pa = [(self.ps[4], self.buf("ps4")), (self.ps[5], self.buf("ps5"))]
                cnt = 0
                for j, (c0, m) in enumerate(tiles(NJ * 128, 128)):
                    for half in range(2):
                        pt, PBk = pa[half]
                        for gg in range(2):
                            g = half * 2 + gg
                            self.MM(pt[:m, gg * 256:gg * 256 + 128], Ue[:, g, c0:c0 + m], cs[:, 0:128], rd=[UEB, CSB], wr=[PBk])
                            self.MM(pt[:m, gg * 256 + 128:gg * 256 + 256], Uo[:, g, c0:c0 + m], cs[:, 128:256], rd=[UOB, CSB], wr=[PBk])
                        self.COPY('act' if cnt % 2 == 0 else 'dve', A_sb[:m, j, half * 512:(half + 1) * 512], pt[:m, :], rd=[PBk], wr=[ABs[j]])
                        cnt += 1
            mk.barrier()
            with contextlib.ExitStack() as es2:
                cts = [self.sb(es2, "a_ct%d" % i, [128, NJ, 512], BF16) for i in range(2)]
                sts = [self.sb(es2, "a_st%d" % i, [128, NJ, 512], BF16) for i in range(2)]
                pcs = [self.sb(es2, "a_pc%d" % i, [128, 512]) for i in range(2)]
                pb4 = [(self.ps[i], self.buf("ps%d" % i)) for i in range(4)]
                TH = tiles(HL, 512)

                def d_load(ti):
                    T0_, n_ = TH[ti]
                    n_ = max(n_, 16)
                    self.DMA('sp', cts[ti % 2][0][:, :, :n_], self.dft[0, :, T0_:T0_ + n_].rearrange("(j p) t -> p j t", p=128), wr=[cts[ti % 2][1]])
                    self.DMA('pool', sts[ti % 2][0][:, :, :n_], self.dft[1, :, T0_:T0_ + n_].rearrange("(j p) t -> p j t", p=128), wr=[sts[ti % 2][1]])
                d_load(0)
                cnt = 0
                for ti, (T0, n) in enumerate(TH):
                    ct, CTB = cts[ti % 2]
                    st, STB_ = sts[ti % 2]
                    nc_ = max(n, 16)
                    if ti + 1 < len(TH):
                        d_load(ti + 1)
                    for g in range(4):
                        pP, PPB = pb4[(2 * g) % 4]
                        pQ, PQB = pb4[(2 * g + 1) % 4]
                        for j in range(NJ):
                            kp = 128
                            self.MM(pP[:, :nc_], A_sb[:kp, j, g * 256:g * 256 + 128], ct[:kp, j, :nc_], start=(j == 0), stop=(j == NJ - 1), rd=[ABs[j], CTB], wr=[PPB])
                        for j in range(NJ):
                            kp = 128
                            self.MM(pQ[:, :nc_], A_sb[:kp, j, g * 256 + 128:g * 256 + 256], st[:kp, j, :nc_], start=(j == 0), stop=(j == NJ - 1), rd=[ABs[j], STB_], wr=[PQB])
                        pc, PCB = pcs[cnt % 2]
                        cnt += 1
                        self.COPY('act', pc[:, :n], pP[:, :n], rd=[PPB], wr=[PCB])
                        self.TT('dve', ys[:, g, T0:T0 + n], pQ[:, :n], pc[:, :n], ALU.add, rd=[PQB, PCB], wr=[YSB])
                        a_ = max(T0, 1)
                        b_ = min(T0 + n, HL - 1)
                        if b_ > a_:
                            self.TT('dve', ys[:, g, L - a_:L - b_:-1], pc[:, a_ - T0:b_ - T0], pQ[:, a_ - T0:b_ - T0], ALU.subtract, rd=[PQB, PCB], wr=[YSB])
                for gi, (T0, gn) in enumerate(tiles(L, 512)):
                    self.DMA('pool' if gi % 2 else 'sp', self.yfT_d[:, T0:T0 + gn].rearrange("(g p) t -> p g t", p=128), ys[:, :, T0:T0 + gn], rd=[YSB], wr=[self.buf("yfT%d" % gi)])
        mk.barrier()

    def phaseB1(self, l):
        mk = self.mk
        with contextlib.ExitStack() as es:
            wq, WB = self.sb(es, "b_wq", [128, 8, 1536], BF16)
            WQB = [Buf("wq%d" % c) for c in range(3)]
            for c in range(3):
                self.DMA('pool', wq[:, :, c * 512:(c + 1) * 512], self.w_in[l, :, 512 + c * 512:1024 + c * 512].rearrange("(k p) n -> p k n", p=128), wr=[WQB[c]])
            wbd, _ = self.sb(es, "b_wbd", [128, 8, 16], BF16)
            self.DMA('pool', wbd[:], self.w_in[l, :, 2048:2064].rearrange("(k p) n -> p k n", p=128), wr=[WB])
            cw, CWB = self.sb(es, "b_cw", [128, 60])
            self.DMA('sp', cw[:], self.convw[l, :, :], wr=[CWB])
            dg, DGB = self.sb(es, "b_dg", [128, 60, 128], BF16)
            for i in range(60):
                self.TS('pool', dg[:, i, :], self.ident[:, :], cw[:, i:i + 1], None, ALU.mult, rd=[CWB, self.IDB], wr=[DGB])
            xpre, XPB = self.sb(es, "b_xp", [128, 12, L + 4], BF16)
            XPBs = [[Buf("xp%d_%d" % (cc, gi)) for gi in range(9)] for cc in range(12)]
            self.MEMSET('pool', xpre[:, :, 0:2], 0.0, wr=[XPB])
            self.MEMSET('pool', xpre[:, :, L + 2:L + 4], 0.0, wr=[XPB])
            bg, BGB = self.sb(es, "b_bg", [128, 4, NCH, 4])
            self.MEMSET('pool', bg[:], 0.0, wr=[BGB])
            ss, SSB = self.sb(es, "b_ss", [128, NCH, 8])
            hts = [self.sb(es, "b_h%d" % i, [128, 8, 512], BF16) for i in range(2)]
            pp = [(self.ps[i], self.buf("ps%d" % i)) for i in range(4)]
            pbgt, PBG = self.ps[4], self.buf("ps4")
            pss, PSS = self.ps[5], self.buf("ps5")
            ptr = [(self.ps[6], self.buf("ps6"))]
            cnt = 0
            for gi, (T0, gn) in enumerate(tiles(L, 512)):
                ht, HB = hts[gi % 2]
                self.DMA('sp', ht[:, :, :gn], self.hT_d[:, T0:T0 + gn].rearrange("(k p) t -> p k t", p=128), rd=[self.buf("hT%d" % gi)], wr=[HB])
                for cc in range(12):
                    pt, PBk = pp[cnt % 4]
                    for k in range(8):
                        self.MM(pt[:, :gn], wq[:, k, cc * 128:(cc + 1) * 128], ht[:, k, :gn], start=(k == 0), stop=(k == 7), rd=[WQB[cc // 4], HB], wr=[PBk])
                    self.COPY('act' if cnt % 2 == 0 else 'dve', xpre[:, cc, 2 + T0:2 + T0 + gn], pt[:, :gn], rd=[PBk, XPB], wr=[XPBs[cc][gi]])
                    cnt += 1
                for s, (s0, m) in enumerate(tiles(gn, 128)):
                    j = gi * 4 + s
                    for k in range(8):
                        self.MM(pbgt[:m, 0:16], ht[:, k, s0:s0 + m], wbd[:, k, :], start=(k == 0), stop=(k == 7), rd=[WB, HB], wr=[PBG])
                    self.COPY('dve', bg[:m, :, j, :], pbgt[:m, 0:16].rearrange("p (a h) -> p a h", a=4), rd=[PBG], wr=[BGB])
            self.bg, self.BGB = bg, BGB
            import os
            CUT = int(os.environ.get("B1CUT", "9"))
            if CUT <= 1:
                mk.barrier(); return
            zt, ZB = self.sb(es, "b_z", [128, 1024], BF16)
            self.MEMSET('pool', zt[:], 0.0, wr=[ZB])
            self.DMA('sp', self.qkT_d[:, L:NCH * 128].rearrange("(k p) t -> p k t", p=128), zt[:, 0:8 * 112].rearrange("p (k t) -> p k t", k=8), rd=[ZB], wr=[self.buf("qkT8")])
            self.DMA('sp', self.kvtm_d[32, :, :, :], zt[:, :].rearrange("p (c d) -> p c d", c=8), rd=[ZB], wr=[self.buf("kvtm8")])
            if CUT <= 2:
                mk.barrier(); return
            sacts = [self.sb(es, "b_sa%d" % i, [128, 512]) for i in range(3)]
            q16s = [self.sb(es, "b_q16%d" % i, [128, 512], BF16) for i in range(3)]
            sqs = [self.sb(es, "b_sq%d" % i, [128, 512], BF16) for i in range(2)]
            kvs = [self.sb(es, "b_kv%d" % i, [128, 4, 8, 128], BF16) for i in range(2)]
            cnt = 0
            pend = []
            for gi, (T0, gn) in enumerate(tiles(L, 512)):
                kv, KVB = kvs[gi % 2]
                subt = tiles(gn, 128)
                for cc in range(12):
                    pt, PBk = pp[cnt % 4]
                    sa, SAB = sacts[cnt % 3]
                    rdx = [DGB] + [XPBs[cc][g2] for g2 in (gi - 1, gi, gi + 1) if 0 <= g2 < 9] + [XPB]
                    for tau in range(5):
                        self.MM(pt[:, :gn], dg[:, cc * 5 + tau, :], xpre[:, cc, T0 + tau:T0 + tau + gn], start=(tau == 0), stop=(tau == 4), rd=rdx, wr=[PBk])
                    self.ACT(sa[:, :gn], pt[:, :gn], AF.Silu, rd=[PBk], wr=[SAB])
                    sq, SQB = sqs[cnt % 2]
                    if cc < 8:
                        q16, QB = q16s[cnt % 3]
                        self.COPY('dve', q16[:, :gn], sa[:, :gn], rd=[SAB], wr=[QB])
                        self.DMA('sp', self.qkT_d[cc * 128:(cc + 1) * 128, T0:T0 + gn], q16[:, :gn], rd=[QB], wr=[self.buf("qkT%d_%d" % (gi, cc))])
                        self.TT('pool', sq[:, :gn], sa[:, :gn], sa[:, :gn], ALU.mult, rd=[SAB], wr=[SQB])
                    while pend:
                        pend.pop(0)()

                    def part2(cc=cc, gi=gi, T0=T0, gn=gn, sa=sa, SAB=SAB, sq=sq, SQB=SQB, kv=kv, KVB=KVB, subt=subt):
                        if cc < 8:
                            for s, (s0, m) in enumerate(subt):
                                j = gi * 4 + s
                                self.MM(pss[:m, j * 8 + cc:j * 8 + cc + 1], sq[:, s0:s0 + m], self.ones16[:, 0:1], rd=[SQB, self.IDB], wr=[PSS])
                        if cc >= 4:
                            tp, TPB = ptr[0]
                            for s, (s0, m) in enumerate(subt):
                                self.TR(tp[:, s * 128:(s + 1) * 128], sa[:, s0:s0 + 128], self.ident[:, :], rd=[SAB, self.IDB], wr=[TPB])
                            ns = len(subt)
                            mlast = subt[-1][1]
                            if mlast == 128:
                                self.COPY('dve', kv[:, 0:ns, cc - 4, :], tp[:, 0:ns * 128].rearrange("p (s d) -> p s d", s=ns), rd=[TPB], wr=[KVB])
                            else:
                                self.COPY('dve', kv[:mlast, 0, cc - 4, :], tp[:mlast, 0:128], rd=[TPB], wr=[KVB])
                        if cc == 11:
                            if gn == 512:
                                self.DMA('sp', self.kvtm_d[gi * 4:gi * 4 + 4, :, :, :].rearrange("j p c d -> p j c d"), kv[:, :, :, :], rd=[KVB], wr=[self.buf("kvtm%d" % gi)])
                            else:
                                self.DMA('sp', self.kvtm_d[32, 0:gn, :, :], kv[:gn, 0, :, :], rd=[KVB, self.buf("kvtm8")], wr=[self.buf("kvtm8")])
                    pend.append(part2)
                    cnt += 1
            while pend:
                pend.pop(0)()
            self.MEMSET('pool', ss[:], 1.0, wr=[SSB])
            self.COPY('dve', ss[:, 0:32, :], pss[:, 0:256].rearrange("p (j c) -> p j c", c=8), rd=[PSS], wr=[SSB])
            self.COPY('dve', ss[:16, 32, :], pss[:16, 256:264], rd=[PSS], wr=[SSB])
            self.DMA('sp', self.sc_d[:, 0:NCH * 8], ss[:].rearrange("p j c -> p (j c)"), rd=[SSB], wr=[self.buf("sc_ss")])
            self.DMA('sp', self.sc_d[:, NCH * 8:NCH * 8 + 16 * NCH], bg[:].rearrange("p a j h -> p (a j h)"), rd=[BGB], wr=[self.buf("sc_bg")])
        mk.barrier()

    def phaseB2(self, l):
        mk = self.mk
        with contextlib.ExitStack() as es:
            ss, SCB = self.sb(es, "d_ss", [128, NCH, 8])
            bg, _ = self.sb(es, "d_bg", [128, 4, NCH, 4])
            self.DMA('sp', ss[:].rearrange("p j c -> p (j c)"), self.sc_d[:, 0:NCH * 8], rd=[self.buf("sc_ss")], wr=[SCB])
            self.DMA('sp', bg[:].rearrange("p a j h -> p (a j h)"), self.sc_d[:, NCH * 8:NCH * 8 + 16 * NCH], rd=[self.buf("sc_bg")], wr=[SCB])
            adt, _ = self.sb(es, "d_adt", [128, 2, 2, NCH, 4])
            self.DMA('sp', adt[:].rearrange("p a r j h -> p (a r j h)"), self.adt[l:l + 1, :, :].rearrange("o a n -> o (a n)").to_broadcast([128, 528]), wr=[SCB])
            names = ["rk2", "irk", "s2m", "scq"]
            A4 = {n: self.sb(es, "d_" + n, [128, NCH, 4])[0] for n in names}
            names8 = ["beta", "g", "b2", "nb2", "gc", "egc", "ekt", "cd", "b2e", "tmp8"]
            A8 = {n: self.sb(es, "d_" + n, [128, 2, NCH, 4])[0] for n in names8}
            nea, _ = self.sb(es, "d_nea", [128, 2, NCH, 4])
            R = [SCB]
            ssk = ss[:, :, 4:8]
            ssq = ss[:, :, 0:4]
            self.TS('dve', A4["rk2"][:], ssk, 1e-6, None, ALU.add, rd=R, wr=R)
            self.ACT(A4["irk"][:], A4["rk2"][:], AF.Sqrt, rd=R, wr=R)
            self.mk.op('dve', lambda e: e.reciprocal(out=A4["rk2"][:], in_=A4["rk2"][:]), reads=R, writes=R)
            self.TS('dve', A4["s2m"][:], ssq, 1e-6, 128.0, ALU.add, ALU.mult, rd=R, wr=R)
            self.mk.op('dve', lambda e: e.reciprocal(out=A4["s2m"][:], in_=A4["s2m"][:]), reads=R, writes=R)
            self.ACT(A4["scq"][:], A4["s2m"][:], AF.Sqrt, rd=R, wr=R)
            self.TS('dve', A4["s2m"][:], A4["s2m"][:], 1.0 / 128.0, None, ALU.mult, rd=R, wr=R)
            self.MEMSET('pool', A8["beta"][:], 0.0, wr=R)
            self.MEMSET('pool', A8["g"][:], 0.0, wr=R)
            self.ACT(nea[:], adt[:, 0, :, :, :], AF.Exp, rd=R, wr=R)
            for (ps_, js) in ((slice(0, 128), slice(0, 32)), (slice(0, 16), slice(32, 33))):
                self.ACT(A8["beta"][ps_, :, js, :], bg[ps_, 0:2, js, :], AF.Sigmoid, rd=R, wr=R)
                self.TT('dve', A8["tmp8"][ps_, :, js, :], bg[ps_, 2:4, js, :], adt[ps_, 1, :, js, :], ALU.add, rd=R, wr=R)
                self.ACT(A8["tmp8"][ps_, :, js, :], A8["tmp8"][ps_, :, js, :], AF.Exp, rd=R, wr=R)
                self.ACT(A8["tmp8"][ps_, :, js, :], A8["tmp8"][ps_, :, js, :], AF.Ln, bias=1.0, rd=R, wr=R)
                self.STT(A8["g"][ps_, :, js, :], A8["tmp8"][ps_, :, js, :], -1.0, nea[ps_, :, js, :], ALU.mult, ALU.mult, rd=R, wr=R)
            for r in range(2):
                self.TT('dve', A8["b2"][:, r, :, :], A8["beta"][:, r, :, :], A4["rk2"][:], ALU.mult, rd=R, wr=R)
            self.TS('dve', A8["nb2"][:], A8["b2"][:], -1.0, None, ALU.mult, rd=R, wr=R)
            pg, PG = self.ps[0], self.buf("ps0")
            pl, PL = self.ps[1], self.buf("ps1")
            g2 = A8["g"]
            self.MM(pg[:, 0:132], self.triu[:, :], g2[:, 0, :, :].rearrange("p j h -> p (j h)"), rd=R + [self.IDB], wr=[PG])
            self.MM(pg[:, 132:264], self.tril[:, :], g2[:, 1, :, :].rearrange("p j h -> p (j h)"), rd=R + [self.IDB], wr=[PG])
            self.MM(pl[:, 0:264], self.ones32[:, :], g2[:].rearrange("p r j h -> p (r j h)"), rd=R + [self.IDB], wr=[PL])
            fl = lambda t: t[:].rearrange("p r j h -> p (r j h)")
            self.COPY('dve', fl(A8["gc"]), pg[:, 0:264], rd=[PG], wr=R)
            self.ACT(fl(A8["egc"]), fl(A8["gc"]), AF.Exp, rd=R, wr=R)
            self.ACT(fl(A8["cd"]), pl[:, 0:264], AF.Exp, rd=[PL], wr=R)
            self.TT('dve', fl(A8["ekt"]), pl[:, 0:264], fl(A8["gc"]), ALU.subtract, rd=[PL] + R, wr=R)
            self.ACT(fl(A8["ekt"]), fl(A8["ekt"]), AF.Exp, rd=R, wr=R)
            self.TT('dve', fl(A8["b2e"]), fl(A8["b2"]), fl(A8["ekt"]), ALU.mult, rd=R, wr=R)
            self.DMA('sp', self.sc_d[:, 800:800 + 132], A4["s2m"][:].rearrange("p j h -> p (j h)"), rd=R, wr=[self.buf("sc_s2m")])
            self.DMA('sp', self.sc_d[:, 1000:1000 + 132], A4["scq"][:].rearrange("p j h -> p (j h)"), rd=R, wr=[self.buf("sc_scq")])
            if self.debug:
                for i, n in enumerate(["beta", "g", "gc", "b2", "egc", "ekt", "cd"]):
                    self.DMA('sp', self.sc_d[:, 1200 + i * 264:1200 + (i + 1) * 264], fl(A8[n]), rd=R, wr=[self.buf("sc_dbg%d" % i)])
            import os
            CUT = int(os.environ.get("B2CUT", "9"))
            if CUT <= 1:
                mk.barrier(); return
            oacc, OB = self.sb(es, "d_oacc", [128, NCH, 4, 128])
            OBs = [Buf("oacc%d" % j) for j in range(NCH)]
            owritten = set()
            H4 = [128, 4, 128]
            bc = lambda ap4: ap4.unsqueeze(2).to_broadcast(H4)
            hb = lambda m: m[:, :].unsqueeze(1).to_broadcast(H4)
            DR = []
            for r in range(2):
                d = {}
                for n in ["tmp", "tmp2", "tS", "S32"]:
                    d[n] = self.sb(es, "d_%s_r%d" % (n, r), H4)
                for n in ["vA", "vS", "S16"]:
                    d[n] = self.sb(es, "d_%s_r%d" % (n, r), H4, BF16)
                d["tsets"] = []
                for c in range(2):
                    t_ = {}
                    for n in ["gM2", "Ei"]:
                        t_[n] = self.sb(es, "d_%s_r%d_c%d" % (n, r, c), H4)
                    t_["EsN"] = t_["gM2"]
                    for n in ["NT", "Nn", "Ma", "Mb", "MTa", "MTb", "Pa", "Pb", "Tw"]:
                        t_[n] = self.sb(es, "d_%s_r%d_c%d" % (n, r, c), H4, BF16)
                    d["tsets"].append(t_)
                for n in ["AT", "Tu", "nW"]:
                    d[n] = [self.sb(es, "d_%s_r%d_%d" % (n, r, i), H4, BF16) for i in range(3)]
                d["kq"] = [self.sb(es, "d_kq_r%d_%d" % (r, i), [128, 2, 4, 128], BF16) for i in range(3)]
                d["kv"] = [self.sb(es, "d_kv_r%d_%d" % (r, i), [128, 8, 128], BF16) for i in range(3)]
                self.MEMSET('pool', d["S32"][0][:], 0.0, wr=[d["S32"][1]])
                self.MEMSET('pool', d["S16"][0][:], 0.0, wr=[d["S16"][1]])
                DR.append(d)
            bank = lambda i: (self.ps[i], self.buf("ps%d" % i))
            b4 = lambda pt: pt[:, :].rearrange("p (h c) -> p h c", h=4)
            psb4 = self.psb[:, 0:512].rearrange("p (h c) -> p h c", h=4)
            PSBB = self.buf("psb")

            def pre(r, j, st, c):
                T = dict(DR[r])
                T.update(DR[r]["tsets"][c])
                M1 = self.m1f if r == 0 else self.m1b
                M2 = self.triu if r == 0 else self.tril
                NM = self.nm16f if r == 0 else self.nm16b
                mybank = bank(2 * r + c)

                def nxt():
                    return mybank
                kq, KQB = T["kq"][st]
                kv, KVB = T["kv"][st]
                s4 = lambda n: A8[n][:, r, j, :]
                self.DMA('sp', kq[:, :, :, :], self.qkT_d[:, j * 128:(j + 1) * 128].rearrange("(x h p) t -> p x h t", x=2, h=4),
                         rd=[self.buf("qkT%d_%d" % (j // 4, c_)) for c_ in range(8)] + [self.buf("qkT8")], wr=[KQB])
                self.DMA('sp', kv[:, :, :], self.kvtm_d[j, :, :, :], rd=[self.buf("kvtm%d" % (j // 4)), self.buf("kvtm8")], wr=[KVB])
                (gM2, GMB), (Ei, EIB), (EsN, ESB) = T["gM2"], T["Ei"], T["EsN"]
                (NT, NTB), (Nn, NNB), (Tw, TWB) = T["NT"], T["Nn"], T["Tw"]
                AT, ATB = T["AT"][st]
                Tu, TUB = T["Tu"][st]
                nW, NWB = T["nW"][st]
                self.TT(os.environ.get("B2_GM2", "pool"), gM2[:], hb(M2), bc(s4("g")), ALU.mult, rd=R + [self.IDB], wr=[GMB])
                pt, PB_ = nxt()
                self.MM(pt[:, :], M1[:, :], gM2[:].rearrange("p h c -> p (h c)"), start=True, stop=False, rd=[GMB, self.IDB], wr=[PB_])
                self.MM(pt[:, :], self.ident16[:, :], NM[:].rearrange("p h c -> p (h c)"), start=False, stop=True, rd=[self.IDB], wr=[PB_])
                self.ACT(Ei[:], b4(pt), AF.Exp, rd=[PB_], wr=[EIB])
                yield
                self.TT('pool', EsN[:], Ei[:], bc(s4("nb2")), ALU.mult, rd=[EIB] + R, wr=[ESB])
                self.mk.op('pool', lambda e: e.affine_select(out=EsN[:], in_=EsN[:], pattern=[[0, 4], [-1, 128]], compare_op=ALU.not_equal, fill=0.0, base=0, channel_multiplier=1), reads=[ESB], writes=[ESB])
                pt, PB_ = nxt()
                for h in range(4):
                    self.MM(pt[:, h * 128:(h + 1) * 128], kq[:, 1, h, :], kq[:, 0, h, :], rd=[KQB], wr=[PB_])
                self.TT('dve', AT[:], b4(pt), Ei[:], ALU.mult, rd=[PB_, EIB], wr=[ATB])
                yield
                pt, PB_ = nxt()
                for h in range(4):
                    self.MM(pt[:, h * 128:(h + 1) * 128], kq[:, 1, h, :], kq[:, 1, h, :], rd=[KQB], wr=[PB_])
                self.TT('dve', NT[:], b4(pt), EsN[:], ALU.mult, rd=[PB_, ESB], wr=[NTB])
                yield
                for h in range(4):
                    self.TR(self.psb[:, h * 128:(h + 1) * 128], NT[:, h, :], self.ident16[:, :], rd=[NTB, self.IDB], wr=[PSBB])
                self.COPY('act', Nn[:], psb4, rd=[PSBB], wr=[NNB])
                (Pa, PAB) = T["Pa"]
                self.TT('pool', Pa[:], NT[:], hb(self.ident16), ALU.add, rd=[NTB, self.IDB], wr=[PAB])
                yield
                M_prev, MB_prev = Nn, NNB
                MT_prev, MTB_prev = NT, NTB
                P_prev, PB_prev = Pa, PAB
                Ms = [T["Ma"], T["Mb"]]
                MTs = [T["MTa"], T["MTb"]]
                Ps = [T["Pb"], T["Pa"]]
                for k in range(1, 7):
                    Mk, MKB = Ms[k % 2]
                    pt, PB_ = nxt()
                    for h in range(4):
                        self.MM(pt[:, h * 128:(h + 1) * 128], MT_prev[:, h, :], M_prev[:, h, :], rd=[MB_prev, MTB_prev], wr=[PB_])
                    self.COPY('act', Mk[:], b4(pt), rd=[PB_], wr=[MKB])
                    yield
                    if k <= 5:
                        MTk, MTKB = MTs[k % 2]
                        pt, PB_ = nxt()
                        for h in range(4):
                            self.MM(pt[:, h * 128:(h + 1) * 128], M_prev[:, h, :], MT_prev[:, h, :], rd=[MB_prev, MTB_prev], wr=[PB_])
                        self.COPY('dve', MTk[:], b4(pt), rd=[PB_], wr=[MTKB])
                        yield
                    pt, PB_ = nxt()
                    self.MM(pt[:, :], self.ident16[:, :], P_prev[:].rearrange("p h c -> p (h c)"), start=True, stop=False, rd=[PB_prev, self.IDB], wr=[PB_])
                    for h in range(4):
                        self.MM(pt[:, h * 128:(h + 1) * 128], Mk[:, h, :], P_prev[:, h, :], start=False, stop=(h == 3), rd=[MKB, PB_prev], wr=[PB_])
                    if k <= 5:
                        Pk, PKB = Ps[(k - 1) % 2]
                        self.COPY('act', Pk[:], b4(pt), rd=[PB_], wr=[PKB])
                        P_prev, PB_prev = Pk, PKB
                        M_prev, MB_prev = Mk, MKB
                        MT_prev, MTB_prev = MTk, MTKB
                    else:
                        self.TT('dve', Tu[:], b4(pt), bc(A4["irk"][:, j, :]), ALU.mult, rd=[PB_] + R, wr=[TUB])
                        self.TT('dve', Tw[:], b4(pt), bc(s4("egc")), ALU.mult, rd=[PB_] + R, wr=[TWB])
                    yield
                pt, PB_ = nxt()
                for h in range(4):
                    self.MM(pt[:, h * 128:(h + 1) * 128], kv[:, h, :], Tw[:, h, :], rd=[KVB, TWB], wr=[PB_])
                self.ACT(nW[:], b4(pt), AF.Identity, scale=-1.0, rd=[PB_], wr=[NWB])
                yield

            def scan(r, j, st):
                T = DR[r]
                banks = [bank(4 + r), bank(6)]
                kq, KQB = T["kq"][st]
                kv, KVB = T["kv"][st]
                AT, ATB = T["AT"][st]
                Tu, TUB = T["Tu"][st]
                nW, NWB = T["nW"][st]
                s4 = lambda n: A8[n][:, r, j, :]
                (s32, S32B), (s16, S16B) = T["S32"], T["S16"]
                (vA, VAB), (vS, VSB) = T["vA"], T["vS"]
                (tmp, TMB), (tmp2, TM2B), (tS, TSB) = T["tmp"], T["tmp2"], T["tS"]
                self.TT('pool', tS[:], s32[:], bc(s4("cd")), ALU.mult, rd=[S32B] + R, wr=[TSB])
                pt, PB_ = banks[0]
                for h in range(4):
                    self.MM(pt[:, h * 128:(h + 1) * 128], Tu[:, h, :], kv[:, 4 + h, :], start=True, stop=False, rd=[TUB, KVB], wr=[PB_])
                    self.MM(pt[:, h * 128:(h + 1) * 128], nW[:, h, :], s16[:, h, :], start=False, stop=True, rd=[NWB, S16B], wr=[PB_])
                self.TT('dve', vS[:], b4(pt), bc(s4("b2e")), ALU.mult, rd=[PB_] + R, wr=[VSB])
                self.TT('dve', vA[:], b4(pt), bc(s4("b2")), ALU.mult, rd=[PB_] + R, wr=[VAB])
                yield
                pt, PB_ = banks[1]
                for h in range(4):
                    self.MM(pt[:, h * 128:(h + 1) * 128], kq[:, 0, h, :], s16[:, h, :], rd=[KQB, S16B], wr=[PB_])
                self.TT('dve', tmp[:], b4(pt), bc(s4("egc")), ALU.mult, rd=[PB_] + R, wr=[TMB])
                yield
                pt, PB_ = banks[0]
                for h in range(4):
                    self.MM(pt[:, h * 128:(h + 1) * 128], kv[:, h, :], vS[:, h, :], rd=[KVB, VSB], wr=[PB_])
                self.TT('dve', s32[:], b4(pt), tS[:], ALU.add, rd=[PB_, TSB], wr=[S32B])
                self.COPY('act', s16[:], s32[:], rd=[S32B], wr=[S16B])
                yield
                pt, PB_ = banks[1]
                for h in range(4):
                    self.MM(pt[:, h * 128:(h + 1) * 128], AT[:, h, :], vA[:, h, :], rd=[ATB, VAB], wr=[PB_])
                if j not in owritten:
                    owritten.add(j)
                    self.TT('dve', oacc[:, j, :, :], b4(pt), tmp[:], ALU.add, rd=[PB_, TMB], wr=[OBs[j]])
                else:
                    self.TT('pool', tmp2[:], tmp[:], oacc[:, j, :, :], ALU.add, rd=[TMB, OBs[j]], wr=[TM2B])
                    self.TT('dve', oacc[:, j, :, :], b4(pt), tmp2[:], ALU.add, rd=[PB_, TM2B], wr=[OBs[j]])
                yield

            def dir_gen(r):
                order = list(range(NCH)) if r == 0 else list(range(NCH - 1, -1, -1))
                N = len(order)
                pres = {}
                pre_done = set()
                next_pre = 0
                scans_done = 0
                scan_g = None
                next_scan = 0
                tick = 0
                for _ in range(6 * r):
                    yield
                while scans_done < N:
                    tick += 1
                    for c in range(2):
                        if c == 1 and tick < 12:
                            continue
                        if c not in pres and next_pre < N and next_pre < scans_done + 3:
                            pres[c] = (next_pre, pre(r, order[next_pre], next_pre % 3, c))
                            next_pre += 1
                    if scan_g is None and next_scan < N and next_scan in pre_done:
                        scan_g = scan(r, order[next_scan], next_scan % 3)
                    progressed = False
                    for c in list(pres):
                        idx, g = pres[c]
                        try:
                            next(g)
                            progressed = True
                            yield
                        except StopIteration:
                            pre_done.add(idx)
                            del pres[c]
                            progressed = True
                    if scan_g is not None:
                        try:
                            next(scan_g)
                            progressed = True
                            yield
                        except StopIteration:
                            scan_g = None
                            scans_done += 1
                            next_scan += 1
                            progressed = True
                    assert progressed

            gens = [dir_gen(0), dir_gen(1)]
            while gens:
                for g in list(gens):
                    try:
                        next(g)
                    except StopIteration:
                        gens.remove(g)
            self.DMA('sp', self.oacc_d[:, :], oacc[:].rearrange("p j h d -> p (j h d)"), rd=OBs, wr=[self.buf("oacc")])
        mk.barrier()

    def phaseB3(self, l):
        mk = self.mk
        with contextlib.ExitStack() as es:
            wz, WB = self.sb(es, "e_wz", [128, 8, 512], BF16)
            self.DMA('pool', wz[:], self.w_in[l, :, 2064:2576].rearrange("(k p) n -> p k n", p=128), wr=[WB])
            oacc, OB = self.sb(es, "e_oacc", [128, NCH, 4, 128])
            self.DMA('sp', oacc[:].rearrange("p j h d -> p (j h d)"), self.oacc_d[:, :], rd=[self.buf("oacc")], wr=[OB])
            s2m, SB_ = self.sb(es, "e_s2m", [128, NCH, 4])
            scq, _ = self.sb(es, "e_scq", [128, NCH, 4])
            self.DMA('sp', s2m[:].rearrange("p j h -> p (j h)"), self.sc_d[:, 800:932], rd=[self.buf("sc_s2m")], wr=[SB_])
            self.DMA('sp', scq[:].rearrange("p j h -> p (j h)"), self.sc_d[:, 1000:1132], rd=[self.buf("sc_scq")], wr=[SB_])
            ng, NGB = self.sb(es, "e_ng", [128, 1])
            self.DMA('sp', ng[:], self.dng[l, :, :], wr=[NGB])
            hts = [self.sb(es, "e_h%d" % i, [128, 8, 512], BF16) for i in range(2)]
            szs = [self.sb(es, "e_sz%d" % i, [128, 512]) for i in range(2)]
            dns = [self.sb(es, "e_dn%d" % i, [128, 4, 128]) for i in range(2)]
            junk, JB = self.sb(es, "e_junk", [128, 128])
            sms = [self.sb(es, "e_sm%d" % i, [128, 4, 4]) for i in range(2)]
            stg = [self.sb(es, "e_st%d" % i, [128, 4, 512], BF16) for i in range(2)]
            pz = [(self.ps[0], self.buf("ps0")), (self.ps[1], self.buf("ps1"))]
            ptt = [(self.ps[2], self.buf("ps2")), (self.ps[3], self.buf("ps3"))]
            T5 = tiles(L, 512)
            ssum, FB = self.sb(es, "e_ssum", [128, NCH, 4])
            fac, _ = self.sb(es, "e_fac", [128, NCH, 4])
            self.MEMSET('pool', ssum[:], 1.0, wr=[FB])
            for j, (t0_, m_) in enumerate(tiles(L, 128)):
                for h in range(4):
                    self.ACT(junk[:m_, :], oacc[:m_, j, h, :], AF.Square, accum_out=ssum[:m_, j, h:h + 1], rd=[OB], wr=[JB, FB])
            self.TT('dve', fac[:], ssum[:], s2m[:], ALU.mult, rd=[FB, SB_], wr=[FB])
            self.ACT(fac[:], fac[:], AF.Sqrt, bias=1e-6, rd=[FB], wr=[FB])
            self.mk.op('dve', lambda e: e.reciprocal(out=fac[:], in_=fac[:]), reads=[FB], writes=[FB])
            self.TT('dve', fac[:], fac[:], scq[:], ALU.mult, rd=[FB, SB_], wr=[FB])

            def b3_load(g):
                T0_, gn_ = T5[g]
                self.DMA('sp', hts[g % 2][0][:, :, :gn_], self.hT_d[:, T0_:T0_ + gn_].rearrange("(k p) t -> p k t", p=128), rd=[self.buf("hT%d" % g)], wr=[hts[g % 2][1]])
            pend = []
            b3_load(0)
            for gi, (T0, gn) in enumerate(T5):
                ht, HB = hts[gi % 2]
                stage, STB = stg[gi % 2]
                if gi + 1 < len(T5):
                    b3_load(gi + 1)
                for s, (s0, m) in enumerate(tiles(gn, 128)):
                    j = gi * 4 + s
                    pt, PBk = pz[j % 2]
                    sz, SZB = szs[j % 2]
                    dn, DNB = dns[j % 2]
                    sm, SMB = sms[j % 2]
                    for k in range(8):
                        self.MM(pt[:m, :], ht[:, k, s0:s0 + m], wz[:, k, :], start=(k == 0), stop=(k == 7), rd=[HB, WB], wr=[PBk])
                    self.ACT(sz[:m, :], pt[:m, :], AF.Silu, rd=[PBk], wr=[SZB])
                    while pend:
                        pend.pop(0)()

                    def part2(m=m, j=j, sz=sz, SZB=SZB, dn=dn, DNB=DNB, sm=sm, SMB=SMB, stage=stage, STB=STB, s0=s0, lastsub=(s0 + m == gn), T0=T0, gn=gn, gi=gi):
                        for h in range(4):
                            self.STT(dn[:m, h, :], oacc[:m, j, h, :], fac[:m, j, h:h + 1], sz[:m, h * 128:(h + 1) * 128], ALU.mult, ALU.mult, rd=[OB, FB, SZB], wr=[DNB])
                        tp, TPB = ptt[j % 2]
                        for h in range(4):
                            self.TR(tp[:, h * 128:h * 128 + m], dn[:m, h, :], self.ident[:m, :m], rd=[DNB, self.IDB], wr=[TPB])
                        self.ACT(stage[:, :, s0:s0 + m], tp[:, :].rearrange("p (h t) -> p h t", h=4)[:, :, :m], AF.Identity, scale=ng[:, 0:1], rd=[TPB, NGB], wr=[STB])
                        if lastsub:
                            self.DMA('pool', self.dnT_d[:, T0:T0 + gn].rearrange("(g p) t -> p g t", p=128), stage[:, :, :gn], rd=[STB], wr=[self.buf("dnT%d" % gi)])
                    pend.append(part2)
            while pend:
                pend.pop(0)()
        mk.barrier()

    def c1_weights(self, es, l):
        wg, WB = self.sb(es, "c_wg", [128, 8, 2048], BF16)
        wfp, WFB = self.sb(es, "c_wfp", [128, 4, D], BF16)
        wdp, WDPB = self.sb(es, "c_wdp", [128, 4, D], BF16)
        wo, WOB = self.sb(es, "c_wo", [128, 8, D], BF16)
        self.DMA('pool', wfp[:], self.w_fp[l, :, :].rearrange("(k p) n -> p k n", p=128), wr=[WFB])
        self.DMA('pool', wdp[:], self.w_dp[l, :, :].rearrange("(k p) n -> p k n", p=128), wr=[WDPB])
        WGB = [Buf("wg%d" % c) for c in range(4)]
        for c in (0, 2, 1, 3):
            self.DMA('pool', wg[:, :, c * 512:(c + 1) * 512], self.w_in[l, :, 2576 + c * 512:2576 + (c + 1) * 512].rearrange("(k p) n -> p k n", p=128), wr=[WGB[c]])
        self.DMA('pool', wo[:], self.w_out[l, :, :].rearrange("(k p) n -> p k n", p=128), wr=[WOB])
        return (wg, WB, wfp, WFB, wdp, WDPB, wo, WOB, WGB)

    def phaseC1(self, l, W):
        mk = self.mk
        (wg, WB, wfp, WFB, wdp, WDPB, wo, WOB, WGB) = W
        with contextlib.ExitStack() as es:
            gbt, GB = self.sb(es, "c_g", [128, D])
            bbt, _ = self.sb(es, "c_b", [128, D])
            self.DMA('sp', gbt[:], self.ln1[l, 0:1, :].to_broadcast([128, D]), wr=[GB])
            self.DMA('sp', bbt[:], self.ln1[l, 1:2, :].to_broadcast([128, D]), wr=[GB])
            hts = [self.sb(es, "c_h%d" % i, [128, 8, 512], BF16) for i in range(2)]
            yfs = [self.sb(es, "c_yf%d" % i, [128, 4, 512], BF16) for i in range(2)]
            dnt = [self.sb(es, "c_dn%d" % i, [128, 4, 512], BF16) for i in range(2)]
            GT, GTB = self.sb(es, "c_GT", [128, 8, 512], BF16)
            gas = [self.sb(es, "c_ga%d" % i, [128, 512]) for i in range(2)]
            gbs = [self.sb(es, "c_gb%d" % i, [128, 512]) for i in range(2)]
            t1s = [self.sb(es, "c_t1%d" % i, [128, 512]) for i in range(2)]
            t2s = [self.sb(es, "c_t2%d" % i, [128, 512]) for i in range(2)]
            hrs = [self.sb(es, "c_hr%d" % i, [128, D]) for i in range(2)]
            rs = [self.sb(es, "c_r%d" % i, [128, D]) for i in range(2)]
            stg = [self.sb(es, "c_st%d" % i, [128, 8, 512], BF16) for i in range(2)]
            scs = self.ln_scratch(es, "c1")
            pya, pyb, pga, pgb = [(self.ps[i], self.buf("ps%d" % i)) for i in range(4)]
            pm = [(self.ps[4], self.buf("ps4")), (self.ps[4], self.buf("ps4"))]
            ptb = [(self.ps[5], self.buf("ps5")), (self.ps[6], self.buf("ps6"))]
            deferred = []

            def flush():
                while deferred:
                    deferred.pop(0)()
            T5 = tiles(L, 512)
            T1 = tiles(L, 128)

            def c1_loads(g):
                T0_, gn_ = T5[g]
                ht_, HB_ = hts[g % 2]
                yf_, YB_ = yfs[g % 2]
                dn_, DB_ = dnt[g % 2]
                self.DMA('sp', ht_[:, :, :gn_], self.hT_d[:, T0_:T0_ + gn_].rearrange("(k p) t -> p k t", p=128), rd=[self.buf("hT%d" % g)], wr=[HB_])
                self.DMA('sp', yf_[:, :, :gn_], self.yfT_d[:, T0_:T0_ + gn_].rearrange("(g p) t -> p g t", p=128), rd=[self.buf("yfT%d" % g)], wr=[YB_])
                self.DMA('sp', dn_[:, :, :gn_], self.dnT_d[:, T0_:T0_ + gn_].rearrange("(g p) t -> p g t", p=128), rd=[self.buf("dnT%d" % g)], wr=[DB_])

            def c1_hr(jj):
                if jj < len(T1):
                    t0_, m_ = T1[jj]
                    self.DMA('sp', hrs[jj % 2][0][:m_, :], self.h_tm[t0_:t0_ + m_, :], rd=[self.buf("h_tm%d" % jj)], wr=[hrs[jj % 2][1]])
            c1_hr(0)
            for gi, (T0, gn) in enumerate(T5):
                ht, HB = hts[gi % 2]
                yf, YB = yfs[gi % 2]
                dn, DB = dnt[gi % 2]
                stage, STB = stg[gi % 2]
                if gi == 0:
                    c1_loads(0)
                if gi + 1 < len(T5):
                    c1_loads(gi + 1)
                for fc in range(8):
                    fs = slice(fc * 128, (fc + 1) * 128)
                    for k in range(8):
                        self.MM(pga[0][:, :gn], wg[:, k, fs], ht[:, k, :gn], start=(k == 0), stop=(k == 7), rd=[WGB[fc // 4], HB], wr=[pga[1]])
                    for k in range(8):
                        self.MM(pgb[0][:, :gn], wg[:, k, 1024 + fc * 128:1024 + (fc + 1) * 128], ht[:, k, :gn], start=(k == 0), stop=(k == 7), rd=[WGB[2 + fc // 4], HB], wr=[pgb[1]])
                    for c in range(4):
                        self.MM(pya[0][:, :gn], wfp[:, c, fs], yf[:, c, :gn], start=(c == 0), stop=(c == 3), rd=[WFB, YB], wr=[pya[1]])
                    for c in range(4):
                        self.MM(pyb[0][:, :gn], wdp[:, c, fs], dn[:, c, :gn], start=(c == 0), stop=(c == 3), rd=[WDPB, DB], wr=[pyb[1]])
                    ga, GAB = gas[fc % 2]
                    gb, GBB = gbs[fc % 2]
                    t1, T1B = t1s[fc % 2]
                    t2, T2B = t2s[fc % 2]
                    self.ACT(ga[:, :gn], pga[0][:, :gn], AF.Sigmoid, rd=[pga[1]], wr=[GAB])
                    self.ACT(gb[:, :gn], pgb[0][:, :gn], AF.Sigmoid, rd=[pgb[1]], wr=[GBB])
                    self.TT('dve', t1[:, :gn], pya[0][:, :gn], ga[:, :gn], ALU.mult, rd=[pya[1], GAB], wr=[T1B])
                    self.TT('dve', t2[:, :gn], pyb[0][:, :gn], gb[:, :gn], ALU.mult, rd=[pyb[1], GBB], wr=[T2B])
                    self.TT('pool', GT[:, fc, :gn], t1[:, :gn], t2[:, :gn], ALU.add, rd=[T1B, T2B], wr=[GTB])
                    if fc == 1:
                        flush()
                for s, (s0, m) in enumerate(tiles(gn, 128)):
                    j = gi * 4 + s
                    t0 = T0 + s0
                    hr, HRB = hrs[j % 2]
                    rr, RB = rs[j % 2]
                    c1_hr(j + 1)
                    for half in range(2):
                        pmt, PMB = pm[half]
                        for k in range(8):
                            self.MM(pmt[:m, :], GT[:, k, s0:s0 + m], wo[:, k, half * 512:(half + 1) * 512], start=(k == 0), stop=(k == 7), rd=[GTB, WOB], wr=[PMB])
                        self.STT(rr[:m, half * 512:(half + 1) * 512], hr[:m, half * 512:(half + 1) * 512], ALPHA, pmt[:m, :], ALU.mult, ALU.add, rd=[HRB, PMB], wr=[RB])
                    flush()
                    self.ln_tile(rr, RB, m, gbt, bbt, GB, scs[j % 2])
                    self.DMA('pool', self.h1_tm[t0:t0 + m, :], rr[:m, :], rd=[RB], wr=[self.buf("h1_tm%d" % j)])

                    def later(rr=rr, RB=RB, m=m, stage=stage, STB=STB, s0=s0, j=j, lastsub=(s0 + m == gn), T0=T0, gn=gn, gi=gi):
                        self.transpose_tile(rr, RB, m, stage, STB, s0, ptb, j)
                        if lastsub:
                            self.DMA('pool', self.h1T_d[:, T0:T0 + gn].rearrange("(k p) t -> p k t", p=128), stage[:, :, :gn], rd=[STB], wr=[self.buf("h1T%d" % gi)])
                    deferred.append(later)
            flush()
        mk.barrier()

    def phaseC2(self, l, last):
        mk = self.mk
        with contextlib.ExitStack() as es:
            wup, WB = self.sb(es, "f_wup", [128, 8, 4096], BF16)
            wdn, _ = self.sb(es, "f_wdn", [128, 32, D], BF16)
            WUB = [Buf("wup%d" % c) for c in range(8)]
            WDB = [Buf("wdn%d" % c) for c in range(4)]
            for c in range(8):
                self.DMA('pool', wup[:, :, c * 512:(c + 1) * 512], self.w_up[l, :, c * 512:(c + 1) * 512].rearrange("(k p) n -> p k n", p=128), wr=[WUB[c]])
            for c in range(4):
                self.DMA('pool', wdn[:, c * 8:(c + 1) * 8, :], self.w_dn[l, c * 1024:(c + 1) * 1024, :].rearrange("(k p) n -> p k n", p=128), wr=[WDB[c]])
            gbt, GB = self.sb(es, "f_g", [128, D])
            bbt, _ = self.sb(es, "f_b", [128, D])
            self.DMA('sp', gbt[:], self.ln2[l, 0:1, :].to_broadcast([128, D]), wr=[GB])
            self.DMA('sp', bbt[:], self.ln2[l, 1:2, :].to_broadcast([128, D]), wr=[GB])
            hts = [self.sb(es, "f_h%d" % i, [128, 8, 256], BF16) for i in range(2)]
            aT, ATB = self.sb(es, "f_aT", [128, 32, 256], BF16)
            a1s = [self.sb(es, "f_a1%d" % i, [128, 256], BF16) for i in range(3)]
            hrs = [self.sb(es, "f_hr%d" % i, [128, D]) for i in range(2)]
            rs = [self.sb(es, "f_r%d" % i, [128, D]) for i in range(2)]
            stg = [self.sb(es, "f_st%d" % i, [128, 8, 256], BF16) for i in range(2)]
            scs = self.ln_scratch(es, "c2")
            pu = [(self.ps[i], self.buf("ps%d" % i)) for i in range(3)]
            pm = [(self.ps[3], self.buf("ps3")), (self.ps[4], self.buf("ps4"))]
            ptb = [(self.ps[5], self.buf("ps5")), (self.ps[6], self.buf("ps6"))]
            ATBs = [Buf() for _ in range(32)]
            deferred = []

            def flush():
                while deferred:
                    deferred.pop(0)()
            T2 = tiles(L, 256)
            T1 = tiles(L, 128)

            def c2_loads(g):
                T0_, gn_ = T2[g]
                self.DMA('sp', hts[g % 2][0][:, :, :gn_], self.h1T_d[:, T0_:T0_ + gn_].rearrange("(k p) t -> p k t", p=128), rd=[self.buf("h1T%d" % (T0_ // 512))], wr=[hts[g % 2][1]])

            def c2_hr(jj):
                if jj < len(T1):
                    t0_, m_ = T1[jj]
                    self.DMA('sp', hrs[jj % 2][0][:m_, :], self.h1_tm[t0_:t0_ + m_, :], rd=[self.buf("h1_tm%d" % jj)], wr=[hrs[jj % 2][1]])
            c2_hr(0)
            for gi, (T0, gn) in enumerate(T2):
                ht, HB = hts[gi % 2]
                stage, STB = stg[gi % 2]
                if gi == 0:
                    c2_loads(0)
                if gi + 1 < len(T2):
                    c2_loads(gi + 1)
                for fc in range(32):
                    pt, PBk = pu[fc % 3]
                    a1, A1B = a1s[fc % 3]
                    for k in range(8):
                        self.MM(pt[:, :gn], wup[:, k, fc * 128:(fc + 1) * 128], ht[:, k, :gn], start=(k == 0), stop=(k == 7), rd=[WUB[fc // 4], HB], wr=[PBk])
                    self.ACT(a1[:, :gn], pt[:, :gn], AF.Relu, rd=[PBk], wr=[A1B])
                    self.TT('pool', aT[:, fc, :gn], a1[:, :gn], a1[:, :gn], ALU.mult, rd=[A1B, ATB], wr=[ATBs[fc]])
                    if fc == 3:
                        flush()
                for s, (s0, m) in enumerate(tiles(gn, 128)):
                    j = gi * 2 + s
                    t0 = T0 + s0
                    hr, HRB = hrs[j % 2]
                    rr, RB = rs[j % 2]
                    c2_hr(j + 1)
                    for half in range(2):
                        pmt, PMB = pm[half]
                        for kc in range(32):
                            self.MM(pmt[:m, :], aT[:, kc, s0:s0 + m], wdn[:, kc, half * 512:(half + 1) * 512], start=(kc == 0), stop=(kc == 31), rd=[ATBs[kc], WDB[kc // 8]], wr=[PMB])
                        self.STT(rr[:m, half * 512:(half + 1) * 512], hr[:m, half * 512:(half + 1) * 512], ALPHA, pmt[:m, :], ALU.mult, ALU.add, rd=[HRB, PMB], wr=[RB])
                    flush()
                    self.ln_tile(rr, RB, m, gbt, bbt, GB, scs[j % 2])
                    if last:
                        if j == 0:
                            self.final.append(self.DMA('pool', self.y[0:112, :], rr[16:128, :], rd=[RB], wr=[self.buf("y%d" % j)]))
                        else:
                            self.final.append(self.DMA('pool', self.y[t0 - 16:t0 - 16 + m, :], rr[:m, :], rd=[RB], wr=[self.buf("y%d" % j)]))
                    else:
                        self.DMA('pool', self.h_tm[t0:t0 + m, :], rr[:m, :], rd=[RB], wr=[self.buf("h_tm%d" % j)])

                        def later(rr=rr, RB=RB, m=m, stage=stage, STB=STB, s0=s0, j=j, lastsub=(s0 + m == gn), T0=T0, gn=gn):
                            self.transpose_tile(rr, RB, m, stage, STB, s0, ptb, j)
                            if lastsub:
                                self.DMA('pool', self.hT_d[:, T0:T0 + gn].rearrange("(k p) t -> p k t", p=128), stage[:, :, :gn], rd=[STB], wr=[self.buf("hT%d" % (T0 // 512))])
                        deferred.append(later)
            flush()
        mk.barrier()


_CONST = {}


def _consts():
    if _CONST:
        return _CONST
    HL = L // 2 + 1
    t = np.arange(HL, dtype=np.int64)
    m = (t[:, None] * t[None, :]) % L
    ang = (2.0 * np.pi / L) * m.astype(np.float64)
    dft = np.zeros((2, 2176, 2064), dtype=ml_dtypes.bfloat16)
    dft[0, :HL, :HL] = (np.cos(ang) / np.sqrt(L)).astype(np.float32).astype(ml_dtypes.bfloat16)
    dft[1, :HL, :HL] = (np.sin(ang) / np.sqrt(L)).astype(np.float32).astype(ml_dtypes.bfloat16)
    c = np.arange(128, dtype=np.int64)
    a2 = (2.0 * np.pi / 128) * ((c[:, None] * c[None, :]) % 128).astype(np.float64)
    cs = np.concatenate([np.cos(a2), -np.sin(a2)], axis=1) / np.sqrt(128.0)
    _CONST["dft"] = dft
    _CONST["cs128"] = cs.astype(np.float32).astype(ml_dtypes.bfloat16)
    return _CONST


def make_in_maps(inputs, cores):
    f = lambda a: np.ascontiguousarray(np.asarray(a, dtype=np.float32))
    c = _consts()
    conv = f(inputs["conv_w"])
    convw = np.ascontiguousarray(conv.reshape(2, 5, 12, 128).transpose(0, 3, 2, 1).reshape(2, 128, 60))
    al = f(inputs["a_log"]).reshape(2, 2, 1, 4)
    dtb = f(inputs["dt_bias"]).reshape(2, 2, 1, 4)
    adt = np.stack([np.broadcast_to(al, (2, 2, NCH, 4)).reshape(2, 264),
                    np.broadcast_to(dtb, (2, 2, NCH, 4)).reshape(2, 264)], axis=1)
    shared = {
        "meta": f(inputs["meta_tokens"]),
        "ln0": np.stack([f(inputs["ln0_g"]), f(inputs["ln0_b"])]),
        "w_in": f(inputs["w_in"]),
        "convw": convw,
        "adt": np.ascontiguousarray(adt),
        "dng": f(inputs["delta_norm_g"]).reshape(2, 128, 1),
        "w_fp": f(inputs["w_fourier_proj"]),
        "w_dp": f(inputs["w_delta_proj"]),
        "w_out": f(inputs["w_out"]),
        "ln1": np.stack([f(inputs["ln1_g"]), f(inputs["ln1_b"])], axis=1),
        "w_up": f(inputs["w_up"]),
        "w_dn": f(inputs["w_down"]),
        "ln2": np.stack([f(inputs["ln2_g"]), f(inputs["ln2_b"])], axis=1),
        "dft": c["dft"],
        "cs128": c["cs128"],
    }
    x = f(inputs["x"])
    return [dict(shared, x=x[b]) for b in cores]


_PROG = {}


def kernel(**inputs):
    if "p" not in _PROG:
        _PROG["p"] = Prog()
    nc = _PROG["p"].nc
    in_maps = make_in_maps(inputs, list(range(8)))
    res = run_bass_kernel_spmd(nc, in_maps, core_ids=list(range(8)))
    return np.stack([np.asarray(r["y"], dtype=np.float32) for r in res.results], axis=0)
```

```python
import contextlib
import numpy as np
import ml_dtypes
import concourse.bass as bass
import concourse.mybir as mybir
from concourse.bass_utils import run_bass_kernel_spmd

F32 = mybir.dt.float32
BF16 = mybir.dt.bfloat16
AF = mybir.ActivationFunctionType
ALU = mybir.AluOpType

L = 4112
D = 1024
NCH = 33
ALPHA = 4.0 ** 0.25
COMPUTE = ('pe', 'act', 'dve', 'pool')


def tiles(n, step):
    return [(t0, min(step, n - t0)) for t0 in range(0, n, step)]


class Buf:
    __slots__ = ('name', 'last_w', 'rd_c', 'rd_d', 'excl')

    def __init__(self, name='', excl=False):
        self.name = name
        self.excl = excl
        self.last_w = None
        self.rd_c = {}
        self.rd_d = []


class MK:
    def __init__(self, nc, n_dma_sems=32):
        self.nc = nc
        self.ops = {k: [] for k in ('pe', 'act', 'dve', 'pool', 'sp')}
        self.n_dma_sems = n_dma_sems
        self.dma_rr = {k: 0 for k in self.ops}
        self.dma_last = {}
        self.pending = {k: set() for k in self.ops}
        self.open_dma = []
        self.all_dma = []

    def barrier(self):
        evs = set()
        for e, lst in self.ops.items():
            if lst:
                evs.add((e, len(lst) - 1))
        for ev in self.open_dma:
            evs.add(ev)
        self.open_dma = []
        for e in self.ops:
            self.pending[e] |= evs

    def op(self, eng, fn, reads=(), writes=(), dma=False):
        lst = self.ops[eng]
        idx = len(lst)
        deps = set()
        if any(b.excl for b in reads):
            writes = list(writes) + [b for b in reads if b.excl]
            reads = [b for b in reads if not b.excl]

        def add(ev):
            if ev is None:
                return
            if ev[0] == 'pe' and eng == 'pe' and not dma:
                return
            deps.add(ev)
        for b in reads:
            add(b.last_w)
        for b in writes:
            add(b.last_w)
            for e, i in b.rd_c.items():
                add((e, i))
            for ev in b.rd_d:
                add(ev)
        if self.pending[eng]:
            for ev in self.pending[eng]:
                if ev[0] == eng and not self.ops[ev[0]][ev[1]]['dma']:
                    continue
                deps.add(ev)
            self.pending[eng] = set()
        slot = None
        if dma:
            slot = self.dma_rr[eng]
            self.dma_rr[eng] = (slot + 1) % self.n_dma_sems
            prev = self.dma_last.get((eng, slot))
            if prev is not None:
                deps.add(prev)
            self.dma_last[(eng, slot)] = (eng, idx)
            self.open_dma.append((eng, idx))
            self.all_dma.append((eng, idx))
        lst.append(dict(fn=fn, deps=deps, dma=dma, slot=slot, sig=False))
        me = (eng, idx)
        for b in reads:
            if dma:
                b.rd_d.append(me)
            else:
                b.rd_c[eng] = idx
        for b in writes:
            b.last_w = me
            b.rd_c = {}
            b.rd_d = []
        return me

    def finalize(self, final_waits=()):
        nc = self.nc
        ops = self.ops
        for eng, lst in ops.items():
            for o in lst:
                for (e, i) in o['deps']:
                    if not ops[e][i]['dma']:
                        ops[e][i]['sig'] = True
        st = contextlib.ExitStack()
        csem = {e: st.enter_context(nc.semaphore('s_' + e)) for e in COMPUTE}
        dsem = {}
        for eng in ops:
            if any(o['dma'] for o in ops[eng]):
                for s in range(self.n_dma_sems):
                    dsem[(eng, s)] = st.enter_context(nc.semaphore('d_%s_%d' % (eng, s)))
        ccount = {e: 0 for e in COMPUTE}
        dcount = {k: 0 for k in dsem}
        for eng, lst in ops.items():
            for o in lst:
                if o['dma']:
                    k = (eng, o['slot'])
                    dcount[k] += 16
                    o['ev'] = (('d',) + k, dcount[k])
                elif o['sig']:
                    ccount[eng] += 1
                    o['ev'] = (('c', eng), ccount[eng])
                else:
                    o['ev'] = None

        def semof(key):
            return csem[key[1]] if key[0] == 'c' else dsem[(key[1], key[2])]

        handles = {'pe': 'tensor', 'act': 'scalar', 'dve': 'vector', 'pool': 'gpsimd', 'sp': 'sync'}
        final_evs = [ops[e][i]['ev'] for (e, i) in final_waits]

        def emit_engine(eng, h):
            waited = {}
            for o in ops[eng]:
                need = {}
                for (e, i) in o['deps']:
                    key, val = ops[e][i]['ev']
                    if waited.get(key, 0) >= val:
                        continue
                    if need.get(key, 0) < val:
                        need[key] = val
                for key, val in need.items():
                    h.wait_ge(semof(key), val)
                    waited[key] = val
                ins = o['fn'](h)
                if o['dma']:
                    ins.then_inc(dsem[(eng, o['slot'])], 16)
                elif o['sig']:
                    ins.then_inc(csem[eng], 1)
            if eng == 'sp':
                for key, val in final_evs:
                    if waited.get(key, 0) < val:
                        h.wait_ge(semof(key), val)
                        waited[key] = val

        with nc.Block() as block:
            for eng in ('pe', 'act', 'dve', 'pool', 'sp'):
                getattr(block, handles[eng])(lambda h, eng=eng: emit_engine(eng, h))
        st.close()


class Prog:
    def __init__(self, debug=False, upto=99, nlayers=2):
        self.debug = debug
        self.upto = upto
        self.nlayers = nlayers
        nc = self.nc = bass.Bass("TRN2", target_bir_lowering=False)
        self.mk = MK(nc)
        self.final = []
        I = lambda name, shape, dt=F32: nc.dram_tensor(name, list(shape), dt, kind="ExternalInput").ap()
        self.x = I("x", [4096, D])
        self.meta = I("meta", [16, D])
        self.ln0 = I("ln0", [2, D])
        self.w_in = I("w_in", [2, D, 4624])
        self.convw = I("convw", [2, 128, 60])
        self.adt = I("adt", [2, 2, 264])
        self.dng = I("dng", [2, 128, 1])
        self.w_fp = I("w_fp", [2, 512, D])
        self.w_dp = I("w_dp", [2, 512, D])
        self.w_out = I("w_out", [2, D, D])
        self.ln1 = I("ln1", [2, 2, D])
        self.w_up = I("w_up", [2, D, 4096])
        self.w_dn = I("w_dn", [2, 4096, D])
        self.ln2 = I("ln2", [2, 2, D])
        self.dft = I("dft", [2, 2176, 2064], BF16)
        self.cs128 = I("cs128", [128, 256], BF16)
        self.y = nc.dram_tensor("y", [4096, D], F32, kind="ExternalOutput").ap()
        skind = "ExternalOutput" if debug else "Internal"
        S = lambda name, shape, dt: nc.dram_tensor(name, list(shape), dt, kind=skind).ap()
        self.h_tm = S("h_tm", [L, D], F32)
        self.hT_d = S("hT_d", [D, L], BF16)
        self.yfT_d = S("yfT_d", [512, L], BF16)
        self.qkT_d = S("qkT_d", [D, NCH * 128], BF16)
        self.kvtm_d = S("kvtm_d", [NCH, 128, 8, 128], BF16)
        self.dnT_d = S("dnT_d", [512, L], BF16)
        self.h1_tm = S("h1_tm", [L, D], F32)
        self.h1T_d = S("h1T_d", [D, L], BF16)
        self.oacc_d = S("oacc_d", [128, NCH * 512], F32)
        self.sc_d = S("sc_d", [128, 12 * NCH * 8], F32)
        self.B = {}
        self.ps = [nc.alloc_psum_tensor("ps%d" % i, [128, 512], F32) for i in range(7)]
        self.psb = nc.alloc_psum_tensor("psb", [128, 1024], BF16)
        self.PB = {}
        self.build()

    def buf(self, name):
        b = self.B.get(name)
        if b is None:
            b = self.B[name] = Buf(name, excl=name.startswith("ps"))
        return b

    def MM(self, out, lhsT, rhs, start=True, stop=True, rd=(), wr=()):
        self.mk.op('pe', lambda e: e.matmul(out, lhsT=lhsT, rhs=rhs, start=start, stop=stop), reads=rd, writes=wr)

    def TR(self, out, in_, ident, rd=(), wr=()):
        self.mk.op('pe', lambda e: e.transpose(out=out, in_=in_, identity=ident), reads=rd, writes=wr)

    def ACT(self, out, in_, func, bias=0.0, scale=1.0, accum_out=None, rd=(), wr=()):
        if accum_out is None:
            self.mk.op('act', lambda e: e.activation(out=out, in_=in_, func=func, bias=bias, scale=scale), reads=rd, writes=wr)
        else:
            self.mk.op('act', lambda e: e.activation(out=out, in_=in_, func=func, bias=bias, scale=scale, accum_out=accum_out), reads=rd, writes=wr)

    def COPY(self, eng, out, in_, rd=(), wr=()):
        if eng == 'act':
            self.mk.op('act', lambda e: e.copy(out=out, in_=in_), reads=rd, writes=wr)
        else:
            self.mk.op(eng, lambda e: e.tensor_copy(out=out, in_=in_), reads=rd, writes=wr)

    def TS(self, eng, out, in0, s1, s2, op0, op1=None, rd=(), wr=()):
        if op1 is None:
            self.mk.op(eng, lambda e: e.tensor_scalar(out=out, in0=in0, scalar1=s1, scalar2=None, op0=op0), reads=rd, writes=wr)
        else:
            self.mk.op(eng, lambda e: e.tensor_scalar(out=out, in0=in0, scalar1=s1, scalar2=s2, op0=op0, op1=op1), reads=rd, writes=wr)

    def TT(self, eng, out, in0, in1, op, rd=(), wr=()):
        self.mk.op(eng, lambda e: e.tensor_tensor(out=out, in0=in0, in1=in1, op=op), reads=rd, writes=wr)

    def STT(self, out, in0, scalar, in1, op0, op1, rd=(), wr=()):
        self.mk.op('dve', lambda e: e.scalar_tensor_tensor(out=out, in0=in0, scalar=scalar, in1=in1, op0=op0, op1=op1), reads=rd, writes=wr)

    def MEMSET(self, eng, ap, val, wr=()):
        self.mk.op(eng, lambda e: e.memset(ap, val), writes=wr)

    def DMA(self, q, out, in_, rd=(), wr=()):
        return self.mk.op(q, lambda e: e.dma_start(out=out, in_=in_), reads=rd, writes=wr, dma=True)

    def sb(self, es, name, shape, dt=F32):
        self._uid = getattr(self, "_uid", 0) + 1
        t = es.enter_context(self.nc.sbuf_tensor("%s_u%d" % (name, self._uid), list(shape), dt))
        return t, Buf(name)

    def ln_tile(self, xt, XB, n, gbt, bbt, GB, sc, eps=1e-5):
        stats, mv, rstd, SB_ = sc
        for c in range(2):
            self.mk.op('dve', lambda e, c=c: e.bn_stats(out=stats[:n, c, :], in_=xt[:n, c * 512:(c + 1) * 512]), reads=[XB], writes=[SB_])
        self.mk.op('dve', lambda e: e.bn_aggr(out=mv[:n, :], in_=stats[:n, :, :]), reads=[SB_], writes=[SB_])
        self.ACT(rstd[:n, :], mv[:n, 1:2], AF.Sqrt, bias=eps, rd=[SB_], wr=[SB_])
        self.mk.op('dve', lambda e: e.reciprocal(out=rstd[:n, :], in_=rstd[:n, :]), reads=[SB_], writes=[SB_])
        self.TS('dve', xt[:n, :], xt[:n, :], mv[:n, 0:1], rstd[:n, 0:1], ALU.subtract, ALU.mult, rd=[XB, SB_], wr=[XB])
        self.TT('pool', xt[:n, :], xt[:n, :], gbt[:n, :], ALU.mult, rd=[XB, GB], wr=[XB])
        self.TT('pool', xt[:n, :], xt[:n, :], bbt[:n, :], ALU.add, rd=[XB, GB], wr=[XB])

    def ln_scratch(self, es, tag):
        out = []
        for i in range(2):
            stats, b = self.sb(es, "%s_st%d" % (tag, i), [128, 2, 6])
            mv, _ = self.sb(es, "%s_mv%d" % (tag, i), [128, 2])
            rstd, _ = self.sb(es, "%s_rs%d" % (tag, i), [128, 1])
            out.append((stats, mv, rstd, b))
        return out

    def transpose_tile(self, xt, XB, n, stage, STB, off, pbanks, cnt):
        for half in range(2):
            pt, PBk = pbanks[half]
            for k in range(4):
                kk = half * 4 + k
                self.TR(pt[:, k * 128:k * 128 + n], xt[:n, kk * 128:(kk + 1) * 128], self.ident[:n, :n], rd=[XB, self.IDB], wr=[PBk])
            src = pt[:, :].rearrange("p (k t) -> p k t", k=4)[:, :, :n]
            dst = stage[:, half * 4:(half + 1) * 4, off:off + n]
            self.COPY('act' if (cnt + half) % 2 == 0 else 'dve', dst, src, rd=[PBk], wr=[STB])

    def build(self):
        nc, mk = self.nc, self.mk
        with contextlib.ExitStack() as es0:
            self.ident, self.IDB = self.sb(es0, "ident", [128, 128])
            self.ident16, _ = self.sb(es0, "ident16", [128, 128], BF16)
            self.ones32, _ = self.sb(es0, "ones32", [128, 128])
            self.ones16, _ = self.sb(es0, "ones16", [128, 128], BF16)
            self.triu, _ = self.sb(es0, "triu", [128, 128])
            self.tril, _ = self.sb(es0, "tril", [128, 128])
            self.m1f, _ = self.sb(es0, "m1f", [128, 128])
            self.m1b, _ = self.sb(es0, "m1b", [128, 128])
            self.nm16f, _ = self.sb(es0, "nm16f", [128, 4, 128], BF16)
            self.nm16b, _ = self.sb(es0, "nm16b", [128, 4, 128], BF16)
            CB = self.IDB
            sel = lambda t, pat, cm, op, fill, base=0: mk.op('pool', lambda e: e.affine_select(out=t[:], in_=t[:], pattern=pat, compare_op=op, fill=fill, base=base, channel_multiplier=cm), reads=[CB], writes=[CB])
            self.MEMSET('pool', self.ident[:], 0.0, wr=[CB])
            sel(self.ident, [[-1, 128]], 1, ALU.not_equal, 1.0)
            self.COPY('pool', self.ident16[:], self.ident[:], rd=[CB], wr=[CB])
            self.MEMSET('pool', self.ones32[:], 1.0, wr=[CB])
            self.MEMSET('pool', self.ones16[:], 1.0, wr=[CB])
            self.MEMSET('pool', self.triu[:], 1.0, wr=[CB])
            sel(self.triu, [[1, 128]], -1, ALU.is_ge, 0.0)
            self.MEMSET('pool', self.tril[:], 1.0, wr=[CB])
            sel(self.tril, [[-1, 128]], 1, ALU.is_ge, 0.0)
            self.MEMSET('pool', self.m1f[:], 1.0, wr=[CB])
            sel(self.m1f, [[-1, 128]], 1, ALU.is_gt, 0.0)
            self.MEMSET('pool', self.m1b[:], 1.0, wr=[CB])
            sel(self.m1b, [[1, 128]], -1, ALU.is_gt, 0.0)
            self.MEMSET('pool', self.nm16f[:], 0.0, wr=[CB])
            sel(self.nm16f, [[0, 4], [1, 128]], -1, ALU.is_ge, -1e5)
            self.MEMSET('pool', self.nm16b[:], 0.0, wr=[CB])
            sel(self.nm16b, [[0, 4], [-1, 128]], 1, ALU.is_ge, -1e5)
            self.phase0()
            for l in range(self.nlayers):
                last = (l == 1)
                if self.upto >= 1:
                    self.phaseA(l)
                if self.upto >= 2:
                    self.phaseB1(l)
                if self.upto >= 3:
                    self.phaseB2(l)
                with contextlib.ExitStack() as esw:
                    if self.upto >= 5:
                        W_ = self.c1_weights(esw, l)
                    if self.upto >= 4:
                        self.phaseB3(l)
                    if self.upto >= 5:
                        self.phaseC1(l, W_)
                if self.upto >= 6:
                    self.phaseC2(l, last)
            mk.barrier()
            mk.finalize(final_waits=list(mk.all_dma))

    def phase0(self):
        mk = self.mk
        with contextlib.ExitStack() as es:
            gbt, GB = self.sb(es, "p0_g", [128, D])
            bbt, _ = self.sb(es, "p0_b", [128, D])
            self.DMA('sp', gbt[:], self.ln0[0:1, :].to_broadcast([128, D]), wr=[GB])
            self.DMA('sp', bbt[:], self.ln0[1:2, :].to_broadcast([128, D]), wr=[GB])
            xts = [self.sb(es, "p0_x%d" % i, [128, D]) for i in range(6)]
            stg = [self.sb(es, "p0_s%d" % i, [128, 8, 512], BF16) for i in range(2)]
            scs = self.ln_scratch(es, "p0")
            pb = [(self.ps[0], self.buf("ps0")), (self.ps[1], self.buf("ps1"))]
            pb2 = [(self.ps[2], self.buf("ps2")), (self.ps[3], self.buf("ps3"))]
            pend = []
            for gi, (T0, gn) in enumerate(tiles(L, 512)):
                stage, STB = stg[gi % 2]
                for s, (s0, m) in enumerate(tiles(gn, 128)):
                    j = gi * 4 + s
                    t0 = T0 + s0
                    xt, XB = xts[j % 6]
                    if j == 0:
                        self.DMA('sp', xt[0:16, :], self.meta[:, :], wr=[XB])
                        self.DMA('sp', xt[16:128, :], self.x[0:112, :], wr=[XB])
                    else:
                        self.DMA('sp', xt[:m, :], self.x[t0 - 16:t0 - 16 + m, :], wr=[XB])
                    self.ln_tile(xt, XB, m, gbt, bbt, GB, scs[j % 2])
                    while pend:
                        pend.pop(0)()

                    def part2(xt=xt, XB=XB, m=m, t0=t0, j=j, stage=stage, STB=STB, s0=s0, lastsub=(s0 + m == gn), T0=T0, gn=gn, gi=gi):
                        self.DMA('pool', self.h_tm[t0:t0 + m, :], xt[:m, :], rd=[XB], wr=[self.buf("h_tm%d" % j)])
                        self.transpose_tile(xt, XB, m, stage, STB, s0, pb if j % 2 == 0 else pb2, j)
                        if lastsub:
                            self.DMA('pool', self.hT_d[:, T0:T0 + gn].rearrange("(k p) t -> p k t", p=128), stage[:, :, :gn], rd=[STB], wr=[self.buf("hT%d" % gi)])
                    pend.append(part2)
            while pend:
                pend.pop(0)()
        mk.barrier()

    def phaseA(self, l):
        mk = self.mk
        HL = L // 2 + 1
        NJ = 17
        with contextlib.ExitStack() as es:
            cs, CSB = self.sb(es, "a_cs", [128, 256], BF16)
            self.DMA('sp', cs[:], self.cs128[:, :], wr=[CSB])
            A_sb, AB = self.sb(es, "a_A", [128, NJ, 1024], BF16)
            Ue, UEB = self.sb(es, "a_Ue", [128, 4, NJ * 128], BF16)
            Uo, UOB = self.sb(es, "a_Uo", [128, 4, NJ * 128], BF16)
            ys, YSB = self.sb(es, "a_ys", [128, 4, L], BF16)
            ABs = [Buf("A%d" % j) for j in range(NJ)]
            with contextlib.ExitStack() as es1:
                wuf, WB = self.sb(es1, "a_wuf", [128, 8, 512], BF16)
                self.DMA('pool', wuf[:], self.w_in[l, :, 0:512].rearrange("(k p) n -> p k n", p=128), wr=[WB])
                u32, U32B = self.sb(es1, "a_u32", [128, 4, L])
                hts = [self.sb(es1, "a_h%d" % i, [128, 8, 512], BF16) for i in range(2)]
                pu = [(self.ps[i], self.buf("ps%d" % i)) for i in range(4)]
                T5 = tiles(L, 512)

                def a_load(g):
                    T0_, gn_ = T5[g]
                    self.DMA('sp', hts[g % 2][0][:, :, :gn_], self.hT_d[:, T0_:T0_ + gn_].rearrange("(k p) t -> p k t", p=128), rd=[self.buf("hT%d" % g)], wr=[hts[g % 2][1]])
                a_load(0)
                UBs = [Buf("u%d" % g) for g in range(len(T5))]
                for gi, (T0, gn) in enumerate(T5):
                    ht, HB = hts[gi % 2]
                    if gi + 1 < len(T5):
                        a_load(gi + 1)
                    for g in range(4):
                        pt, PBk = pu[g]
                        for k in range(8):
                            self.MM(pt[:, :gn], wuf[:, k, g * 128:(g + 1) * 128], ht[:, k, :gn], start=(k == 0), stop=(k == 7), rd=[WB, HB], wr=[PBk])
                        self.COPY('act' if g % 2 == 0 else 'dve', u32[:, g, T0:T0 + gn], pt[:, :gn], rd=[PBk], wr=[UBs[gi]])
                self.MEMSET('pool', Ue[:, :, HL:NJ * 128], 0.0, wr=[UEB])
                self.MEMSET('pool', Uo[:, :, HL:NJ * 128], 0.0, wr=[UOB])
                self.MEMSET('pool', Uo[:, :, 0:1], 0.0, wr=[UOB])
                self.MEMSET('pool', Uo[:, :, HL - 1:HL], 0.0, wr=[UOB])
                self.COPY('pool', Ue[:, :, 0:1], u32[:, :, 0:1], rd=UBs, wr=[UEB])
                self.COPY('pool', Ue[:, :, HL - 1:HL], u32[:, :, HL - 1:HL], rd=UBs, wr=[UEB])
                for g in range(4):
                    self.TT('dve', Ue[:, g, 1:HL - 1], u32[:, g, 1:HL - 1], u32[:, g, L - 1:HL - 1:-1], ALU.add, rd=UBs, wr=[UEB])
                    self.TT('dve', Uo[:, g, 1:HL - 1], u32[:, g, 1:HL - 1], u32[:, g, L - 1:HL - 1:-1], ALU.subtract, rd=UBs, wr=[UOB])
                pa = [(self.ps[4], self.buf("ps4")), (self.ps[5], self.buf("ps5"))]
                cnt = 0
                for j, (c0, m) in enumerate(tiles(NJ * 128, 128)):
                    for half in range(2):
                        pt, PBk = pa[half]
                        for gg in range(2):
                            g = half * 2 + gg
                            self.MM(pt[:m, gg * 256:gg * 256 + 128], Ue[:, g, c0:c0 + m], cs[:, 0:128], rd=[UEB, CSB], wr=[PBk])
                            self.MM(pt[:m, gg * 256 + 128:gg * 256 + 256], Uo[:, g, c0:c0 + m], cs[:, 128:256], rd=[UOB, CSB], wr=[PBk])
                        self.COPY('act' if cnt % 2 == 0 else 'dve', A_sb[:m, j, half * 512:(half + 1) * 512], pt[:m, :], rd=[PBk], wr=[ABs[j]])
                        cnt += 1
            mk.barrier()
            with contextlib.ExitStack() as es2:
                cts = [self.sb(es2, "a_ct%d" % i, [128, NJ, 512], BF16) for i in range(2)]
                sts = [self.sb(es2, "a_st%d" % i, [128, NJ, 512], BF16) for i in range(2)]
                pcs = [self.sb(es2, "a_pc%d" % i, [128, 512]) for i in range(2)]
                pb4 = [(self.ps[i], self.buf("ps%d" % i)) for i in range(4)]
                TH = tiles(HL, 512)

                def d_load(ti):
                    T0_, n_ = TH[ti]
                    n_ = max(n_, 16)
                    self.DMA('sp', cts[ti % 2][0][:, :, :n_], self.dft[0, :, T0_:T0_ + n_].rearrange("(j p) t -> p j t", p=128), wr=[cts[ti % 2][1]])
                    self.DMA('pool', sts[ti % 2][0][:, :, :n_], self.dft[1, :, T0_:T0_ + n_].rearrange("(j p) t -> p j t", p=128), wr=[sts[ti % 2][1]])
                d_load(0)
                cnt = 0
                for ti, (T0, n) in enumerate(TH):
                    ct, CTB = cts[ti % 2]
                    st, STB_ = sts[ti % 2]
                    nc_ = max(n, 16)
                    if ti + 1 < len(TH):
                        d_load(ti + 1)
                    for g in range(4):
                        pP, PPB = pb4[(2 * g) % 4]
                        pQ, PQB = pb4[(2 * g + 1) % 4]
                        for j in range(NJ):
                            kp = 128
                            self.MM(pP[:, :nc_], A_sb[:kp, j, g * 256:g * 256 + 128], ct[:kp, j, :nc_], start=(j == 0), stop=(j == NJ - 1), rd=[ABs[j], CTB], wr=[PPB])
                        for j in range(NJ):
                            kp = 128
                            self.MM(pQ[:, :nc_], A_sb[:kp, j, g * 256 + 128:g * 256 + 256], st[:kp, j, :nc_], start=(j == 0), stop=(j == NJ - 1), rd=[ABs[j], STB_], wr=[PQB])
                        pc, PCB = pcs[cnt % 2]
                        cnt += 1
                        self.COPY('act', pc[:, :n], pP[:, :n], rd=[PPB], wr=[PCB])
                        self.TT('dve', ys[:, g, T0:T0 + n], pQ[:, :n], pc[:, :n], ALU.add, rd=[PQB, PCB], wr=[YSB])
                        a_ = max(T0, 1)
                        b_ = min(T0 + n, HL - 1)
                        if b_ > a_:
                            self.TT('dve', ys[:, g, L - a_:L - b_:-1], pc[:, a_ - T0:b_ - T0], pQ[:, a_ - T0:b_ - T0], ALU.subtract, rd=[PQB, PCB], wr=[YSB])
                for gi, (T0, gn) in enumerate(tiles(L, 512)):
                    self.DMA('pool' if gi % 2 else 'sp', self.yfT_d[:, T0:T0 + gn].rearrange("(g p) t -> p g t", p=128), ys[:, :, T0:T0 + gn], rd=[YSB], wr=[self.buf("yfT%d" % gi)])
        mk.barrier()

    def phaseB1(self, l):
        mk = self.mk
        with contextlib.ExitStack() as es:
            wq, WB = self.sb(es, "b_wq", [128, 8, 1536], BF16)
            WQB = [Buf("wq%d" % c) for c in range(3)]
            for c in range(3):
                self.DMA('pool', wq[:, :, c * 512:(c + 1) * 512], self.w_in[l, :, 512 + c * 512:1024 + c * 512].rearrange("(k p) n -> p k n", p=128), wr=[WQB[c]])
            wbd, _ = self.sb(es, "b_wbd", [128, 8, 16], BF16)
            self.DMA('pool', wbd[:], self.w_in[l, :, 2048:2064].rearrange("(k p) n -> p k n", p=128), wr=[WB])
            cw, CWB = self.sb(es, "b_cw", [128, 60])
            self.DMA('sp', cw[:], self.convw[l, :, :], wr=[CWB])
            dg, DGB = self.sb(es, "b_dg", [128, 60, 128], BF16)
            for i in range(60):
                self.TS('pool', dg[:, i, :], self.ident[:, :], cw[:, i:i + 1], None, ALU.mult, rd=[CWB, self.IDB], wr=[DGB])
            xpre, XPB = self.sb(es, "b_xp", [128, 12, L + 4], BF16)
            XPBs = [[Buf("xp%d_%d" % (cc, gi)) for gi in range(9)] for cc in range(12)]
            self.MEMSET('pool', xpre[:, :, 0:2], 0.0, wr=[XPB])
            self.MEMSET('pool', xpre[:, :, L + 2:L + 4], 0.0, wr=[XPB])
            bg, BGB = self.sb(es, "b_bg", [128, 4, NCH, 4])
            self.MEMSET('pool', bg[:], 0.0, wr=[BGB])
            ss, SSB = self.sb(es, "b_ss", [128, NCH, 8])
            hts = [self.sb(es, "b_h%d" % i, [128, 8, 512], BF16) for i in range(2)]
            pp = [(self.ps[i], self.buf("ps%d" % i)) for i in range(4)]
            pbgt, PBG = self.ps[4], self.buf("ps4")
            pss, PSS = self.ps[5], self.buf("ps5")
            ptr = [(self.ps[6], self.buf("ps6"))]
            cnt = 0
            for gi, (T0, gn) in enumerate(tiles(L, 512)):
                ht, HB = hts[gi % 2]
                self.DMA('sp', ht[:, :, :gn], self.hT_d[:, T0:T0 + gn].rearrange("(k p) t -> p k t", p=128), rd=[self.buf("hT%d" % gi)], wr=[HB])
                for cc in range(12):
                    pt, PBk = pp[cnt % 4]
                    for k in range(8):
                        self.MM(pt[:, :gn], wq[:, k, cc * 128:(cc + 1) * 128], ht[:, k, :gn], start=(k == 0), stop=(k == 7), rd=[WQB[cc // 4], HB], wr=[PBk])
                    self.COPY('act' if cnt % 2 == 0 else 'dve', xpre[:, cc, 2 + T0:2 + T0 + gn], pt[:, :gn], rd=[PBk, XPB], wr=[XPBs[cc][gi]])
                    cnt += 1
                for s, (s0, m) in enumerate(tiles(gn, 128)):
                    j = gi * 4 + s
                    for k in range(8):
                        self.MM(pbgt[:m, 0:16], ht[:, k, s0:s0 + m], wbd[:, k, :], start=(k == 0), stop=(k == 7), rd=[WB, HB], wr=[PBG])
                    self.COPY('dve', bg[:m, :, j, :], pbgt[:m, 0:16].rearrange("p (a h) -> p a h", a=4), rd=[PBG], wr=[BGB])
            self.bg, self.BGB = bg, BGB
            import os
            CUT = int(os.environ.get("B1CUT", "9"))
            if CUT <= 1:
                mk.barrier(); return
            zt, ZB = self.sb(es, "b_z", [128, 1024], BF16)
            self.MEMSET('pool', zt[:], 0.0, wr=[ZB])
            self.DMA('sp', self.qkT_d[:, L:NCH * 128].rearrange("(k p) t -> p k t", p=128), zt[:, 0:8 * 112].rearrange("p (k t) -> p k t", k=8), rd=[ZB], wr=[self.buf("qkT8")])
            self.DMA('sp', self.kvtm_d[32, :, :, :], zt[:, :].rearrange("p (c d) -> p c d", c=8), rd=[ZB], wr=[self.buf("kvtm8")])
            if CUT <= 2:
                mk.barrier(); return
            sacts = [self.sb(es, "b_sa%d" % i, [128, 512]) for i in range(3)]
            q16s = [self.sb(es, "b_q16%d" % i, [128, 512], BF16) for i in range(3)]
            sqs = [self.sb(es, "b_sq%d" % i, [128, 512], BF16) for i in range(2)]
            kvs = [self.sb(es, "b_kv%d" % i, [128, 4, 8, 128], BF16) for i in range(2)]
            cnt = 0
            pend = []
            for gi, (T0, gn) in enumerate(tiles(L, 512)):
                kv, KVB = kvs[gi % 2]
                subt = tiles(gn, 128)
                for cc in range(12):
                    pt, PBk = pp[cnt % 4]
                    sa, SAB = sacts[cnt % 3]
                    rdx = [DGB] + [XPBs[cc][g2] for g2 in (gi - 1, gi, gi + 1) if 0 <= g2 < 9] + [XPB]
                    for tau in range(5):
                        self.MM(pt[:, :gn], dg[:, cc * 5 + tau, :], xpre[:, cc, T0 + tau:T0 + tau + gn], start=(tau == 0), stop=(tau == 4), rd=rdx, wr=[PBk])
                    self.ACT(sa[:, :gn], pt[:, :gn], AF.Silu, rd=[PBk], wr=[SAB])
                    sq, SQB = sqs[cnt % 2]
                    if cc < 8:
                        q16, QB = q16s[cnt % 3]
                        self.COPY('dve', q16[:, :gn], sa[:, :gn], rd=[SAB], wr=[QB])
                        self.DMA('sp', self.qkT_d[cc * 128:(cc + 1) * 128, T0:T0 + gn], q16[:, :gn], rd=[QB], wr=[self.buf("qkT%d_%d" % (gi, cc))])
                        self.TT('pool', sq[:, :gn], sa[:, :gn], sa[:, :gn], ALU.mult, rd=[SAB], wr=[SQB])
                    while pend:
                        pend.pop(0)()

                    def part2(cc=cc, gi=gi, T0=T0, gn=gn, sa=sa, SAB=SAB, sq=sq, SQB=SQB, kv=kv, KVB=KVB, subt=subt):
                        if cc < 8:
                            for s, (s0, m) in enumerate(subt):
                                j = gi * 4 + s
                                self.MM(pss[:m, j * 8 + cc:j * 8 + cc + 1], sq[:, s0:s0 + m], self.ones16[:, 0:1], rd=[SQB, self.IDB], wr=[PSS])
                        if cc >= 4:
                            tp, TPB = ptr[0]
                            for s, (s0, m) in enumerate(subt):
                                self.TR(tp[:, s * 128:(s + 1) * 128], sa[:, s0:s0 + 128], self.ident[:, :], rd=[SAB, self.IDB], wr=[TPB])
                            ns = len(subt)
                            mlast = subt[-1][1]
                            if mlast == 128:
                                self.COPY('dve', kv[:, 0:ns, cc - 4, :], tp[:, 0:ns * 128].rearrange("p (s d) -> p s d", s=ns), rd=[TPB], wr=[KVB])
                            else:
                                self.COPY('dve', kv[:mlast, 0, cc - 4, :], tp[:mlast, 0:128], rd=[TPB], wr=[KVB])
                        if cc == 11:
                            if gn == 512:
                                self.DMA('sp', self.kvtm_d[gi * 4:gi * 4 + 4, :, :, :].rearrange("j p c d -> p j c d"), kv[:, :, :, :], rd=[KVB], wr=[self.buf("kvtm%d" % gi)])
                            else:
                                self.DMA('sp', self.kvtm_d[32, 0:gn, :, :], kv[:gn, 0, :, :], rd=[KVB, self.buf("kvtm8")], wr=[self.buf("kvtm8")])
                    pend.append(part2)
                    cnt += 1
            while pend:
                pend.pop(0)()
            self.MEMSET('pool', ss[:], 1.0, wr=[SSB])
            self.COPY('dve', ss[:, 0:32, :], pss[:, 0:256].rearrange("p (j c) -> p j c", c=8), rd=[PSS], wr=[SSB])
            self.COPY('dve', ss[:16, 32, :], pss[:16, 256:264], rd=[PSS], wr=[SSB])
            self.DMA('sp', self.sc_d[:, 0:NCH * 8], ss[:].rearrange("p j c -> p (j c)"), rd=[SSB], wr=[self.buf("sc_ss")])
            self.DMA('sp', self.sc_d[:, NCH * 8:NCH * 8 + 16 * NCH], bg[:].rearrange("p a j h -> p (a j h)"), rd=[BGB], wr=[self.buf("sc_bg")])
        mk.barrier()

    def phaseB2(self, l):
        mk = self.mk
        with contextlib.ExitStack() as es:
            ss, SCB = self.sb(es, "d_ss", [128, NCH, 8])
            bg, _ = self.sb(es, "d_bg", [128, 4, NCH, 4])
            self.DMA('sp', ss[:].rearrange("p j c -> p (j c)"), self.sc_d[:, 0:NCH * 8], rd=[self.buf("sc_ss")], wr=[SCB])
            self.DMA('sp', bg[:].rearrange("p a j h -> p (a j h)"), self.sc_d[:, NCH * 8:NCH * 8 + 16 * NCH], rd=[self.buf("sc_bg")], wr=[SCB])
            adt, _ = self.sb(es, "d_adt", [128, 2, 2, NCH, 4])
            self.DMA('sp', adt[:].rearrange("p a r j h -> p (a r j h)"), self.adt[l:l + 1, :, :].rearrange("o a n -> o (a n)").to_broadcast([128, 528]), wr=[SCB])
            names = ["rk2", "irk", "s2m", "scq"]
            A4 = {n: self.sb(es, "d_" + n, [128, NCH, 4])[0] for n in names}
            names8 = ["beta", "g", "b2", "nb2", "gc", "egc", "ekt", "cd", "b2e", "tmp8"]
            A8 = {n: self.sb(es, "d_" + n, [128, 2, NCH, 4])[0] for n in names8}
            nea, _ = self.sb(es, "d_nea", [128, 2, NCH, 4])
            R = [SCB]
            ssk = ss[:, :, 4:8]
            ssq = ss[:, :, 0:4]
            self.TS('dve', A4["rk2"][:], ssk, 1e-6, None, ALU.add, rd=R, wr=R)
            self.ACT(A4["irk"][:], A4["rk2"][:], AF.Sqrt, rd=R, wr=R)
            self.mk.op('dve', lambda e: e.reciprocal(out=A4["rk2"][:], in_=A4["rk2"][:]), reads=R, writes=R)
            self.TS('dve', A4["s2m"][:], ssq, 1e-6, 128.0, ALU.add, ALU.mult, rd=R, wr=R)
            self.mk.op('dve', lambda e: e.reciprocal(out=A4["s2m"][:], in_=A4["s2m"][:]), reads=R, writes=R)
            self.ACT(A4["scq"][:], A4["s2m"][:], AF.Sqrt, rd=R, wr=R)
            self.TS('dve', A4["s2m"][:], A4["s2m"][:], 1.0 / 128.0, None, ALU.mult, rd=R, wr=R)
            self.MEMSET('pool', A8["beta"][:], 0.0, wr=R)
            self.MEMSET('pool', A8["g"][:], 0.0, wr=R)
            self.ACT(nea[:], adt[:, 0, :, :, :], AF.Exp, rd=R, wr=R)
            for (ps_, js) in ((slice(0, 128), slice(0, 32)), (slice(0, 16), slice(32, 33))):
                self.ACT(A8["beta"][ps_, :, js, :], bg[ps_, 0:2, js, :], AF.Sigmoid, rd=R, wr=R)
                self.TT('dve', A8["tmp8"][ps_, :, js, :], bg[ps_, 2:4, js, :], adt[ps_, 1, :, js, :], ALU.add, rd=R, wr=R)
                self.ACT(A8["tmp8"][ps_, :, js, :], A8["tmp8"][ps_, :, js, :], AF.Exp, rd=R, wr=R)
                self.ACT(A8["tmp8"][ps_, :, js, :], A8["tmp8"][ps_, :, js, :], AF.Ln, bias=1.0, rd=R, wr=R)
                self.STT(A8["g"][ps_, :, js, :], A8["tmp8"][ps_, :, js, :], -1.0, nea[ps_, :, js, :], ALU.mult, ALU.mult, rd=R, wr=R)
            for r in range(2):
                self.TT('dve', A8["b2"][:, r, :, :], A8["beta"][:, r, :, :], A4["rk2"][:], ALU.mult, rd=R, wr=R)
            self.TS('dve', A8["nb2"][:], A8["b2"][:], -1.0, None, ALU.mult, rd=R, wr=R)
            pg, PG = self.ps[0], self.buf("ps0")
            pl, PL = self.ps[1], self.buf("ps1")
            g2 = A8["g"]
            self.MM(pg[:, 0:132], self.triu[:, :], g2[:, 0, :, :].rearrange("p j h -> p (j h)"), rd=R + [self.IDB], wr=[PG])
            self.MM(pg[:, 132:264], self.tril[:, :], g2[:, 1, :, :].rearrange("p j h -> p (j h)"), rd=R + [self.IDB], wr=[PG])
            self.MM(pl[:, 0:264], self.ones32[:, :], g2[:].rearrange("p r j h -> p (r j h)"), rd=R + [self.IDB], wr=[PL])
            fl = lambda t: t[:].rearrange("p r j h -> p (r j h)")
            self.COPY('dve', fl(A8["gc"]), pg[:, 0:264], rd=[PG], wr=R)
            self.ACT(fl(A8["egc"]), fl(A8["gc"]), AF.Exp, rd=R, wr=R)
            self.ACT(fl(A8["cd"]), pl[:, 0:264], AF.Exp, rd=[PL], wr=R)
            self.TT('dve', fl(A8["ekt"]), pl[:, 0:264], fl(A8["gc"]), ALU.subtract, rd=[PL] + R, wr=R)
            self.ACT(fl(A8["ekt"]), fl(A8["ekt"]), AF.Exp, rd=R, wr=R)
            self.TT('dve', fl(A8["b2e"]), fl(A8["b2"]), fl(A8["ekt"]), ALU.mult, rd=R, wr=R)
            self.DMA('sp', self.sc_d[:, 800:800 + 132], A4["s2m"][:].rearrange("p j h -> p (j h)"), rd=R, wr=[self.buf("sc_s2m")])
            self.DMA('sp', self.sc_d[:, 1000:1000 + 132], A4["scq"][:].rearrange("p j h -> p (j h)"), rd=R, wr=[self.buf("sc_scq")])
            if self.debug:
                for i, n in enumerate(["beta", "g", "gc", "b2", "egc", "ekt", "cd"]):
                    self.DMA('sp', self.sc_d[:, 1200 + i * 264:1200 + (i + 1) * 264], fl(A8[n]), rd=R, wr=[self.buf("sc_dbg%d" % i)])
            import os
            CUT = int(os.environ.get("B2CUT", "9"))
            if CUT <= 1:
                mk.barrier(); return
            oacc, OB = self.sb(es, "d_oacc", [128, NCH, 4, 128])
            OBs = [Buf("oacc%d" % j) for j in range(NCH)]
            owritten = set()
            H4 = [128, 4, 128]
            bc = lambda ap4: ap4.unsqueeze(2).to_broadcast(H4)
            hb = lambda m: m[:, :].unsqueeze(1).to_broadcast(H4)
            DR = []
            for r in range(2):
                d = {}
                for n in ["tmp", "tmp2", "tS", "S32"]:
                    d[n] = self.sb(es, "d_%s_r%d" % (n, r), H4)
                for n in ["vA", "vS", "S16"]:
                    d[n] = self.sb(es, "d_%s_r%d" % (n, r), H4, BF16)
                d["tsets"] = []
                for c in range(2):
                    t_ = {}
                    for n in ["gM2", "Ei"]:
                        t_[n] = self.sb(es, "d_%s_r%d_c%d" % (n, r, c), H4)
                    t_["EsN"] = t_["gM2"]
                    for n in ["NT", "Nn", "Ma", "Mb", "MTa", "MTb", "Pa", "Pb", "Tw"]:
                        t_[n] = self.sb(es, "d_%s_r%d_c%d" % (n, r, c), H4, BF16)
                    d["tsets"].append(t_)
                for n in ["AT", "Tu", "nW"]:
                    d[n] = [self.sb(es, "d_%s_r%d_%d" % (n, r, i), H4, BF16) for i in range(3)]
                d["kq"] = [self.sb(es, "d_kq_r%d_%d" % (r, i), [128, 2, 4, 128], BF16) for i in range(3)]
                d["kv"] = [self.sb(es, "d_kv_r%d_%d" % (r, i), [128, 8, 128], BF16) for i in range(3)]
                self.MEMSET('pool', d["S32"][0][:], 0.0, wr=[d["S32"][1]])
                self.MEMSET('pool', d["S16"][0][:], 0.0, wr=[d["S16"][1]])
                DR.append(d)
            bank = lambda i: (self.ps[i], self.buf("ps%d" % i))
            b4 = lambda pt: pt[:, :].rearrange("p (h c) -> p h c", h=4)
            psb4 = self.psb[:, 0:512].rearrange("p (h c) -> p h c", h=4)
            PSBB = self.buf("psb")

            def pre(r, j, st, c):
                T = dict(DR[r])
                T.update(DR[r]["tsets"][c])
                M1 = self.m1f if r == 0 else self.m1b
                M2 = self.triu if r == 0 else self.tril
                NM = self.nm16f if r == 0 else self.nm16b
                mybank = bank(2 * r + c)

                def nxt():
                    return mybank
                kq, KQB = T["kq"][st]
                kv, KVB = T["kv"][st]
                s4 = lambda n: A8[n][:, r, j, :]
                self.DMA('sp', kq[:, :, :, :], self.qkT_d[:, j * 128:(j + 1) * 128].rearrange("(x h p) t -> p x h t", x=2, h=4),
                         rd=[self.buf("qkT%d_%d" % (j // 4, c_)) for c_ in range(8)] + [self.buf("qkT8")], wr=[KQB])
                self.DMA('sp', kv[:, :, :], self.kvtm_d[j, :, :, :], rd=[self.buf("kvtm%d" % (j // 4)), self.buf("kvtm8")], wr=[KVB])
                (gM2, GMB), (Ei, EIB), (EsN, ESB) = T["gM2"], T["Ei"], T["EsN"]
                (NT, NTB), (Nn, NNB), (Tw, TWB) = T["NT"], T["Nn"], T["Tw"]
                AT, ATB = T["AT"][st]
                Tu, TUB = T["Tu"][st]
                nW, NWB = T["nW"][st]
                self.TT(os.environ.get("B2_GM2", "pool"), gM2[:], hb(M2), bc(s4("g")), ALU.mult, rd=R + [self.IDB], wr=[GMB])
                pt, PB_ = nxt()
                self.MM(pt[:, :], M1[:, :], gM2[:].rearrange("p h c -> p (h c)"), start=True, stop=False, rd=[GMB, self.IDB], wr=[PB_])
                self.MM(pt[:, :], self.ident16[:, :], NM[:].rearrange("p h c -> p (h c)"), start=False, stop=True, rd=[self.IDB], wr=[PB_])
                self.ACT(Ei[:], b4(pt), AF.Exp, rd=[PB_], wr=[EIB])
                yield
                self.TT('pool', EsN[:], Ei[:], bc(s4("nb2")), ALU.mult, rd=[EIB] + R, wr=[ESB])
                self.mk.op('pool', lambda e: e.affine_select(out=EsN[:], in_=EsN[:], pattern=[[0, 4], [-1, 128]], compare_op=ALU.not_equal, fill=0.0, base=0, channel_multiplier=1), reads=[ESB], writes=[ESB])
                pt, PB_ = nxt()
                for h in range(4):
                    self.MM(pt[:, h * 128:(h + 1) * 128], kq[:, 1, h, :], kq[:, 0, h, :], rd=[KQB], wr=[PB_])
                self.TT('dve', AT[:], b4(pt), Ei[:], ALU.mult, rd=[PB_, EIB], wr=[ATB])
                yield
                pt, PB_ = nxt()
                for h in range(4):
                    self.MM(pt[:, h * 128:(h + 1) * 128], kq[:, 1, h, :], kq[:, 1, h, :], rd=[KQB], wr=[PB_])
                self.TT('dve', NT[:], b4(pt), EsN[:], ALU.mult, rd=[PB_, ESB], wr=[NTB])
                yield
                for h in range(4):
                    self.TR(self.psb[:, h * 128:(h + 1) * 128], NT[:, h, :], self.ident16[:, :], rd=[NTB, self.IDB], wr=[PSBB])
                self.COPY('act', Nn[:], psb4, rd=[PSBB], wr=[NNB])
                (Pa, PAB) = T["Pa"]
                self.TT('pool', Pa[:], NT[:], hb(self.ident16), ALU.add, rd=[NTB, self.IDB], wr=[PAB])
                yield
                M_prev, MB_prev = Nn, NNB
                MT_prev, MTB_prev = NT, NTB
                P_prev, PB_prev = Pa, PAB
                Ms = [T["Ma"], T["Mb"]]
                MTs = [T["MTa"], T["MTb"]]
                Ps = [T["Pb"], T["Pa"]]
                for k in range(1, 7):
                    Mk, MKB = Ms[k % 2]
                    pt, PB_ = nxt()
                    for h in range(4):
                        self.MM(pt[:, h * 128:(h + 1) * 128], MT_prev[:, h, :], M_prev[:, h, :], rd=[MB_prev, MTB_prev], wr=[PB_])
                    self.COPY('act', Mk[:], b4(pt), rd=[PB_], wr=[MKB])
                    yield
                    if k <= 5:
                        MTk, MTKB = MTs[k % 2]
                        pt, PB_ = nxt()
                        for h in range(4):
                            self.MM(pt[:, h * 128:(h + 1) * 128], M_prev[:, h, :], MT_prev[:, h, :], rd=[MB_prev, MTB_prev], wr=[PB_])
                        self.COPY('dve', MTk[:], b4(pt), rd=[PB_], wr=[MTKB])
                        yield
                    pt, PB_ = nxt()
                    self.MM(pt[:, :], self.ident16[:, :], P_prev[:].rearrange("p h c -> p (h c)"), start=True, stop=False, rd=[PB_prev, self.IDB], wr=[PB_])
                    for h in range(4):
                        self.MM(pt[:, h * 128:(h + 1) * 128], Mk[:, h, :], P_prev[:, h, :], start=False, stop=(h == 3), rd=[MKB, PB_prev], wr=[PB_])
                    if k <= 5:
                        Pk, PKB = Ps[(k - 1) % 2]
                        self.COPY('act', Pk[:], b4(pt), rd=[PB_], wr=[PKB])
                        P_prev, PB_prev = Pk, PKB
                        M_prev, MB_prev = Mk, MKB
                        MT_prev, MTB_prev = MTk, MTKB
                    else:
                        self.TT('dve', Tu[:], b4(pt), bc(A4["irk"][:, j, :]), ALU.mult, rd=[PB_] + R, wr=[TUB])
                        self.TT('dve', Tw[:], b4(pt), bc(s4("egc")), ALU.mult, rd=[PB_] + R, wr=[TWB])
                    yield
                pt, PB_ = nxt()
                for h in range(4):
                    self.MM(pt[:, h * 128:(h + 1) * 128], kv[:, h, :], Tw[:, h, :], rd=[KVB, TWB], wr=[PB_])
                self.ACT(nW[:], b4(pt), AF.Identity, scale=-1.0, rd=[PB_], wr=[NWB])
                yield

            def scan(r, j, st):
                T = DR[r]
                banks = [bank(4 + r), bank(6)]
                kq, KQB = T["kq"][st]
                kv, KVB = T["kv"][st]
                AT, ATB = T["AT"][st]
                Tu, TUB = T["Tu"][st]
                nW, NWB = T["nW"][st]
                s4 = lambda n: A8[n][:, r, j, :]
                (s32, S32B), (s16, S16B) = T["S32"], T["S16"]
                (vA, VAB), (vS, VSB) = T["vA"], T["vS"]
                (tmp, TMB), (tmp2, TM2B), (tS, TSB) = T["tmp"], T["tmp2"], T["tS"]
                self.TT('pool', tS[:], s32[:], bc(s4("cd")), ALU.mult, rd=[S32B] + R, wr=[TSB])
                pt, PB_ = banks[0]
                for h in range(4):
                    self.MM(pt[:, h * 128:(h + 1) * 128], Tu[:, h, :], kv[:, 4 + h, :], start=True, stop=False, rd=[TUB, KVB], wr=[PB_])
                    self.MM(pt[:, h * 128:(h + 1) * 128], nW[:, h, :], s16[:, h, :], start=False, stop=True, rd=[NWB, S16B], wr=[PB_])
                self.TT('dve', vS[:], b4(pt), bc(s4("b2e")), ALU.mult, rd=[PB_] + R, wr=[VSB])
                self.TT('dve', vA[:], b4(pt), bc(s4("b2")), ALU.mult, rd=[PB_] + R, wr=[VAB])
                yield
                pt, PB_ = banks[1]
                for h in range(4):
                    self.MM(pt[:, h * 128:(h + 1) * 128], kq[:, 0, h, :], s16[:, h, :], rd=[KQB, S16B], wr=[PB_])
                self.TT('dve', tmp[:], b4(pt), bc(s4("egc")), ALU.mult, rd=[PB_] + R, wr=[TMB])
                yield
                pt, PB_ = banks[0]
                for h in range(4):
                    self.MM(pt[:, h * 128:(h + 1) * 128], kv[:, h, :], vS[:, h, :], rd=[KVB, VSB], wr=[PB_])
                self.TT('dve', s32[:], b4(pt), tS[:], ALU.add, rd=[PB_, TSB], wr=[S32B])
                self.COPY('act', s16[:], s32[:], rd=[S32B], wr=[S16B])
                yield
                pt, PB_ = banks[1]
                for h in range(4):
                    self.MM(pt[:, h * 128:(h + 1) * 128], AT[:, h, :], vA[:, h, :], rd=[ATB, VAB], wr=[PB_])
                if j not in owritten:
                    owritten.add(j)
                    self.TT('dve', oacc[:, j, :, :], b4(pt), tmp[:], ALU.add, rd=[PB_, TMB], wr=[OBs[j]])
                else:
                    self.TT('pool', tmp2[:], tmp[:], oacc[:, j, :, :], ALU.add, rd=[TMB, OBs[j]], wr=[TM2B])
                    self.TT('dve', oacc[:, j, :, :], b4(pt), tmp2[:], ALU.add, rd=[PB_, TM2B], wr=[OBs[j]])
                yield

            def dir_gen(r):
                order = list(range(NCH)) if r == 0 else list(range(NCH - 1, -1, -1))
                N = len(order)
                pres = {}
                pre_done = set()
                next_pre = 0
                scans_done = 0
                scan_g = None
                next_scan = 0
                tick = 0
                for _ in range(8 * r):
                    yield
                while scans_done < N:
                    tick += 1
                    for c in range(2):
                        if c == 1 and tick < 16:
                            continue
                        if c not in pres and next_pre < N and next_pre < scans_done + 3:
                            pres[c] = (next_pre, pre(r, order[next_pre], next_pre % 3, c))
                            next_pre += 1
                    if scan_g is None and next_scan < N and next_scan in pre_done:
                        scan_g = scan(r, order[next_scan], next_scan % 3)
                    progressed = False
                    for c in list(pres):
                        idx, g = pres[c]
                        try:
                            next(g)
                            progressed = True
                            yield
                        except StopIteration:
                            pre_done.add(idx)
                            del pres[c]
                            progressed = True
                    if scan_g is not None:
                        try:
                            next(scan_g)
                            progressed = True
                            yield
                        except StopIteration:
                            scan_g = None
                            scans_done += 1
                            next_scan += 1
                            progressed = True
                    assert progressed

            gens = [dir_gen(0), dir_gen(1)]
            while gens:
                for g in list(gens):
                    try:
                        next(g)
                    except StopIteration:
                        gens.remove(g)
            self.DMA('sp', self.oacc_d[:, :], oacc[:].rearrange("p j h d -> p (j h d)"), rd=OBs, wr=[self.buf("oacc")])
        mk.barrier()

    def phaseB3(self, l):
        mk = self.mk
        with contextlib.ExitStack() as es:
            wz, WB = self.sb(es, "e_wz", [128, 8, 512], BF16)
            self.DMA('pool', wz[:], self.w_in[l, :, 2064:2576].rearrange("(k p) n -> p k n", p=128), wr=[WB])
            oacc, OB = self.sb(es, "e_oacc", [128, NCH, 4, 128])
            self.DMA('sp', oacc[:].rearrange("p j h d -> p (j h d)"), self.oacc_d[:, :], rd=[self.buf("oacc")], wr=[OB])
            s2m, SB_ = self.sb(es, "e_s2m", [128, NCH, 4])
            scq, _ = self.sb(es, "e_scq", [128, NCH, 4])
            self.DMA('sp', s2m[:].rearrange("p j h -> p (j h)"), self.sc_d[:, 800:932], rd=[self.buf("sc_s2m")], wr=[SB_])
            self.DMA('sp', scq[:].rearrange("p j h -> p (j h)"), self.sc_d[:, 1000:1132], rd=[self.buf("sc_scq")], wr=[SB_])
            ng, NGB = self.sb(es, "e_ng", [128, 1])
            self.DMA('sp', ng[:], self.dng[l, :, :], wr=[NGB])
            hts = [self.sb(es, "e_h%d" % i, [128, 8, 512], BF16) for i in range(2)]
            szs = [self.sb(es, "e_sz%d" % i, [128, 512]) for i in range(2)]
            dns = [self.sb(es, "e_dn%d" % i, [128, 4, 128]) for i in range(2)]
            junk, JB = self.sb(es, "e_junk", [128, 128])
            sms = [self.sb(es, "e_sm%d" % i, [128, 4, 4]) for i in range(2)]
            stg = [self.sb(es, "e_st%d" % i, [128, 4, 512], BF16) for i in range(2)]
            pz = [(self.ps[0], self.buf("ps0")), (self.ps[1], self.buf("ps1"))]
            ptt = [(self.ps[2], self.buf("ps2")), (self.ps[3], self.buf("ps3"))]
            T5 = tiles(L, 512)
            ssum, FB = self.sb(es, "e_ssum", [128, NCH, 4])
            fac, _ = self.sb(es, "e_fac", [128, NCH, 4])
            self.MEMSET('pool', ssum[:], 1.0, wr=[FB])
            for j, (t0_, m_) in enumerate(tiles(L, 128)):
                for h in range(4):
                    self.ACT(junk[:m_, :], oacc[:m_, j, h, :], AF.Square, accum_out=ssum[:m_, j, h:h + 1], rd=[OB], wr=[JB, FB])
            self.TT('dve', fac[:], ssum[:], s2m[:], ALU.mult, rd=[FB, SB_], wr=[FB])
            self.ACT(fac[:], fac[:], AF.Sqrt, bias=1e-6, rd=[FB], wr=[FB])
            self.mk.op('dve', lambda e: e.reciprocal(out=fac[:], in_=fac[:]), reads=[FB], writes=[FB])
            self.TT('dve', fac[:], fac[:], scq[:], ALU.mult, rd=[FB, SB_], wr=[FB])

            def b3_load(g):
                T0_, gn_ = T5[g]
                self.DMA('sp', hts[g % 2][0][:, :, :gn_], self.hT_d[:, T0_:T0_ + gn_].rearrange("(k p) t -> p k t", p=128), rd=[self.buf("hT%d" % g)], wr=[hts[g % 2][1]])
            pend = []
            b3_load(0)
            for gi, (T0, gn) in enumerate(T5):
                ht, HB = hts[gi % 2]
                stage, STB = stg[gi % 2]
                if gi + 1 < len(T5):
                    b3_load(gi + 1)
                for s, (s0, m) in enumerate(tiles(gn, 128)):
                    j = gi * 4 + s
                    pt, PBk = pz[j % 2]
                    sz, SZB = szs[j % 2]
                    dn, DNB = dns[j % 2]
                    sm, SMB = sms[j % 2]
                    for k in range(8):
                        self.MM(pt[:m, :], ht[:, k, s0:s0 + m], wz[:, k, :], start=(k == 0), stop=(k == 7), rd=[HB, WB], wr=[PBk])
                    self.ACT(sz[:m, :], pt[:m, :], AF.Silu, rd=[PBk], wr=[SZB])
                    while pend:
                        pend.pop(0)()

                    def part2(m=m, j=j, sz=sz, SZB=SZB, dn=dn, DNB=DNB, sm=sm, SMB=SMB, stage=stage, STB=STB, s0=s0, lastsub=(s0 + m == gn), T0=T0, gn=gn, gi=gi):
                        for h in range(4):
                            self.STT(dn[:m, h, :], oacc[:m, j, h, :], fac[:m, j, h:h + 1], sz[:m, h * 128:(h + 1) * 128], ALU.mult, ALU.mult, rd=[OB, FB, SZB], wr=[DNB])
                        tp, TPB = ptt[j % 2]
                        for h in range(4):
                            self.TR(tp[:, h * 128:h * 128 + m], dn[:m, h, :], self.ident[:m, :m], rd=[DNB, self.IDB], wr=[TPB])
                        self.ACT(stage[:, :, s0:s0 + m], tp[:, :].rearrange("p (h t) -> p h t", h=4)[:, :, :m], AF.Identity, scale=ng[:, 0:1], rd=[TPB, NGB], wr=[STB])
                        if lastsub:
                            self.DMA('pool', self.dnT_d[:, T0:T0 + gn].rearrange("(g p) t -> p g t", p=128), stage[:, :, :gn], rd=[STB], wr=[self.buf("dnT%d" % gi)])
                    pend.append(part2)
            while pend:
                pend.pop(0)()
        mk.barrier()

    def c1_weights(self, es, l):
        wg, WB = self.sb(es, "c_wg", [128, 8, 2048], BF16)
        wfp, WFB = self.sb(es, "c_wfp", [128, 4, D], BF16)
        wdp, WDPB = self.sb(es, "c_wdp", [128, 4, D], BF16)
        wo, WOB = self.sb(es, "c_wo", [128, 8, D], BF16)
        self.DMA('pool', wfp[:], self.w_fp[l, :, :].rearrange("(k p) n -> p k n", p=128), wr=[WFB])
        self.DMA('pool', wdp[:], self.w_dp[l, :, :].rearrange("(k p) n -> p k n", p=128), wr=[WDPB])
        WGB = [Buf("wg%d" % c) for c in range(4)]
        for c in (0, 2, 1, 3):
            self.DMA('pool', wg[:, :, c * 512:(c + 1) * 512], self.w_in[l, :, 2576 + c * 512:2576 + (c + 1) * 512].rearrange("(k p) n -> p k n", p=128), wr=[WGB[c]])
        self.DMA('pool', wo[:], self.w_out[l, :, :].rearrange("(k p) n -> p k n", p=128), wr=[WOB])
        return (wg, WB, wfp, WFB, wdp, WDPB, wo, WOB, WGB)

    def phaseC1(self, l, W):
        mk = self.mk
        (wg, WB, wfp, WFB, wdp, WDPB, wo, WOB, WGB) = W
        with contextlib.ExitStack() as es:
            gbt, GB = self.sb(es, "c_g", [128, D])
            bbt, _ = self.sb(es, "c_b", [128, D])
            self.DMA('sp', gbt[:], self.ln1[l, 0:1, :].to_broadcast([128, D]), wr=[GB])
            self.DMA('sp', bbt[:], self.ln1[l, 1:2, :].to_broadcast([128, D]), wr=[GB])
            hts = [self.sb(es, "c_h%d" % i, [128, 8, 512], BF16) for i in range(2)]
            yfs = [self.sb(es, "c_yf%d" % i, [128, 4, 512], BF16) for i in range(2)]
            dnt = [self.sb(es, "c_dn%d" % i, [128, 4, 512], BF16) for i in range(2)]
            GT, GTB = self.sb(es, "c_GT", [128, 8, 512], BF16)
            gas = [self.sb(es, "c_ga%d" % i, [128, 512]) for i in range(2)]
            gbs = [self.sb(es, "c_gb%d" % i, [128, 512]) for i in range(2)]
            t1s = [self.sb(es, "c_t1%d" % i, [128, 512]) for i in range(2)]
            t2s = [self.sb(es, "c_t2%d" % i, [128, 512]) for i in range(2)]
            hrs = [self.sb(es, "c_hr%d" % i, [128, D]) for i in range(2)]
            rs = [self.sb(es, "c_r%d" % i, [128, D]) for i in range(2)]
            stg = [self.sb(es, "c_st%d" % i, [128, 8, 512], BF16) for i in range(2)]
            scs = self.ln_scratch(es, "c1")
            pya, pyb, pga, pgb = [(self.ps[i], self.buf("ps%d" % i)) for i in range(4)]
            pm = [(self.ps[4], self.buf("ps4")), (self.ps[4], self.buf("ps4"))]
            ptb = [(self.ps[5], self.buf("ps5")), (self.ps[6], self.buf("ps6"))]
            deferred = []

            def flush():
                while deferred:
                    deferred.pop(0)()
            T5 = tiles(L, 512)
            T1 = tiles(L, 128)

            def c1_loads(g):
                T0_, gn_ = T5[g]
                ht_, HB_ = hts[g % 2]
                yf_, YB_ = yfs[g % 2]
                dn_, DB_ = dnt[g % 2]
                self.DMA('sp', ht_[:, :, :gn_], self.hT_d[:, T0_:T0_ + gn_].rearrange("(k p) t -> p k t", p=128), rd=[self.buf("hT%d" % g)], wr=[HB_])
                self.DMA('sp', yf_[:, :, :gn_], self.yfT_d[:, T0_:T0_ + gn_].rearrange("(g p) t -> p g t", p=128), rd=[self.buf("yfT%d" % g)], wr=[YB_])
                self.DMA('sp', dn_[:, :, :gn_], self.dnT_d[:, T0_:T0_ + gn_].rearrange("(g p) t -> p g t", p=128), rd=[self.buf("dnT%d" % g)], wr=[DB_])

            def c1_hr(jj):
                if jj < len(T1):
                    t0_, m_ = T1[jj]
                    self.DMA('sp', hrs[jj % 2][0][:m_, :], self.h_tm[t0_:t0_ + m_, :], rd=[self.buf("h_tm%d" % jj)], wr=[hrs[jj % 2][1]])
            c1_hr(0)
            for gi, (T0, gn) in enumerate(T5):
                ht, HB = hts[gi % 2]
                yf, YB = yfs[gi % 2]
                dn, DB = dnt[gi % 2]
                stage, STB = stg[gi % 2]
                if gi == 0:
                    c1_loads(0)
                if gi + 1 < len(T5):
                    c1_loads(gi + 1)
                for fc in range(8):
                    fs = slice(fc * 128, (fc + 1) * 128)
                    for k in range(8):
                        self.MM(pga[0][:, :gn], wg[:, k, fs], ht[:, k, :gn], start=(k == 0), stop=(k == 7), rd=[WGB[fc // 4], HB], wr=[pga[1]])
                    for k in range(8):
                        self.MM(pgb[0][:, :gn], wg[:, k, 1024 + fc * 128:1024 + (fc + 1) * 128], ht[:, k, :gn], start=(k == 0), stop=(k == 7), rd=[WGB[2 + fc // 4], HB], wr=[pgb[1]])
                    for c in range(4):
                        self.MM(pya[0][:, :gn], wfp[:, c, fs], yf[:, c, :gn], start=(c == 0), stop=(c == 3), rd=[WFB, YB], wr=[pya[1]])
                    for c in range(4):
                        self.MM(pyb[0][:, :gn], wdp[:, c, fs], dn[:, c, :gn], start=(c == 0), stop=(c == 3), rd=[WDPB, DB], wr=[pyb[1]])
                    ga, GAB = gas[fc % 2]
                    gb, GBB = gbs[fc % 2]
                    t1, T1B = t1s[fc % 2]
                    t2, T2B = t2s[fc % 2]
                    self.ACT(ga[:, :gn], pga[0][:, :gn], AF.Sigmoid, rd=[pga[1]], wr=[GAB])
                    self.ACT(gb[:, :gn], pgb[0][:, :gn], AF.Sigmoid, rd=[pgb[1]], wr=[GBB])
                    self.TT('dve', t1[:, :gn], pya[0][:, :gn], ga[:, :gn], ALU.mult, rd=[pya[1], GAB], wr=[T1B])
                    self.TT('dve', t2[:, :gn], pyb[0][:, :gn], gb[:, :gn], ALU.mult, rd=[pyb[1], GBB], wr=[T2B])
                    self.TT('pool', GT[:, fc, :gn], t1[:, :gn], t2[:, :gn], ALU.add, rd=[T1B, T2B], wr=[GTB])
                    if fc == 1:
                        flush()
                for s, (s0, m) in enumerate(tiles(gn, 128)):
                    j = gi * 4 + s
                    t0 = T0 + s0
                    hr, HRB = hrs[j % 2]
                    rr, RB = rs[j % 2]
                    c1_hr(j + 1)
                    for half in range(2):
                        pmt, PMB = pm[half]
                        for k in range(8):
                            self.MM(pmt[:m, :], GT[:, k, s0:s0 + m], wo[:, k, half * 512:(half + 1) * 512], start=(k == 0), stop=(k == 7), rd=[GTB, WOB], wr=[PMB])
                        self.STT(rr[:m, half * 512:(half + 1) * 512], hr[:m, half * 512:(half + 1) * 512], ALPHA, pmt[:m, :], ALU.mult, ALU.add, rd=[HRB, PMB], wr=[RB])
                    flush()
                    self.ln_tile(rr, RB, m, gbt, bbt, GB, scs[j % 2])
                    self.DMA('pool', self.h1_tm[t0:t0 + m, :], rr[:m, :], rd=[RB], wr=[self.buf("h1_tm%d" % j)])

                    def later(rr=rr, RB=RB, m=m, stage=stage, STB=STB, s0=s0, j=j, lastsub=(s0 + m == gn), T0=T0, gn=gn, gi=gi):
                        self.transpose_tile(rr, RB, m, stage, STB, s0, ptb, j)
                        if lastsub:
                            self.DMA('pool', self.h1T_d[:, T0:T0 + gn].rearrange("(k p) t -> p k t", p=128), stage[:, :, :gn], rd=[STB], wr=[self.buf("h1T%d" % gi)])
                    deferred.append(later)
            flush()
        mk.barrier()

    def phaseC2(self, l, last):
        mk = self.mk
        with contextlib.ExitStack() as es:
            wup, WB = self.sb(es, "f_wup", [128, 8, 4096], BF16)
            wdn, _ = self.sb(es, "f_wdn", [128, 32, D], BF16)
            WUB = [Buf("wup%d" % c) for c in range(8)]
            WDB = [Buf("wdn%d" % c) for c in range(4)]
            for c in range(8):
                self.DMA('pool', wup[:, :, c * 512:(c + 1) * 512], self.w_up[l, :, c * 512:(c + 1) * 512].rearrange("(k p) n -> p k n", p=128), wr=[WUB[c]])
            for c in range(4):
                self.DMA('pool', wdn[:, c * 8:(c + 1) * 8, :], self.w_dn[l, c * 1024:(c + 1) * 1024, :].rearrange("(k p) n -> p k n", p=128), wr=[WDB[c]])
            gbt, GB = self.sb(es, "f_g", [128, D])
            bbt, _ = self.sb(es, "f_b", [128, D])
            self.DMA('sp', gbt[:], self.ln2[l, 0:1, :].to_broadcast([128, D]), wr=[GB])
            self.DMA('sp', bbt[:], self.ln2[l, 1:2, :].to_broadcast([128, D]), wr=[GB])
            hts = [self.sb(es, "f_h%d" % i, [128, 8, 256], BF16) for i in range(2)]
            aT, ATB = self.sb(es, "f_aT", [128, 32, 256], BF16)
            a1s = [self.sb(es, "f_a1%d" % i, [128, 256], BF16) for i in range(3)]
            hrs = [self.sb(es, "f_hr%d" % i, [128, D]) for i in range(2)]
            rs = [self.sb(es, "f_r%d" % i, [128, D]) for i in range(2)]
            stg = [self.sb(es, "f_st%d" % i, [128, 8, 256], BF16) for i in range(2)]
            scs = self.ln_scratch(es, "c2")
            pu = [(self.ps[i], self.buf("ps%d" % i)) for i in range(3)]
            pm = [(self.ps[3], self.buf("ps3")), (self.ps[4], self.buf("ps4"))]
            ptb = [(self.ps[5], self.buf("ps5")), (self.ps[6], self.buf("ps6"))]
            ATBs = [Buf() for _ in range(32)]
            deferred = []

            def flush():
                while deferred:
                    deferred.pop(0)()
            T2 = tiles(L, 256)
            T1 = tiles(L, 128)

            def c2_loads(g):
                T0_, gn_ = T2[g]
                self.DMA('sp', hts[g % 2][0][:, :, :gn_], self.h1T_d[:, T0_:T0_ + gn_].rearrange("(k p) t -> p k t", p=128), rd=[self.buf("h1T%d" % (T0_ // 512))], wr=[hts[g % 2][1]])

            def c2_hr(jj):
                if jj < len(T1):
                    t0_, m_ = T1[jj]
                    self.DMA('sp', hrs[jj % 2][0][:m_, :], self.h1_tm[t0_:t0_ + m_, :], rd=[self.buf("h1_tm%d" % jj)], wr=[hrs[jj % 2][1]])
            c2_hr(0)
            for gi, (T0, gn) in enumerate(T2):
                ht, HB = hts[gi % 2]
                stage, STB = stg[gi % 2]
                if gi == 0:
                    c2_loads(0)
                if gi + 1 < len(T2):
                    c2_loads(gi + 1)
                for fc in range(32):
                    pt, PBk = pu[fc % 3]
                    a1, A1B = a1s[fc % 3]
                    for k in range(8):
                        self.MM(pt[:, :gn], wup[:, k, fc * 128:(fc + 1) * 128], ht[:, k, :gn], start=(k == 0), stop=(k == 7), rd=[WUB[fc // 4], HB], wr=[PBk])
                    self.ACT(a1[:, :gn], pt[:, :gn], AF.Relu, rd=[PBk], wr=[A1B])
                    self.TT('pool', aT[:, fc, :gn], a1[:, :gn], a1[:, :gn], ALU.mult, rd=[A1B, ATB], wr=[ATBs[fc]])
                    if fc == 3:
                        flush()
                for s, (s0, m) in enumerate(tiles(gn, 128)):
                    j = gi * 2 + s
                    t0 = T0 + s0
                    hr, HRB = hrs[j % 2]
                    rr, RB = rs[j % 2]
                    c2_hr(j + 1)
                    for half in range(2):
                        pmt, PMB = pm[half]
                        for kc in range(32):
                            self.MM(pmt[:m, :], aT[:, kc, s0:s0 + m], wdn[:, kc, half * 512:(half + 1) * 512], start=(kc == 0), stop=(kc == 31), rd=[ATBs[kc], WDB[kc // 8]], wr=[PMB])
                        self.STT(rr[:m, half * 512:(half + 1) * 512], hr[:m, half * 512:(half + 1) * 512], ALPHA, pmt[:m, :], ALU.mult, ALU.add, rd=[HRB, PMB], wr=[RB])
                    flush()
                    self.ln_tile(rr, RB, m, gbt, bbt, GB, scs[j % 2])
                    if last:
                        if j == 0:
                            self.final.append(self.DMA('pool', self.y[0:112, :], rr[16:128, :], rd=[RB], wr=[self.buf("y%d" % j)]))
                        else:
                            self.final.append(self.DMA('pool', self.y[t0 - 16:t0 - 16 + m, :], rr[:m, :], rd=[RB], wr=[self.buf("y%d" % j)]))
                    else:
                        self.DMA('pool', self.h_tm[t0:t0 + m, :], rr[:m, :], rd=[RB], wr=[self.buf("h_tm%d" % j)])

                        def later(rr=rr, RB=RB, m=m, stage=stage, STB=STB, s0=s0, j=j, lastsub=(s0 + m == gn), T0=T0, gn=gn):
                            self.transpose_tile(rr, RB, m, stage, STB, s0, ptb, j)
                            if lastsub:
                                self.DMA('pool', self.hT_d[:, T0:T0 + gn].rearrange("(k p) t -> p k t", p=128), stage[:, :, :gn], rd=[STB], wr=[self.buf("hT%d" % (T0 // 512))])
                        deferred.append(later)
            flush()
        mk.barrier()


_CONST = {}


def _consts():
    if _CONST:
        return _CONST
    HL = L // 2 + 1
    t = np.arange(HL, dtype=np.int64)
    m = (t[:, None] * t[None, :]) % L
    ang = (2.0 * np.pi / L) * m.astype(np.float64)
    dft = np.zeros((2, 2176, 2064), dtype=ml_dtypes.bfloat16)
    dft[0, :HL, :HL] = (np.cos(ang) / np.sqrt(L)).astype(np.float32).astype(ml_dtypes.bfloat16)
    dft[1, :HL, :HL] = (np.sin(ang) / np.sqrt(L)).astype(np.float32).astype(ml_dtypes.bfloat16)
    c = np.arange(128, dtype=np.int64)
    a2 = (2.0 * np.pi / 128) * ((c[:, None] * c[None, :]) % 128).astype(np.float64)
    cs = np.concatenate([np.cos(a2), -np.sin(a2)], axis=1) / np.sqrt(128.0)
    _CONST["dft"] = dft
    _CONST["cs128"] = cs.astype(np.float32).astype(ml_dtypes.bfloat16)
    return _CONST


def make_in_maps(inputs, cores):
    f = lambda a: np.ascontiguousarray(np.asarray(a, dtype=np.float32))
    c = _consts()
    conv = f(inputs["conv_w"])
    convw = np.ascontiguousarray(conv.reshape(2, 5, 12, 128).transpose(0, 3, 2, 1).reshape(2, 128, 60))
    al = f(inputs["a_log"]).reshape(2, 2, 1, 4)
    dtb = f(inputs["dt_bias"]).reshape(2, 2, 1, 4)
    adt = np.stack([np.broadcast_to(al, (2, 2, NCH, 4)).reshape(2, 264),
                    np.broadcast_to(dtb, (2, 2, NCH, 4)).reshape(2, 264)], axis=1)
    shared = {
        "meta": f(inputs["meta_tokens"]),
        "ln0": np.stack([f(inputs["ln0_g"]), f(inputs["ln0_b"])]),
        "w_in": f(inputs["w_in"]),
        "convw": convw,
        "adt": np.ascontiguousarray(adt),
        "dng": f(inputs["delta_norm_g"]).reshape(2, 128, 1),
        "w_fp": f(inputs["w_fourier_proj"]),
        "w_dp": f(inputs["w_delta_proj"]),
        "w_out": f(inputs["w_out"]),
        "ln1": np.stack([f(inputs["ln1_g"]), f(inputs["ln1_b"])], axis=1),
        "w_up": f(inputs["w_up"]),
        "w_dn": f(inputs["w_down"]),
        "ln2": np.stack([f(inputs["ln2_g"]), f(inputs["ln2_b"])], axis=1),
        "dft": c["dft"],
        "cs128": c["cs128"],
    }
    x = f(inputs["x"])
    return [dict(shared, x=x[b]) for b in cores]


_PROG = {}


def kernel(**inputs):
    if "p" not in _PROG:
        _PROG["p"] = Prog()
    nc = _PROG["p"].nc
    in_maps = make_in_maps(inputs, list(range(8)))
    res = run_bass_kernel_spmd(nc, in_maps, core_ids=list(range(8)))
    return np.stack([np.asarray(r["y"], dtype=np.float32) for r in res.results], axis=0)
```
